# Optimizing a Trainium2 kernel written in Bass

```python
import jax, jax.numpy as jnp
from jax import lax
import numpy as np

D_MODEL = 1024
BATCH = 2
SEQ = 8192
DEPTH = 4
DEC_BATCH = 128
DEC_SEQ = 1
PAST_LEN = 2048
PAGE_SIZE = 128

HQ = 32
KVH = 4
DH = 64
GQ = HQ // KVH
NSA_W = HQ * DH
KV_W = KVH * DH
ROT_DIM = DH // 4
ROPE_THETA = 500000.0
CMP_LEN = 32
CMP_STRIDE = 16
CMP_HID = 2 * DH
SEL_BLK = 64
N_SEL = 16
WINDOW = 512
QBLK = 128
FORCE_BONUS = 1000.0
NEG = -1.0e30
NSA_IN = 2 * NSA_W + 6 * KV_W + 3 * HQ
RH = 4
RDK = D_MODEL // RH
RDV = 2 * RDK
RET_W = RH * RDV
RET_IN = 2 * RH * RDK + 2 * RET_W
RET_CHUNK = 128
XPOS_BASE = 10000.0
N_MIXERS = 2
N_NSA = (DEPTH + 1) // 2
N_RET = DEPTH // 2
DEEPNORM_ALPHA = (2.0 * DEPTH) ** 0.25
DEEPNORM_BETA = (8.0 * DEPTH) ** -0.25
LN_EPS = 1e-5
F32 = jnp.float32

kernel_name = 'nsa_retention_hybrid_step'


def layer_norm(h, g, b):
    h32 = h.astype(F32)
    d = h32 - jnp.mean(h32, -1, keepdims=True)
    var = jnp.mean(d * d, -1, keepdims=True)
    return (d * lax.rsqrt(var + LN_EPS) * g.astype(F32) + b.astype(F32)).astype(h.dtype)


def masked_softmax(s, mask):
    p = jax.nn.softmax(jnp.where(mask, s.astype(F32), NEG), axis=-1)
    return jnp.where(mask, p, 0.0)


def rope_partial(x, pos):
    half = ROT_DIM // 2
    inv = ROPE_THETA ** (-jnp.arange(half, dtype=F32) / half)
    ang = pos.astype(F32)[:, None] * inv[None, :]
    cos = jnp.cos(ang)[None, :, None, :]
    sin = jnp.sin(ang)[None, :, None, :]
    x1 = x[..., :half].astype(F32)
    x2 = x[..., half:ROT_DIM].astype(F32)
    rot = jnp.concatenate([x1 * cos - x2 * sin, x1 * sin + x2 * cos], -1).astype(x.dtype)
    return jnp.concatenate([rot, x[..., ROT_DIM:]], -1)


def xpos_rotate(x, pos):
    half = RDK // 2
    inv = 1.0 / (XPOS_BASE ** jnp.linspace(0.0, 1.0, half, dtype=F32))
    ang = pos.astype(F32)[:, None] * inv[None, :]
    cos = jnp.cos(ang)[None, :, None, :]
    sin = jnp.sin(ang)[None, :, None, :]
    xe = x[..., 0::2].astype(F32)
    xo = x[..., 1::2].astype(F32)
    out = jnp.stack([xe * cos - xo * sin, xe * sin + xo * cos], -1).reshape(x.shape)
    return out.astype(x.dtype)


def nsa_project(x, pos, w_in):
    B, T, _ = x.shape
    proj = x @ w_in
    cuts = np.cumsum([NSA_W] + [KV_W] * 6 + [3 * HQ]).tolist()
    q, kc, vc, ks, vs, kw, vw, gl, z = jnp.split(proj, cuts, axis=-1)
    heads = lambda a: a.reshape(B, T, KVH, DH)
    q = rope_partial(q.reshape(B, T, HQ, DH), pos)
    kc = rope_partial(heads(kc), pos)
    ks = rope_partial(heads(ks), pos)
    kw = rope_partial(heads(kw), pos)
    g = jax.nn.sigmoid(gl.astype(F32)).astype(x.dtype).reshape(B, T, 3, HQ)
    return q, kc, heads(vc), ks, heads(vs), kw, heads(vw), g, z


def compress_tokens(k, pe, w1, w2):
    B, L = k.shape[:2]
    nc = (L - CMP_LEN) // CMP_STRIDE + 1
    idx = jnp.arange(nc)[:, None] * CMP_STRIDE + jnp.arange(CMP_LEN)[None, :]
    blk = k[:, idx] + pe[:, None, :]
    flat = blk.transpose(0, 1, 3, 2, 4).reshape(B, nc, KVH, CMP_LEN * DH)
    return jax.nn.gelu(flat @ w1) @ w2


def to_sel_blocks(k):
    B, L = k.shape[:2]
    ns = -(-L // SEL_BLK)
    k = jnp.pad(k, ((0, 0), (0, ns * SEL_BLK - L), (0, 0), (0, 0)))
    return k.reshape(B, ns, SEL_BLK, KVH, DH).transpose(0, 3, 1, 2, 4)


def nsa_attend(q, qpos, g, kc, vc, ks_blk, vs_blk, kw, vw, kw_pos):
    B, T = q.shape[:2]
    qg = q.reshape(B, T, KVH, GQ, DH).transpose(0, 2, 3, 1, 4) * (DH ** -0.5)
    nc = kc.shape[1]
    c_start = jnp.arange(nc) * CMP_STRIDE
    m_c = (c_start + CMP_LEN - 1)[None, :] <= qpos[:, None]
    p_c = masked_softmax(jnp.einsum('bkgtd,bckd->bkgtc', qg, kc), m_c)
    o_c = jnp.einsum('bkgtc,bckd->bkgtd', p_c.astype(vc.dtype), vc)
    ns = ks_blk.shape[2]
    s_start = jnp.arange(ns) * SEL_BLK
    ov = jnp.clip(jnp.minimum(c_start[:, None] + CMP_LEN, s_start[None, :] + SEL_BLK)
                  - jnp.maximum(c_start[:, None], s_start[None, :]), 0, None).astype(F32) / CMP_LEN
    imp = jnp.einsum('bktc,cs->bkts', p_c.sum(2), ov)
    blk_q = qpos // SEL_BLK
    j = jnp.arange(ns)
    valid = s_start[None, :] <= qpos[:, None]
    forced = (j[None, :] == 0) | (j[None, :] == blk_q[:, None]) | (j[None, :] == blk_q[:, None] - 1)
    score = jnp.where(valid, imp + FORCE_BONUS * forced.astype(F32), NEG)
    _, sel = lax.top_k(score, min(N_SEL, ns))
    n = sel.shape[-1]
    bi = jnp.arange(B)[:, None, None, None]
    hi = jnp.arange(KVH)[None, :, None, None]
    ksel = ks_blk[bi, hi, sel].reshape(B, KVH, T, n * SEL_BLK, DH)
    vsel = vs_blk[bi, hi, sel].reshape(B, KVH, T, n * SEL_BLK, DH)
    spos = (sel[..., None] * SEL_BLK + jnp.arange(SEL_BLK)).reshape(B, KVH, T, n * SEL_BLK)
    m_s = (spos <= qpos[None, None, :, None])[:, :, None]
    p_s = masked_softmax(jnp.einsum('bkgtd,bktsd->bkgts', qg, ksel), m_s)
    o_s = jnp.einsum('bkgts,bktsd->bkgtd', p_s.astype(vsel.dtype), vsel)
    dist = qpos[:, None] - kw_pos[None, :]
    m_w = (kw_pos[None, :] >= 0) & (dist >= 0) & (dist <= WINDOW)
    p_w = masked_softmax(jnp.einsum('bkgtd,bskd->bkgts', qg, kw), m_w)
    o_w = jnp.einsum('bkgts,bskd->bkgtd', p_w.astype(vw.dtype), vw)
    gg = g.reshape(B, T, 3, KVH, GQ).transpose(2, 0, 3, 4, 1)[..., None]
    o = gg[0] * o_c + gg[1] * o_s + gg[2] * o_w
    return o.transpose(0, 3, 1, 2, 4).reshape(B, T, NSA_W)


def nsa_prompt(x, w_in, w_out, pe_k, w1_k, w2_k, pe_v, w1_v, w2_v):
    B, T, _ = x.shape
    pos = jnp.arange(T, dtype=jnp.int32)
    q, kcr, vcr, ks, vs, kw, vw, g, z = nsa_project(x, pos, w_in)
    kc = compress_tokens(kcr, pe_k, w1_k, w2_k)
    vc = compress_tokens(vcr, pe_v, w1_v, w2_v)
    ks_blk, vs_blk = to_sel_blocks(ks), to_sel_blocks(vs)
    wpad = ((0, 0), (WINDOW, 0), (0, 0), (0, 0))
    kw_pad, vw_pad = jnp.pad(kw, wpad), jnp.pad(vw, wpad)

    def one_block(b):
        st = b * QBLK
        qpos = st + jnp.arange(QBLK, dtype=jnp.int32)
        kpos = st - WINDOW + jnp.arange(WINDOW + QBLK, dtype=jnp.int32)
        sl = lambda a, n: lax.dynamic_slice_in_dim(a, st, n, axis=1)
        return nsa_attend(sl(q, QBLK), qpos, sl(g, QBLK), kc, vc, ks_blk, vs_blk,
                          sl(kw_pad, WINDOW + QBLK), sl(vw_pad, WINDOW + QBLK), kpos)

    o = lax.map(one_block, jnp.arange(T // QBLK, dtype=jnp.int32))
    o = o.transpose(1, 0, 2, 3).reshape(B, T, NSA_W)
    y = (o * jax.nn.silu(z)) @ w_out
    wb = min(WINDOW, T)
    return y, (kcr, vcr, ks, vs, kw[:, T - wb:], vw[:, T - wb:])


def nsa_sample(x, page_table, ck_c, cv_c, ck_s, cv_s, ck_w, cv_w,
               w_in, w_out, pe_k, w1_k, w2_k, pe_v, w1_v, w2_v):
    B, T, _ = x.shape
    past = page_table.shape[1] * PAGE_SIZE
    pos = past + jnp.arange(T, dtype=jnp.int32)
    q, kcr, vcr, ks, vs, kw, vw, g, z = nsa_project(x, pos, w_in)
    gather = lambda pool: pool[page_table].reshape(B, past, KVH, DH)
    kc = compress_tokens(jnp.concatenate([gather(ck_c), kcr], 1), pe_k, w1_k, w2_k)
    vc = compress_tokens(jnp.concatenate([gather(cv_c), vcr], 1), pe_v, w1_v, w2_v)
    ks_blk = to_sel_blocks(jnp.concatenate([gather(ck_s), ks], 1))
    vs_blk = to_sel_blocks(jnp.concatenate([gather(cv_s), vs], 1))
    wb = ck_w.shape[1]
    kw_all = jnp.concatenate([ck_w, kw], 1)
    vw_all = jnp.concatenate([cv_w, vw], 1)
    kpos = past - wb + jnp.arange(wb + T, dtype=jnp.int32)
    o = nsa_attend(q, pos, g, kc, vc, ks_blk, vs_blk, kw_all, vw_all, kpos)
    y = (o * jax.nn.silu(z)) @ w_out
    return y, (kcr, vcr, ks, vs, kw_all[:, T:], vw_all[:, T:])


def retention_chunkwise(q, k, v, s0):
    B, T, H, _ = q.shape
    c = RET_CHUNK if T % RET_CHUNK == 0 else T
    nch = T // c
    log_g = jnp.log1p(-jnp.power(2.0, -5.0 - jnp.arange(H, dtype=F32)))
    i = jnp.arange(c, dtype=F32)
    diff = i[:, None] - i[None, :]
    dmask = jnp.where(diff >= 0, jnp.exp(jnp.maximum(diff, 0.0)[None] * log_g[:, None, None]), 0.0)
    q_dec = jnp.exp((i + 1.0)[None, :] * log_g[:, None])[..., None]
    k_dec = jnp.exp((c - 1.0 - i)[None, :] * log_g[:, None])[..., None]
    c_dec = jnp.exp(c * log_g)[:, None, None]
    chunks = lambda a: a.astype(F32).reshape(B, nch, c, H, a.shape[-1]).transpose(1, 0, 3, 2, 4)

    def step(s, xs):
        qc, kc, vc = xs
        inner = jnp.einsum('bhid,bhjd->bhij', qc, kc) * dmask
        o = jnp.einsum('bhij,bhje->bhie', inner, vc) + jnp.einsum('bhid,bhde->bhie', qc * q_dec, s)
        s = s * c_dec + jnp.einsum('bhjd,bhje->bhde', kc * k_dec, vc)
        return s, o

    s, o = lax.scan(step, s0.astype(F32), (chunks(q), chunks(k), chunks(v)))
    return o.transpose(1, 0, 3, 2, 4).reshape(B, T, H, RDV), s


def retention_mixer(x, pos, s0, w_in, gn_g, w_out):
    B, T, _ = x.shape
    proj = x @ w_in
    q, k, v, z = jnp.split(proj, [RH * RDK, 2 * RH * RDK, 2 * RH * RDK + RET_W], axis=-1)
    q = xpos_rotate(q.reshape(B, T, RH, RDK), pos)
    k = xpos_rotate(k.reshape(B, T, RH, RDK), pos) * (RDK ** -0.5)
    o, s = retention_chunkwise(q, k, v.reshape(B, T, RH, RDV), s0)
    d = o - jnp.mean(o, -1, keepdims=True)
    o = d * lax.rsqrt(jnp.mean(d * d, -1, keepdims=True) + LN_EPS)
    o = (o.reshape(B, T, RET_W) * gn_g.astype(F32)).astype(x.dtype)
    y = (jax.nn.silu(z) * o) @ w_out
    return y, s


def setup_inputs(seed: int = 0) -> dict:
    key = jax.random.key(seed)
    ks = jax.random.split(key, 24)
    n_pages = PAST_LEN // PAGE_SIZE
    n_used = DEC_BATCH * n_pages
    n_pool = n_used + max(1, n_used // 4)
    wb = min(WINDOW, PAST_LEN)
    nrm = lambda k, shape, s: jax.random.normal(k, shape, F32) * s
    pool_shape = (N_NSA, n_pool, PAGE_SIZE, KVH, DH)
    win_shape = (N_NSA, DEC_BATCH, wb, KVH, DH)
    page_table = jax.random.permutation(ks[9], n_pool)[:n_used].reshape(DEC_BATCH, n_pages).astype(jnp.int32)
    return {
        'x_prompt': nrm(ks[0], (BATCH, SEQ, D_MODEL), 1.0),
        'x_sample': nrm(ks[1], (DEC_BATCH, DEC_SEQ, D_MODEL), 1.0),
        'cache_k_cmp': nrm(ks[2], pool_shape, 1.0),
        'cache_v_cmp': nrm(ks[3], pool_shape, 1.0),
        'cache_k_sel': nrm(ks[4], pool_shape, 1.0),
        'cache_v_sel': nrm(ks[5], pool_shape, 1.0),
        'cache_k_win': nrm(ks[6], win_shape, 1.0),
        'cache_v_win': nrm(ks[7], win_shape, 1.0),
        'state_ret': nrm(ks[8], (N_RET, DEC_BATCH, RH, RDK, RDV), 0.5),
        'page_table': page_table,
        'nsa_w_in': nrm(ks[10], (N_NSA, D_MODEL, NSA_IN), D_MODEL ** -0.5),
        'nsa_w_out': nrm(ks[11], (N_NSA, NSA_W, D_MODEL), DEEPNORM_BETA * NSA_W ** -0.5),
        'nsa_pe_k': nrm(ks[12], (N_NSA, CMP_LEN, DH), 0.02),
        'nsa_w1_k': nrm(ks[13], (N_NSA, CMP_LEN * DH, CMP_HID), (CMP_LEN * DH) ** -0.5),
        'nsa_w2_k': nrm(ks[14], (N_NSA, CMP_HID, DH), CMP_HID ** -0.5),
        'nsa_pe_v': nrm(ks[15], (N_NSA, CMP_LEN, DH), 0.02),
        'nsa_w1_v': nrm(ks[16], (N_NSA, CMP_LEN * DH, CMP_HID), (CMP_LEN * DH) ** -0.5),
        'nsa_w2_v': nrm(ks[17], (N_NSA, CMP_HID, DH), CMP_HID ** -0.5),
        'ret_w_in': nrm(ks[18], (N_RET, D_MODEL, RET_IN), D_MODEL ** -0.5),
        'ret_gn_g': 1.0 + nrm(ks[19], (N_RET, RET_W), 0.02),
        'ret_w_out': nrm(ks[20], (N_RET, RET_W, D_MODEL), DEEPNORM_BETA * RET_W ** -0.5),
        'ln_g': 1.0 + nrm(ks[21], (DEPTH, D_MODEL), 0.02),
        'ln_b': nrm(ks[22], (DEPTH, D_MODEL), 0.02),
    }


def reference(x_prompt, x_sample, cache_k_cmp, cache_v_cmp, cache_k_sel, cache_v_sel,
              cache_k_win, cache_v_win, state_ret, page_table,
              nsa_w_in, nsa_w_out, nsa_pe_k, nsa_w1_k, nsa_w2_k, nsa_pe_v, nsa_w1_v, nsa_w2_v,
              ret_w_in, ret_gn_g, ret_w_out, ln_g, ln_b):
    xp, xs = x_prompt, x_sample
    bp, tp = xp.shape[:2]
    ts = xs.shape[1]
    past = page_table.shape[1] * PAGE_SIZE
    nsa_p, nsa_s, ret_p, ret_s = [], [], [], []
    for i in range(DEPTH):
        li = i // N_MIXERS
        if i % N_MIXERS == 0:
            w = (nsa_w_in[li], nsa_w_out[li], nsa_pe_k[li], nsa_w1_k[li], nsa_w2_k[li],
                 nsa_pe_v[li], nsa_w1_v[li], nsa_w2_v[li])
            yp, st_p = nsa_prompt(xp, *w)
            ys, st_s = nsa_sample(xs, page_table, cache_k_cmp[li], cache_v_cmp[li], cache_k_sel[li],
                                  cache_v_sel[li], cache_k_win[li], cache_v_win[li], *w)
            nsa_p.append(st_p)
            nsa_s.append(st_s)
        else:
            s0 = jnp.zeros((bp, RH, RDK, RDV), F32)
            yp, sp = retention_mixer(xp, jnp.arange(tp, dtype=jnp.int32), s0,
                                     ret_w_in[li], ret_gn_g[li], ret_w_out[li])
            ys, ss = retention_mixer(xs, past + jnp.arange(ts, dtype=jnp.int32), state_ret[li],
                                     ret_w_in[li], ret_gn_g[li], ret_w_out[li])
            ret_p.append(sp)
            ret_s.append(ss)
        xp = layer_norm(DEEPNORM_ALPHA * xp + yp, ln_g[i], ln_b[i])
        xs = layer_norm(DEEPNORM_ALPHA * xs + ys, ln_g[i], ln_b[i])
    stk = lambda lst, j: jnp.stack([e[j] for e in lst])
    return (xp, xs,
            stk(nsa_p, 0), stk(nsa_s, 0), stk(nsa_p, 1), stk(nsa_s, 1),
            stk(nsa_p, 2), stk(nsa_s, 2), stk(nsa_p, 3), stk(nsa_s, 3),
            stk(nsa_p, 4), stk(nsa_s, 4), stk(nsa_p, 5), stk(nsa_s, 5),
            jnp.stack(ret_p), jnp.stack(ret_s))
```

```python
import os
from contextlib import ExitStack

import numpy as np
import concourse.bass as bass
import concourse.mybir as mybir
from concourse.bass_utils import run_bass_kernel_spmd

F32 = mybir.dt.float32
BF16 = mybir.dt.bfloat16
I32 = mybir.dt.int32
ALU = mybir.AluOpType
AF = mybir.ActivationFunctionType
AX = mybir.AxisListType

T = 8192
NT = T // 128
D = 1024
KC = D // 128
NEGB = -30000.0
ALPHA = 8.0 ** 0.25
LN_EPS = 1e-5


class Buf:
    __slots__ = ("name", "lw", "rd_eng", "rd_dma")

    def __init__(self, name):
        self.name = name
        self.lw = None
        self.rd_eng = {}
        self.rd_dma = []


class Op:
    __slots__ = ("eng", "fn", "deps", "kind", "sem", "val", "need", "semkey")


class Prog:
    def __init__(self, nc):
        self.nc = nc
        self.ops = []
        self.bufs = {}

    def buf(self, name):
        b = self.bufs.get(name)
        if b is None:
            b = Buf(name)
            self.bufs[name] = b
        return b

    def add(self, eng, fn, r=(), w=(), kind="c", semkey=None):
        op = Op()
        op.eng, op.fn, op.kind, op.semkey = eng, fn, kind, semkey
        op.need = kind != "c"
        op.sem = None
        op.val = 0
        deps = {}
        for b in r:
            if b.lw is not None:
                deps[id(b.lw)] = b.lw
        for b in w:
            if b.lw is not None:
                deps[id(b.lw)] = b.lw
            for o in b.rd_eng.values():
                deps[id(o)] = o
            for o in b.rd_dma:
                deps[id(o)] = o
        for b in r:
            if kind == "c":
                b.rd_eng[eng] = op
            else:
                b.rd_dma.append(op)
        for b in w:
            b.lw = op
            b.rd_eng = {}
            b.rd_dma = []
        dl = []
        for d in deps.values():
            if d is op:
                continue
            if d.kind == "c" and kind == "c" and d.eng == "pe" and eng == "pe":
                continue
            d.need = True
            dl.append(d)
        op.deps = dl
        self.ops.append(op)
        return op

    def mm(self, out, lhsT, rhs, start, stop, r, w, **kw):
        return self.add("pe", lambda e: e.matmul(out, lhsT, rhs, start=start, stop=stop, **kw), r, w)

    def tr(self, out, in_, ident, r, w):
        return self.add("pe", lambda e: e.transpose(out, in_, ident), r, w)

    def act(self, out, in_, func, r, w, **kw):
        return self.add("act", lambda e: e.activation(out, in_, func, **kw), r, w)

    def dma(self, eng, out, in_, r, w, semkey):
        return self.add(eng, lambda e: e.dma_start(out=out, in_=in_), r, w, kind="d", semkey=semkey)

    def emit(self, es):
        nc = self.nc
        engsem = {}
        for e in ("pe", "act", "dve", "pool", "sp"):
            engsem[e] = es.enter_context(nc.semaphore("sem_" + e))
        dsem = {}
        cnt = {e: 0 for e in engsem}
        dcnt = {}
        for op in self.ops:
            if op.kind == "c":
                if op.need:
                    cnt[op.eng] += 1
                    op.sem = engsem[op.eng]
                    op.val = cnt[op.eng]
            else:
                k = op.semkey
                if k not in dsem:
                    dsem[k] = es.enter_context(nc.semaphore("dsem_%d" % len(dsem)))
                    dcnt[k] = 0
                dcnt[k] += 16 if op.kind == "d" else 1
                op.sem = dsem[k]
                op.val = dcnt[k]
        self.n_sems = len(dsem) + 5
        ops = self.ops
        final = [(dsem[k], dcnt[k]) for k in dsem]

        def run(name, is_last=False):
            def f(e):
                waited = {}
                for op in ops:
                    if op.eng != name:
                        continue
                    need = {}
                    for d in op.deps:
                        key = id(d.sem)
                        if need.get(key, (None, 0))[1] < d.val:
                            need[key] = (d.sem, d.val)
                    for key, (sem, val) in need.items():
                        if waited.get(key, 0) < val:
                            e.wait_ge(sem, val)
                            waited[key] = val
                    ins = op.fn(e)
                    if op.kind == "d":
                        ins.then_inc(op.sem, 16)
                    elif op.kind == "cc":
                        ins.then_inc(op.sem)
                    elif op.need:
                        ins.then_inc(op.sem, 1)
                if is_last:
                    for sem, val in final:
                        e.wait_ge(sem, val)
            return f

        with nc.Block() as block:
            block.tensor(run("pe"))
            block.scalar(run("act"))
            block.vector(run("dve"))
            block.gpsimd(run("pool"))
            block.sync(run("sp", True))


def bcast(ap, pos, n):
    a = [list(x) for x in ap.ap]
    a.insert(pos, [0, n])
    return bass.AP(ap.tensor, ap.offset, a)


def _rope_table():
    half = 8
    inv = (np.float32(500000.0) ** (-np.arange(half, dtype=np.float32) / np.float32(half))).astype(np.float32)
    pos = np.arange(T, dtype=np.float32)
    ang = (pos[:, None] * inv[None, :]).astype(np.float32)
    tab = np.concatenate([np.cos(ang), np.sin(ang)], -1).astype(np.float32)
    return np.ascontiguousarray(tab.reshape(NT, 128, 16).transpose(1, 0, 2))


def _consts_bf16like():
    c = {}
    c["ident"] = np.eye(128, dtype=np.float32)
    m = np.arange(128)[:, None]
    r = np.arange(128)[None, :]
    c["triu"] = np.where(m <= r, 0.0, NEGB).astype(np.float32)
    c["tril"] = np.where(m >= r, 0.0, NEGB).astype(np.float32)
    xx = np.arange(4096)[None, :]
    c["ew"] = ((np.arange(128)[:, None] % 64) == (2 * (xx // 128) + (xx % 128) // 64)).astype(np.float32)
    p = np.arange(32)[:, None]
    x = np.arange(272)[None, :]
    c["lw"] = ((p == np.clip(x - 126, 0, 10)) & (p <= 10)).astype(np.float32)
    rr = np.arange(128)[None, :]
    wb = np.where((p - 1) <= ((rr + 1) // 16), 0.0, NEGB)
    wb[11:] = 0.0
    c["wb"] = wb.astype(np.float32)
    cst = np.arange(512)[:, None] * 16
    sst = np.arange(128)[None, :] * 64
    ov = np.clip(np.minimum(cst + 32, sst + 64) - np.maximum(cst, sst), 0, None).astype(np.float32) / 32.0
    ov[511:] = 0.0
    c["ov"] = np.ascontiguousarray(ov.reshape(4, 128, 128).transpose(1, 0, 2))
    r_ = np.arange(128)[:, None]
    sp = np.arange(254)[None, :] - 126
    bq = (r_ >= 64).astype(np.int64)
    forced = (sp == bq) | (sp == bq - 1)
    c["tb"] = (1000.0 * forced + np.where(sp <= bq, 0.0, -1.0e30)).astype(np.float32)
    return c


def _pool_slice(inp, nm, l, k):
    key = (nm, l, k)
    if key not in _CACHE["pool"]:
        a = inp[nm][l][:, :, k, :]
        _CACHE["pool"][key] = np.ascontiguousarray(a).reshape(a.shape[0], 8192)
    return _CACHE["pool"][key]


def _sample_tables():
    t = {}
    p = np.arange(128)
    s8, j16 = p // 16, p % 16
    t["d8"] = (s8[:, None] == np.arange(8)[None, :]).astype(np.float32)
    t["d0"] = ((s8[:, None] == np.arange(8)[None, :]) & (j16[:, None] == 0)).astype(np.float32)
    t["dd"] = (s8[:, None] == s8[None, :]).astype(np.float32)
    cm = np.zeros((128, 8), np.float32)
    cm[j16 == 15, 7] = NEGB
    t["cmaskl"] = cm
    cst = (8 * j16[:, None] + np.arange(8)[None, :]) * 16
    sst = np.arange(33) * 64
    ov = np.clip(np.minimum(cst[:, :, None] + 32, sst[None, None, :] + 64) - np.maximum(cst[:, :, None], sst[None, None, :]), 0, None) / 32.0
    ov[j16 == 15, 7, :] = 0.0
    t["ovt"] = np.ascontiguousarray(ov.astype(np.float32).reshape(128, 264))
    jb = np.arange(33)
    forced = (jb == 0) | (jb == 32) | (jb == 31)
    t["tbs"] = np.ascontiguousarray(np.broadcast_to((1000.0 * forced).astype(np.float32)[None, :], (64, 33)))
    return t


def _xpos_tables():
    half = 128
    inv = (1.0 / (np.float32(10000.0) ** np.linspace(0.0, 1.0, half, dtype=np.float32))).astype(np.float32)
    pos = np.arange(T, dtype=np.float32)
    ang = (pos[None, :] * inv[:, None]).astype(np.float32)
    return np.ascontiguousarray(np.cos(ang).astype(np.float32)), np.ascontiguousarray(np.sin(ang).astype(np.float32))


def _decay_tables(h):
    lg = np.log1p(-np.float64(2.0) ** (-5.0 - h))
    i = np.arange(128, dtype=np.float64)
    diff = i[None, :] - i[:, None]
    dm = np.where(diff >= 0, np.exp(np.maximum(diff, 0.0) * lg), 0.0) / 16.0
    qd = np.broadcast_to(np.exp((i + 1.0) * lg)[None, :], (128, 128))
    kc = np.stack([np.exp((127.0 - i) * lg) / 16.0, np.full(128, np.exp(128.0 * lg)), np.full(128, np.exp(lg))], 1)
    return {"dmaskT": np.ascontiguousarray(dm.astype(np.float32)), "qdec": np.ascontiguousarray(qd.astype(np.float32)),
            "kcdec": np.ascontiguousarray(kc.astype(np.float32))}


def build(stage):
    nc = bass.Bass("TRN2", target_bir_lowering=False)
    P = Prog(nc)
    es = ExitStack()
    NL = int(os.environ.get("KNL", "4"))
    NQT = int(os.environ.get("KNQT", str(NT)))
    NT1 = int(os.environ.get("KNT1", str(NT)))

    def din(name, shape, dt=F32):
        return nc.dram_tensor(name, list(shape), dt, kind="ExternalInput").ap()

    def dout(name, shape, dt=F32):
        return nc.dram_tensor(name, list(shape), dt, kind="ExternalOutput").ap()

    def sb(name, shape, dt):
        return es.enter_context(nc.sbuf_tensor("s_" + name, list(shape), dt))

    xp = din("xp", [T, D])
    rope_d = din("rope", [128, NT, 16])
    ident_d = din("ident", [128, 128])
    triu_d = din("triu", [128, 128])
    tril_d = din("tril", [128, 128])
    ew_d = din("ew", [128, 4096])
    lw_d = din("lw", [32, 272])
    wb_d = din("wb", [32, 128])
    ov_d = din("ov", [128, 4, 128])
    tb_d = din("tb", [128, 254])
    lng_d = din("ln_g", [4, D])
    lnb_d = din("ln_b", [4, D])
    nsa_win = [din("nsa_win%d" % l, [D, 1432]) for l in range(2)]
    nsa_wout = [din("nsa_wout%d" % l, [512, D]) for l in range(2)]
    nsa_w1 = [din("nsa_w1_%d" % l, [128, 32 * 128]) for l in range(2)]
    nsa_pe = [din("nsa_pe%d" % l, [128, 32]) for l in range(2)]
    nsa_w2 = [din("nsa_w2_%d" % l, [128, 192]) for l in range(2)]
    ret_win = [din("ret_win%d" % l, [D, 1536]) for l in range(2)]
    ret_wout = [din("ret_wout%d" % l, [512, D]) for l in range(2)]
    ret_gn = [din("ret_gn%d" % l, [1, 512]) for l in range(2)]
    xcos_d = din("xcos", [128, T])
    xsin_d = din("xsin", [128, T])
    dmask_d = din("dmaskT", [128, 128])
    qdec_d = din("qdec", [128, 128])
    kcdec_d = din("kcdec", [128, 3])
    kvout = [dout("kvout%d" % l, [T, 384]) for l in range(2)]
    retout = [dout("retout%d" % l, [256, 512]) for l in range(2)]
    SAMPLE = int(os.environ.get("KSAMPLE", "1"))
    xs_d = din("xs", [64, D])
    sret_d = [din("sret%d" % l, [64, 256, 512]) for l in range(2)]
    xp2048_d = din("xp2048", [1, 256])
    ys_out = dout("ys_out", [64, D])
    rets_out = [dout("rets_out%d" % l, [64, 256, 512]) for l in range(2)]
    pt_d = din("ptab", [8, 128], I32)
    pools = [[din("pool%d_%d" % (l, j), [2560, 8192]) for j in range(4)] for l in range(2)]
    cwin = [[din("cwin%d_%d" % (l, j), [64, 32768]) for j in range(2)] for l in range(2)]
    nsa_w1s = [din("nsa_w1s%d" % l, [128, 4096]) for l in range(2)]
    d8_d = din("d8", [128, 8]); d0_d = din("d0", [128, 8]); dd_d = din("dd", [128, 128]); cmask_d = din("cmaskl", [128, 8])
    ovt_d = din("ovt", [128, 264]); tbs_d = din("tbs", [64, 33]); rope2048_d = din("rope2048", [1, 16])
    kvs_out = [dout("kvs_out%d" % l, [64, 384]) for l in range(2)]
    wins_out = [[dout("wins_out%d_%d" % (l, j), [64, 32768]) for j in range(2)] for l in range(2)]
    sq_d = nc.dram_tensor("sq_d", [64, 920], F32).ap()
    selb_d = nc.dram_tensor("selb_d", [64, 32], F32).ap()
    so_d = nc.dram_tensor("so_d", [512, 64], F32).ap()
    xs_cur = nc.dram_tensor("xs_cur", [64, D], F32).ap()
    ypart_s = [nc.dram_tensor("ypart_s%d" % l, [64, D], F32) for l in range(4)]
    ysum_s = [nc.dram_tensor("ysum_s%d" % l, [64, D], F32) for l in range(4)]
    yout = dout("yout", [T, D])
    xT = nc.dram_tensor("xT", [KC, 128, T], BF16).ap()
    xcur = nc.dram_tensor("xcur", [T, D], F32).ap()
    ypart = [[nc.dram_tensor("ypart%d_%d" % (l, c), [1024, D], F32) for c in range(8)] for l in range(4)]
    ysum = [[nc.dram_tensor("ysum%d_%d" % (l, c), [1024, D], F32) for c in range(8)] for l in range(4)]

    ident_f = sb("ident_f", [128, 128], F32)
    ident = sb("ident", [128, 128], BF16)
    rope = sb("rope_sb", [128, NT, 16], F32)
    fbuf = [sb("fbuf%d" % i, [128, D], F32) for i in range(4)]
    xin = fbuf[0:2]
    xbf = [sb("xbf%d" % i, [128, D], BF16) for i in range(2)]
    xTt = [sb("xTt%d" % i, [128, KC * 128], BF16) for i in range(2)]
    stg = [sb("stg%d" % i, [128, 1536], F32) for i in range(2)]
    w_nsa = sb("w_in", [128, KC, 1536], BF16)
    w_out = sb("w_out", [128, 4, D], BF16)
    kvsb = [sb("kvsb%d" % i, [128, 384], F32) for i in range(2)]
    ropetmp = [sb("ropetmp%d" % i, [128, 4, 64], F32) for i in range(2)]
    kstage = [sb("kstage%d" % i, [128, 2, 128], BF16) for i in range(2)]
    KT = sb("KT", [128, 2, T], BF16)
    Vs = sb("Vs", [128, NT, 65], BF16)
    Vw = sb("Vw", [128, NT, 65], BF16)
    Vc = sb("Vc", [128, 4, 65], BF16)
    KcT2 = sb("KcT2", [128, 512], BF16)
    w1sb = sb("w1sb", [128, 32, 128], BF16)
    pesb = sb("pesb", [128, 32], BF16)
    w2sb = sb("w2sb", [128, 192], BF16)
    hb = sb("hbias", [128, 2], F32)
    hsb = [sb("hsb%d" % i, [128, 512], BF16) for i in range(2)]
    gtmp = fbuf[0:3]
    ew = sb("ew", [128, 4096], BF16)
    lw = sb("lw", [32, 272], BF16)
    wb4 = sb("wb4", [32, 512], BF16)
    triu4 = sb("triu4", [128, 512], BF16)
    tril4 = sb("tril4", [128, 512], BF16)
    ov = sb("ov", [128, 4, 128], BF16)
    tb = sb("tb", [128, 254], F32)
    lng, lnb = stg[0], stg[1]
    xq = [sb("xq%d" % i, [128, KC, 128], BF16) for i in range(2)]
    qf = sb("qf", [128, 512], F32)
    qbf = sb("qbf", [128, 1024], BF16)
    QT = sb("QT", [128, 1024], BF16)
    gsb = sb("gsb", [128, 24], F32)
    szs = sb("szs", [128, 512], F32)
    PT = [sb("PT%d" % i, [128, 512], BF16) for i in range(3)]
    PTc = sb("PTc", [128, 8, 512], BF16)
    oacc = sb("oacc", [128, 512], F32)
    otmp = sb("otmp", [128, 512], F32)
    rz = sb("rz", [128, 3, 8], F32)
    wgt = sb("wgt", [128, 3, 8], F32)
    imp = sb("imp", [128, 128], F32)
    score = sb("score", [128, 128], F32)
    score2 = sb("score2", [128, 128], F32)
    m8 = sb("m8", [128, 16], F32)
    selb = sb("selb", [128, 128], BF16)
    selT4 = sb("selT4", [128, 512], BF16)
    og = sb("og", [128, 512], BF16)
    ogT = sb("ogT", [128, 512], BF16)
    ysb = fbuf[0:2]
    lnx = fbuf[0:2]
    lny = fbuf[2:4]
    lnst = [sb("lnst%d" % i, [128, 8], F32) for i in range(2)]
    zeros = sb("zeros", [128, 512], BF16)
    xcs = [sb("xcs%d" % i, [128, 2, 128], F32) for i in range(2)]
    dmaskT = sb("dmaskT", [128, 128], F32)
    qdec = sb("qdec", [128, 128], F32)
    kcdec = sb("kcdec", [128, 3], F32)
    xsT = sb("xsT", [128, 512], BF16)
    zp = sb("zp", [128, 8], F32); rZs = sb("rZs", [128, 8], F32); Ws = sb("Ws", [128, 8], F32); pcs = sb("pcs", [128, 8], F32)
    ssm = sb("ssm", [128, 8], F32); pself = sb("pself", [128, 8], F32); wself = sb("wself", [128, 8], F32)
    pnL = sb("pnL", [128, 64], F32); Wse = sb("Wse", [128, 64], BF16); vsn = sb("vsn", [128, 64], BF16)
    ssm64 = sb("ssm64", [128, 64], F32); selbL = sb("selbL", [128, 2], F32); ptL = sb("ptL", [128, 8], I32)
    d8 = sb("d8", [128, 8], F32); d0 = sb("d0", [128, 8], F32); cmaskL = sb("cmaskL", [128, 8], F32)
    ddsb = sb("ddsb", [128, 128], F32); ovt = sb("ovt", [128, 8, 33], BF16); tbs = sb("tbs", [64, 33], F32)
    rp2048 = sb("rp2048", [64, 16], F32)
    rtmp = sb("rtmp", [128, 4, 256], F32)
    QKr = sb("QKr", [128, 4, 128], BF16)
    qdT = sb("qdT", [128, 2, 128], BF16)
    kdsb = sb("kdsb", [128, 256], BF16)
    vbf = sb("vbf", [128, 512], BF16)
    ATs = sb("ATs", [128, 128], BF16)
    Sst = sb("Sst", [128, 2, 512], F32)
    Sbf = sb("Sbf", [128, 2, 512], BF16)
    gng = sb("gng", [128, 512], F32)

    if os.environ.get("KDBG"):
        print("SBUF bytes remaining:", nc.sbuf_bytes_remaining)
    ps = [es.enter_context(nc.psum_tensor("ps%d" % i, [128, 512], F32)) for i in range(8)]
    psb = [p[:, :].bitcast(BF16) for p in ps]

    B = P.buf
    PSB = [B("ps%d" % i) for i in range(8)]

    def dve(fn, r, w):
        return P.add("dve", fn, r, w)

    def pool(fn, r, w):
        return P.add("pool", fn, r, w)

    stg_i = [0]

    def load_bf16(dst_ap, src_ap, npart, ncols, dst_bufs):
        s = stg_i[0] % 2
        stg_i[0] += 1
        P.dma("sp", stg[s][0:npart, 0:ncols], src_ap, [], [B("stg%d" % s)], "stg%d" % s)
        P.act(dst_ap, stg[s][0:npart, 0:ncols], AF.Copy, [B("stg%d" % s)], dst_bufs)

    P.dma("sp", ident_f[:, :], ident_d[:, :], [], [B("ident_f")], "c0")
    dve(lambda e: e.tensor_copy(ident[:, :], ident_f[:, :]), [B("ident_f")], [B("ident")])
    P.dma("sp", rope[:, :, :], rope_d[:, :, :], [], [B("rope")], "c1")
    P.dma("sp", tb[:, :], tb_d[:, :], [], [B("tb")], "c2")
    pool(lambda e: e.memset(Vs[:, :, 64:65], 1.0), [], [B("Vs_ones")])
    pool(lambda e: e.memset(Vw[:, :, 64:65], 1.0), [], [B("Vw_ones")])
    pool(lambda e: e.memset(Vc[:, :, 64:65], 1.0), [], [B("Vc_ones")])
    pool(lambda e: e.memset(zeros[:, :], 0.0), [], [B("zeros")])
    pool(lambda e: e.memset(hsb[1][:, 511:512], 0.0), [], [B("hsb1")])
    for (dst_, src_, nm_) in ((d8, d8_d, "d8"), (d0, d0_d, "d0"), (cmaskL, cmask_d, "cmaskL"), (ddsb, dd_d, "ddsb"), (tbs, tbs_d, "tbs")):
        P.dma("sp", dst_[:, :], src_[:, :], [], [B(nm_)], "c_" + nm_)
    load_bf16(ovt[:, :, :].rearrange("p a b -> p (a b)"), ovt_d[:, :], 128, 264, [B("ovt")])
    for g_ in range(8):
        P.dma("sp", ptL[:, g_:g_ + 1], bass.AP(pt_d.tensor, 128 * g_, [[1, 128], [1, 1]]), [], [B("ptL")], "c_ptL")
    for c in range(4):
        load_bf16(ew[:, c * 1024:(c + 1) * 1024], ew_d[:, c * 1024:(c + 1) * 1024], 128, 1024, [B("ew")])
    load_bf16(lw[:, :], lw_d[:, :], 32, 272, [B("lw")])
    load_bf16(ov[:, :, :].rearrange("p a b -> p (a b)"), ov_d[:, :, :].rearrange("p a b -> p (a b)"), 128, 512, [B("ov")])
    for (dst, src, npart, nm) in ((wb4, wb_d, 32, "wb4"), (triu4, triu_d, 128, "triu4"), (tril4, tril_d, 128, "tril4")):
        s = stg_i[0] % 2
        stg_i[0] += 1
        P.dma("sp", stg[s][0:npart, 0:128], src[:, :], [], [B("stg%d" % s)], "stg%d" % s)
        P.act(dst[0:npart, :].rearrange("p (a b) -> p a b", a=4), bcast(stg[s][0:npart, 0:128], 1, 4), AF.Copy,
              [B("stg%d" % s)], [B(nm)])

    def to_xT(t, src_ap, src_bufs, slot):
        xb = xbf[slot]
        P.act(xb[:, :], src_ap, AF.Copy, src_bufs, [B("xbf%d" % slot)])
        pb = psb[6 + slot]
        for c in range(KC):
            P.tr(pb[:, c * 128:(c + 1) * 128], xb[:, c * 128:(c + 1) * 128], ident[:, :],
                 [B("xbf%d" % slot), B("ident")], [PSB[6 + slot]])
        xt = xTt[slot]
        dve(lambda e: e.tensor_copy(xt[:, :], pb[:, :]), [PSB[6 + slot]], [B("xTt%d" % slot)])
        P.dma("sp", xT[:, :, t * 128:(t + 1) * 128].rearrange("c p t -> p c t"),
              xt[:, :].rearrange("p (c t) -> p c t", c=KC),
              [B("xTt%d" % slot)], [B("xT_%d" % t)], "xTst%d" % slot)

    if NT1 < NT:
        pool(lambda e: e.memset(KT[:, :, :], 0.0), [], [B("KT_%d" % t) for t in range(NT)])
        pool(lambda e: e.memset(Vs[:, :, 0:64], 0.0), [], [B("Vs_%d" % t) for t in range(NT)])
        pool(lambda e: e.memset(Vw[:, :, 0:64], 0.0), [], [B("Vw_%d" % t) for t in range(NT)])
    for t in range(NT1):
        s = t % 2
        P.dma("sp", xin[s][:, :], xp[t * 128:(t + 1) * 128, :], [], [B("fbuf%d" % s)], "fbufld%d" % s)
        to_xT(t, xin[s][:, :], [B("fbuf%d" % s)], s)

    def nsa_layer(li, layer):
        for c in range(KC):
            load_bf16(w_nsa[:, c, 0:1432], nsa_win[li][c * 128:(c + 1) * 128, :], 128, 1432, [B("w_nsa")])
        for c in range(4):
            load_bf16(w_out[:, c, :], nsa_wout[li][c * 128:(c + 1) * 128, :], 128, D, [B("w_out")])
        for c in range(4):
            load_bf16(w1sb[:, 8 * c:8 * c + 8, :].rearrange("p a b -> p (a b)"), nsa_w1[li][:, 1024 * c:1024 * c + 1024], 128, 1024, [B("w1sb")])
        load_bf16(pesb[:, :], nsa_pe[li][:, :], 128, 32, [B("pesb")])
        load_bf16(w2sb[:, :], nsa_w2[li][:, :], 128, 192, [B("w2sb")])
        for kv_i in range(2):
            lo = 64 * kv_i
            pbias = ps[7 - 3 * kv_i]
            for l in range(32):
                P.mm(pbias[:, 0:1], w1sb[lo:lo + 64, l, :], pesb[lo:lo + 64, l:l + 1], l == 0, l == 31,
                     [B("w1sb"), B("pesb")], [PSB[7 - 3 * kv_i]])
            dve(lambda e, pbias=pbias, kv_i=kv_i: e.tensor_copy(hb[:, kv_i:kv_i + 1], pbias[:, 0:1]), [PSB[7 - 3 * kv_i]], [B("hb")])
        if SAMPLE:
            nsa_sample(li, layer)
            pool(lambda e: e.memset(Vs[:, :, 64:65], 1.0), [], Vskeys + [B("Vs_ones")])
            pool(lambda e: e.memset(Vw[:, :, 64:65], 1.0), [], Vwkeys + [B("Vw_ones")])
        if os.environ.get("KNOPROMPT"):
            return

        for u in range(NT1):
            for j in range(1):
                t = u
                s = t % 2
                P.dma("sp", xq[s][:, :, :], xT[:, :, t * 128:(t + 1) * 128].rearrange("c p t -> p c t"),
                      [B("xT_%d" % t)], [B("xq%d" % s)], "xq%d" % s)
                pk = ps[4 + s]
                for c in range(KC):
                    P.mm(pk[:, 0:384], xq[s][:, c, :], w_nsa[:, c, 512:896],
                         c == 0, c == KC - 1, [B("xq%d" % s), B("w_nsa")], [PSB[4 + s]])
                kv = kvsb[s]
                P.act(kv[:, :], pk[:, 0:384], AF.Copy, [PSB[4 + s]], [B("kvsb%d" % s)])
                kv3 = kv[:, :].rearrange("p (a b) -> p a b", a=3)
                rope_apply(kv3[:, :, 0:8], kv3[:, :, 8:16], 3, t, ropetmp[s], "rt%d" % s, [B("kvsb%d" % s)], [B("kvsb%d" % s)])
                P.dma("sp", kvout[li][t * 128:(t + 1) * 128, :], kv[:, :], [B("kvsb%d" % s)],
                      [B("kvout_%d_%d" % (li, t))], "kvo%d" % s)
                kst = kstage[s]
                sbk = [B("kvsb%d" % s)]
                pool(lambda e, kst=kst, kv=kv: e.tensor_copy(kst[:, 0, :], kv[:, 0:128]), sbk, [B("kst%d_0" % s)])
                gs = t // 32
                pool(lambda e, kst=kst, kv=kv, gs=gs: e.tensor_copy(kst[:, 1, 64 * gs:64 * gs + 64], kv[:, 128:192]), sbk, [B("kst%d_1" % s)])
                pool(lambda e, kst=kst, kv=kv, gs=gs: e.tensor_copy(kst[:, 1, 64 - 64 * gs:128 - 64 * gs], kv[:, 256:320]), sbk + [B("kst%d_1" % s)], [B("kst%d_1" % s)])
                pb = psb[6 + s]
                for a in range(2):
                    P.tr(pb[:, a * 128:(a + 1) * 128], kst[:, a, :], ident[:, :], [B("kst%d_%d" % (s, a)), B("ident")], [PSB[6 + s]])
                dve(lambda e, pb=pb, t=t: e.tensor_copy(KT[:, :, t * 128:(t + 1) * 128], pb[:, 0:256].rearrange("p (a t) -> p a t", a=2)),
                    [PSB[6 + s]], [B("KT_%d" % t)])
                pool(lambda e, kv=kv, t=t: e.tensor_copy(Vs[:, t, 0:64], kv[:, 192:256]), sbk, [B("Vs_%d" % t)])
                pool(lambda e, kv=kv, t=t: e.tensor_copy(Vw[:, t, 0:64], kv[:, 320:384]), sbk, [B("Vw_%d" % t)])
        KTall = [B("KT_%d" % t) for t in range(NT)]

        for kv_i in range(2):
            lo = 64 * kv_i
            ph = ps[5 + kv_i]
            for l in range(32):
                rhs = KT[lo:lo + 64, 0, l:l + 16 * 510 + 1:16]
                P.mm(ph[:, 0:511], w1sb[lo:lo + 64, l, :], rhs, l == 0, l == 31, [B("w1sb")] + KTall, [PSB[5 + kv_i]])
            g0, g1, g2 = gtmp
            P.act(g0[:, 0:511], ph[:, 0:511], AF.Identity, [PSB[5 + kv_i], B("hb")], [B("fbuf0")], bias=hb[:, kv_i:kv_i + 1])
            dve(lambda e: e.tensor_tensor(g1[:, 0:511], g0[:, 0:511], g0[:, 0:511], ALU.mult), [B("fbuf0")], [B("fbuf1")])
            dve(lambda e: e.tensor_scalar(g1[:, 0:511], g1[:, 0:511], 0.044715, 1.0, ALU.mult, ALU.add), [B("fbuf1")], [B("fbuf1")])
            dve(lambda e: e.tensor_tensor(g1[:, 0:511], g1[:, 0:511], g0[:, 0:511], ALU.mult), [B("fbuf1"), B("fbuf0")], [B("fbuf1")])
            P.act(g2[:, 0:511], g1[:, 0:511], AF.Sigmoid, [B("fbuf1")], [B("fbuf2")], scale=1.5957691216)
            h = hsb[kv_i]
            dve(lambda e, h=h: e.tensor_tensor(h[:, 0:511], g2[:, 0:511], g0[:, 0:511], ALU.mult), [B("fbuf2"), B("fbuf0")], [B("hsb%d" % kv_i)])
        pk2 = ps[7]
        P.mm(pk2[:, 0:511], w2sb[:, 0:128], hsb[0][:, 0:511], True, True, [B("w2sb"), B("hsb0")], [PSB[7]])
        dve(lambda e: e.tensor_copy(KcT2[:, 0:511], pk2[:, 0:511]), [PSB[7]], [B("KcT2")])
        for ci in range(4):
            n = min(128, 511 - 128 * ci)
            pv = ps[4]
            P.mm(pv[0:n, 0:64], hsb[1][:, 128 * ci:128 * ci + n], w2sb[:, 128:192], True, True, [B("w2sb"), B("hsb1")], [PSB[4]])
            dve(lambda e, ci=ci, n=n, pv=pv: e.tensor_copy(Vc[0:n, ci, 0:64], pv[0:n, 0:64]), [PSB[4]], [B("Vc")])

        srot = [0]

        def unit(lhsT_k, lo, nkeys, half, masks, pt_ap, pt_buf, kbufs):
            si = srot[0] % 3
            srot[0] += 1
            S = ps[si]
            nm = len(masks)
            P.mm(S[0:nkeys, :], lhsT_k, QT[lo:lo + 64, half * 512:(half + 1) * 512], True, nm == 0, kbufs + [B("QT")], [PSB[si]])
            for mi, (ml, mr, mb) in enumerate(masks):
                P.mm(S[0:nkeys, :], ml, mr, False, mi == nm - 1, mb, [PSB[si]])
            P.act(pt_ap, S[0:nkeys, :], AF.Exp, [PSB[si]], [pt_buf], scale=0.125)

        def pv_acc(pt_ap, pt_buf, nkeys, v_ap, vbufs, half, first, last):
            O = ps[3 + half]
            if first:
                P.mm(O[:, 0:260], zeros[0:32, 0:128], zeros[0:32, 0:260], True, False, [B("zeros")], [PSB[3 + half]])
            for jj in range(4):
                P.mm(O[:, jj * 65:(jj + 1) * 65], pt_ap[:, jj * 128:(jj + 1) * 128], v_ap, False, last and jj == 3,
                     [pt_buf] + vbufs, [PSB[3 + half]])

        def combine(br, first_branch):
            for half in range(2):
                O3 = ps[3 + half][:, 0:260].rearrange("p (j e) -> p j e", j=4)
                rzs = rz[:, br, 4 * half:4 * half + 4]
                wg = wgt[:, br, 4 * half:4 * half + 4]
                kz = B("rz_%d_%d" % (br, half))
                kw_ = B("wgt_%d_%d" % (br, half))
                dve(lambda e, O3=O3, rzs=rzs: e.tensor_scalar(rzs, O3[:, :, 64], 1.0e-30, None, ALU.max), [PSB[3 + half]], [kz])
                dve(lambda e, rzs=rzs: e.reciprocal(rzs, rzs), [kz], [kz])
                gsl = gsb[:, br * 8 + 4 * half:br * 8 + 4 * half + 4]
                dve(lambda e, wg=wg, rzs=rzs, gsl=gsl: e.tensor_tensor(wg, rzs, gsl, ALU.mult), [kz, B("gsb")], [kw_])
                oa = oacc[:, 256 * half:256 * half + 256].rearrange("p (j e) -> p j e", j=4)
                if first_branch:
                    dve(lambda e, oa=oa, O3=O3, wg=wg: e.tensor_tensor(oa, O3[:, :, 0:64], bcast(wg, 2, 64), ALU.mult),
                        [PSB[3 + half], kw_], [B("oacc%d" % half)])
                else:
                    ot = otmp[:, 256 * half:256 * half + 256].rearrange("p (j e) -> p j e", j=4)
                    dve(lambda e, ot=ot, O3=O3, wg=wg: e.tensor_tensor(ot, O3[:, :, 0:64], bcast(wg, 2, 64), ALU.mult),
                        [PSB[3 + half], kw_], [B("otmp%d" % half)])
                    pool(lambda e, oa=oa, ot=ot: e.tensor_tensor(oa, oa, ot, ALU.add), [B("otmp%d" % half), B("oacc%d" % half)], [B("oacc%d" % half)])

        for i in range(int(os.environ.get("KQ0", "0")), NQT):
            sq = i % 2
            P.dma("sp", xq[sq][:, :, :], xT[:, :, i * 128:(i + 1) * 128].rearrange("c p t -> p c t"),
                  [B("xT_%d" % i)], [B("xq%d" % sq)], "xq%d" % sq)
            for (col0, ncol, pi) in ((0, 512, 5), (920, 512, 6), (896, 24, 7)):
                for c in range(KC):
                    P.mm(ps[pi][:, 0:ncol], xq[sq][:, c, :], w_nsa[:, c, col0:col0 + ncol], c == 0, c == KC - 1,
                         [B("xq%d" % sq), B("w_nsa")], [PSB[pi]])
            P.act(qf[:, :], ps[5][:, :], AF.Copy, [PSB[5]], [B("qf")])
            P.act(szs[:, :], ps[6][:, :], AF.Silu, [PSB[6]], [B("szs")])
            P.act(gsb[:, :], ps[7][:, 0:24], AF.Sigmoid, [PSB[7]], [B("gsb")])
            q3 = qf[:, :].rearrange("p (a b) -> p a b", a=8)
            rope_apply(q3[:, :, 0:8], q3[:, :, 8:16], 8, i, ropetmp[0], "rt0", [B("qf")], [B("qf")])
            pool(lambda e: e.tensor_copy(qbf[:, :].rearrange("p (h a d) -> p h a d", h=8, a=2), bcast(qf[:, :].rearrange("p (h d) -> p h d", h=8), 2, 2)), [B("qf")], [B("qbf")])
            pq = psb[7]
            for j in range(8):
                P.tr(pq[:, j * 128:(j + 1) * 128], qbf[:, j * 128:(j + 1) * 128], ident[:, :], [B("qbf"), B("ident")], [PSB[7]])
            dve(lambda e, pq=pq: e.tensor_copy(QT[:, :], pq[:, :]), [PSB[7]], [B("QT")])

            chunks = [ci for ci in range(4) if 128 * ci <= 8 * i + 6]
            for half in range(2):
                for k_, ci in enumerate(chunks):
                    cs = 128 * ci
                    n = min(128, 511 - cs)
                    delta = 8 * i - cs
                    masks = []
                    if delta <= 129:
                        off = 129 - delta
                        masks.append((lw[0:32, off:off + n], wb4[0:32, :], [B("lw"), B("wb4")]))
                    pt = PTc[0:n, 2 * ci + half, :]
                    ptb = B("PTc_%d_%d" % (ci, half))
                    unit(KcT2[0:64, cs:cs + n], 0, n, half, masks, pt, ptb, [B("KcT2")])
                    pv_acc(pt, ptb, n, Vc[0:n, ci, :], [B("Vc"), B("Vc_ones")], half, k_ == 0, k_ == len(chunks) - 1)
            combine(0, True)
            for half in range(2):
                for jj in range(4):
                    for k_, ci in enumerate(chunks):
                        n = min(128, 511 - 128 * ci)
                        P.mm(ps[5 + half][:, jj * 128:(jj + 1) * 128], PTc[0:n, 2 * ci + half, jj * 128:(jj + 1) * 128], ov[0:n, ci, :],
                             k_ == 0, k_ == len(chunks) - 1, [B("PTc_%d_%d" % (ci, half)), B("ov")], [PSB[5 + half]])
            first = True
            for half in range(2):
                for jj in range(4):
                    A = ps[5 + half][:, jj * 128:(jj + 1) * 128]
                    sc = rz[:, 0, 4 * half + jj:4 * half + jj + 1]
                    rb = [PSB[5 + half], B("rz_0_%d" % half)]
                    if first:
                        dve(lambda e, A=A, sc=sc: e.tensor_scalar(imp[:, :], A, sc, None, ALU.mult), rb, [B("imp")])
                        first = False
                    else:
                        dve(lambda e, A=A, sc=sc: e.scalar_tensor_tensor(imp[:, :], A, sc, imp[:, :], ALU.mult, ALU.add), rb + [B("imp")], [B("imp")])
            toff = 126 - 2 * i
            dve(lambda e, toff=toff: e.tensor_tensor(score[:, :], imp[:, :], tb[:, toff:toff + 128], ALU.add), [B("imp"), B("tb")], [B("score")])
            if i >= 1:
                dve(lambda e: e.tensor_scalar(score[:, 0:1], score[:, 0:1], 1000.0, None, ALU.add), [B("score")], [B("score")])
            dve(lambda e: e.max(m8[:, 0:8], score[:, :]), [B("score")], [B("m8a")])
            dve(lambda e: e.match_replace(score2[:, :], m8[:, 0:8], score[:, :], -1.0e30), [B("score"), B("m8a")], [B("score2")])
            dve(lambda e: e.max(m8[:, 8:16], score2[:, :]), [B("score2")], [B("m8b")])
            dve(lambda e: e.tensor_scalar(selb[:, :], score[:, :], m8[:, 15:16], NEGB, ALU.is_lt, ALU.mult), [B("score"), B("m8b")], [B("selb")])
            P.tr(pq[:, 512:640], selb[:, :], ident[:, :], [B("selb"), B("ident")], [PSB[7]])
            dve(lambda e, pq=pq: e.tensor_copy(selT4[:, :].rearrange("p (a b) -> p a b", a=4), bcast(pq[:, 512:640], 1, 4)), [PSB[7]], [B("selT4")])

            for half in range(2):
                for t in range(i + 1):
                    gg, tm = t // 32, t % 32
                    masks = [(ew[64 * gg:64 * gg + 64, tm * 128:(tm + 1) * 128], selT4[64 * gg:64 * gg + 64, :], [B("ew"), B("selT4")])]
                    if t == i:
                        masks.append((ident[:, :], triu4[:, :], [B("ident"), B("triu4")]))
                    si = srot[0] % 3
                    pt = PT[si][:, :]
                    ptb = B("PT%d" % si)
                    unit(KT[64 * gg:64 * gg + 64, 1, t * 128:(t + 1) * 128], 64 * gg, 128, half, masks, pt, ptb, [B("KT_%d" % t)])
                    pv_acc(pt, ptb, 128, Vs[:, t, :], [B("Vs_%d" % t), B("Vs_ones")], half, t == 0, t == i)
            combine(1, False)
            t0 = max(0, i - 4)
            for half in range(2):
                for t in range(t0, i + 1):
                    masks = []
                    if t == i:
                        masks.append((ident[:, :], triu4[:, :], [B("ident"), B("triu4")]))
                    elif t == i - 4:
                        masks.append((ident[:, :], tril4[:, :], [B("ident"), B("tril4")]))
                    si = srot[0] % 3
                    pt = PT[si][:, :]
                    ptb = B("PT%d" % si)
                    low = 64 - 64 * (t // 32)
                    unit(KT[low:low + 64, 1, t * 128:(t + 1) * 128], low, 128, half, masks, pt, ptb, [B("KT_%d" % t)])
                    pv_acc(pt, ptb, 128, Vw[:, t, :], [B("Vw_%d" % t), B("Vw_ones")], half, t == t0, t == i)
            combine(2, False)

            dve(lambda e: e.tensor_tensor(og[:, :], oacc[:, :], szs[:, :], ALU.mult), [B("oacc0"), B("oacc1"), B("szs")], [B("og")])
            out_tail(i, layer)

    def out_tail(i, layer):
        sq = i % 2
        pq = psb[7]
        for j in range(4):
            P.tr(pq[:, j * 128:(j + 1) * 128], og[:, j * 128:(j + 1) * 128], ident[:, :], [B("og"), B("ident")], [PSB[7]])
        dve(lambda e: e.tensor_copy(ogT[:, :], pq[:, 0:512]), [PSB[7]], [B("ogT")])
        for hh in range(2):
            for c in range(4):
                P.mm(ps[5 + hh][:, :], ogT[:, c * 128:(c + 1) * 128], w_out[:, c, hh * 512:(hh + 1) * 512], c == 0, c == 3,
                     [B("ogT"), B("w_out")], [PSB[5 + hh]])
        ys = ysb[sq]
        P.act(ys[:, 0:512], ps[5][:, :], AF.Copy, [PSB[5]], [B("fbuf%d" % sq)])
        dve(lambda e: e.tensor_copy(ys[:, 512:1024], ps[6][:, :]), [PSB[6]], [B("fbuf%d" % sq)])
        P.dma("sp", ypart[layer][i // 8][(i % 8) * 128:(i % 8 + 1) * 128, :], ys[:, :], [B("fbuf%d" % sq)], [B("ypart_%d_%d" % (layer, i))], "yst%d" % sq)

    def ret_layer(li, layer):
        for c in range(KC):
            load_bf16(w_nsa[:, c, :], ret_win[li][c * 128:(c + 1) * 128, :], 128, 1536, [B("w_nsa")])
        for c in range(4):
            load_bf16(w_out[:, c, :], ret_wout[li][c * 128:(c + 1) * 128, :], 128, D, [B("w_out")])
        P.dma("sp", gng[:, :], bass.AP(ret_gn[li].tensor, 0, [[0, 128], [1, 512]]), [], [B("gng")], "gng")
        P.dma("sp", dmaskT[:, :], dmask_d[:, :], [], [B("dmaskT")], "rc0")
        P.dma("sp", qdec[:, :], qdec_d[:, :], [], [B("qdec")], "rc1")
        P.dma("sp", kcdec[:, :], kcdec_d[:, :], [], [B("kcdec")], "rc2")
        if SAMPLE:
            ret_sample(li, layer)
        nch = NQT
        for n in range(nch):
            sq = n % 2
            P.dma("sp", xq[sq][:, :, :], xT[:, :, n * 128:(n + 1) * 128].rearrange("c p t -> p c t"),
                  [B("xT_%d" % n)], [B("xq%d" % sq)], "xq%d" % sq)
            cs_t = xcs[sq]
            P.dma("sp", cs_t[:, 0, :], xcos_d[:, n * 128:(n + 1) * 128], [], [B("xcs%d" % sq)], "xcsa%d" % sq)
            P.dma("sp", cs_t[:, 1, :], xsin_d[:, n * 128:(n + 1) * 128], [], [B("xcs%d" % sq)], "xcsb%d" % sq)
            for col in range(4):
                for c in range(KC):
                    P.mm(ps[0][:, col * 128:(col + 1) * 128], w_nsa[:, c, col * 128:(col + 1) * 128], xq[sq][:, c, :], c == 0, c == KC - 1,
                         [B("xq%d" % sq), B("w_nsa")], [PSB[0]])
            for (col0, pi) in ((512, 1), (1024, 2)):
                for c in range(KC):
                    P.mm(ps[pi][:, :], xq[sq][:, c, :], w_nsa[:, c, col0:col0 + 512], c == 0, c == KC - 1,
                         [B("xq%d" % sq), B("w_nsa")], [PSB[pi]])
            P.act(vbf[:, :], ps[1][:, :], AF.Copy, [PSB[1]], [B("vbf")])
            P.act(szs[:, :], ps[2][:, :], AF.Silu, [PSB[2]], [B("szs")])
            pool(lambda e: e.tensor_tensor(qf[:, :], szs[:, :], gng[:, :], ALU.mult), [B("szs"), B("gng")], [B("qf")])
            pv4 = ps[0][:, :].rearrange("p (a b t) -> p a b t", a=2, b=2)
            E, O_ = pv4[:, :, 0, :], pv4[:, :, 1, :]
            cosb = bcast(cs_t[:, 0, :], 1, 2)
            sinb = bcast(cs_t[:, 1, :], 1, 2)
            rv = [rtmp[:, a, :].rearrange("p (a t) -> p a t", a=2) for a in range(4)]
            rb = [PSB[0], B("xcs%d" % sq)]
            dve(lambda e, E=E, cosb=cosb: e.tensor_tensor(rv[0], E, cosb, ALU.mult), rb, [B("rtmp0")])
            dve(lambda e, O_=O_, sinb=sinb: e.tensor_tensor(rv[1], O_, sinb, ALU.mult), rb, [B("rtmp1")])
            dve(lambda e, E=E, sinb=sinb: e.tensor_tensor(rv[2], E, sinb, ALU.mult), rb, [B("rtmp2")])
            dve(lambda e, O_=O_, cosb=cosb: e.tensor_tensor(rv[3], O_, cosb, ALU.mult), rb, [B("rtmp3")])
            qk4 = QKr[:, :, :].rearrange("p (a b) t -> p a b t", a=2)
            dve(lambda e: e.tensor_tensor(qk4[:, :, 0, :], rv[0], rv[1], ALU.subtract), [B("rtmp0"), B("rtmp1")], [B("QKr_e")])
            dve(lambda e: e.tensor_tensor(qk4[:, :, 1, :], rv[2], rv[3], ALU.add), [B("rtmp2"), B("rtmp3")], [B("QKr_o")])
            qkb = [B("QKr_e"), B("QKr_o")]
            pool(lambda e: e.tensor_tensor(qdT[:, :, :], QKr[:, 0:2, :], bcast(qdec[:, :], 1, 2), ALU.mult), qkb + [B("qdec")], [B("qdT")])
            pq = psb[7]
            for ch in range(2):
                P.tr(pq[:, ch * 128:(ch + 1) * 128], QKr[:, 2 + ch, :], ident[:, :], qkb + [B("ident")], [PSB[7]])
            dve(lambda e: e.tensor_scalar(kdsb[:, :], pq[:, 0:256], kcdec[:, 0:1], None, ALU.mult), [PSB[7], B("kcdec")], [B("kdsb")])
            for ch in range(2):
                P.mm(ps[3][:, 0:128], QKr[:, 2 + ch, :], QKr[:, ch, :], ch == 0, ch == 1, qkb, [PSB[3]])
            dve(lambda e: e.tensor_tensor(ATs[:, :], ps[3][:, 0:128], dmaskT[:, :], ALU.mult), [PSB[3], B("dmaskT")], [B("ATs")])
            P.mm(ps[4][:, :], ATs[:, :], vbf[:, :], True, n == 0, [B("ATs"), B("vbf")], [PSB[4]])
            if n > 0:
                for ch in range(2):
                    P.mm(ps[4][:, :], qdT[:, ch, :], Sbf[:, ch, :], False, ch == 1, [B("qdT"), B("Sbf")], [PSB[4]])
            for ch in range(2):
                P.mm(ps[5 + ch][:, :], kdsb[:, ch * 128:(ch + 1) * 128], vbf[:, :], True, True, [B("kdsb"), B("vbf")], [PSB[5 + ch]])
                if n == 0:
                    dve(lambda e, ch=ch: e.tensor_copy(Sst[:, ch, :], ps[5 + ch][:, :]), [PSB[5 + ch]], [B("Sst%d" % ch)])
                else:
                    dve(lambda e, ch=ch: e.scalar_tensor_tensor(Sst[:, ch, :], Sst[:, ch, :], kcdec[:, 1:2], ps[5 + ch][:, :], ALU.mult, ALU.add),
                        [PSB[5 + ch], B("Sst%d" % ch), B("kcdec")], [B("Sst%d" % ch)])
                pool(lambda e, ch=ch: e.tensor_copy(Sbf[:, ch, :], Sst[:, ch, :]), [B("Sst%d" % ch)], [B("Sbf")])
            st = lnst[sq]
            kst_ = B("lnst%d" % sq)
            dve(lambda e, st=st: e.reduce_sum(st[:, 0:1], ps[4][:, :], AX.X), [PSB[4]], [kst_])
            dve(lambda e, st=st: e.tensor_scalar(st[:, 1:2], st[:, 0:1], -1.0 / 512, None, ALU.mult), [kst_], [kst_])
            P.act(oacc[:, :], ps[4][:, :], AF.Identity, [PSB[4], kst_], [B("oacc0"), B("oacc1")], bias=st[:, 1:2])
            P.act(otmp[:, :], oacc[:, :], AF.Square, [B("oacc0"), B("oacc1")], [B("otmp0"), B("otmp1")])
            dve(lambda e, st=st: e.reduce_sum(st[:, 2:3], otmp[:, :], AX.X), [B("otmp0"), B("otmp1")], [B("lnsq%d" % sq)])
            dve(lambda e, st=st: e.tensor_scalar(st[:, 3:4], st[:, 2:3], 1.0 / 512, LN_EPS, ALU.mult, ALU.add), [B("lnsq%d" % sq)], [B("lnr%d" % sq)])
            P.act(st[:, 5:6], st[:, 3:4], AF.Sqrt, [B("lnr%d" % sq)], [B("lnr%d" % sq)])
            dve(lambda e, st=st: e.reciprocal(st[:, 4:5], st[:, 5:6]), [B("lnr%d" % sq)], [B("lnr%d" % sq)])
            dve(lambda e, st=st: e.scalar_tensor_tensor(og[:, :], oacc[:, :], st[:, 4:5], qf[:, :], ALU.mult, ALU.mult),
                [B("oacc0"), B("oacc1"), B("lnr%d" % sq), B("qf")], [B("og")])
            out_tail(n, layer)
        for ch in range(2):
            P.dma("sp", retout[li][ch * 128:(ch + 1) * 128, :], Sst[:, ch, :], [B("Sst%d" % ch)], [B("retout_%d_%d" % (li, ch))], "reto%d" % ch)

    def rope_apply(x1, x2, nrep, t, rt, rtname, rbufs, wbufs, cs_ap=None, sn_ap=None, npart=128):
        cs = bcast(rope[:, t, 0:8] if cs_ap is None else cs_ap, 1, nrep)
        sn = bcast(rope[:, t, 8:16] if sn_ap is None else sn_ap, 1, nrep)
        n8 = nrep * 8
        v = [rt[0:npart, a, 0:n8].rearrange("p (a b) -> p a b", a=nrep) for a in range(4)]
        rb = list(rbufs) + ([B("rope")] if cs_ap is None else [])
        keys = [B("%s_%d" % (rtname, a)) for a in range(4)]
        dve(lambda e: e.tensor_tensor(v[0], x1, cs, ALU.mult), rb, [keys[0]])
        dve(lambda e: e.tensor_tensor(v[1], x2, sn, ALU.mult), rb, [keys[1]])
        dve(lambda e: e.tensor_tensor(v[2], x1, sn, ALU.mult), rb, [keys[2]])
        dve(lambda e: e.tensor_tensor(v[3], x2, cs, ALU.mult), rb, [keys[3]])
        dve(lambda e: e.tensor_tensor(x1, v[0], v[1], ALU.subtract), [keys[0], keys[1], keys[2]], wbufs)
        dve(lambda e: e.tensor_tensor(x2, v[2], v[3], ALU.add), [keys[2], keys[3]], wbufs)

    def ln_pass(layer, last):
        ntile = NQT if NQT < NT else NT
        for c in range((ntile + 7) // 8):
            P.add("pool", lambda e, c=c: e.collective_compute("AllReduce", ALU.add, replica_groups=[[0, 1, 2, 3], [4, 5, 6, 7]],
                                                              ins=[ypart[layer][c].ap().opt()], outs=[ysum[layer][c].ap().opt()]),
                  [B("ypart_%d_%d" % (layer, i)) for i in range(8 * c, min(8 * c + 8, ntile))], [B("ysum_%d_%d" % (layer, c))],
                  kind="cc", semkey="cc%d_%d" % (layer, c))
        P.dma("sp", lng[:, 0:D], bass.AP(lng_d.tensor, layer * D, [[0, 128], [1, D]]), [], [B("stg0")], "stg0")
        P.dma("sp", lnb[:, 0:D], bass.AP(lnb_d.tensor, layer * D, [[0, 128], [1, D]]), [], [B("stg1")], "stg1")
        for t in range(ntile):
            s = t % 2
            src = xp if layer == 0 else xcur
            X, Y, st = lnx[s], lny[s], lnst[s]
            kx, ky, kst_ = B("fbuf%d" % s), B("fbuf%d" % (2 + s)), B("lnst%d" % s)
            P.dma("sp", X[:, :], src[t * 128:(t + 1) * 128, :], [B("xcur_%d" % t)], [kx], "fbufld%d" % s)
            P.dma("sp", Y[:, :], ysum[layer][t // 8][(t % 8) * 128:(t % 8 + 1) * 128, :], [B("ysum_%d_%d" % (layer, t // 8))], [ky], "fbufld%d" % (2 + s))
            dve(lambda e, X=X, Y=Y: e.scalar_tensor_tensor(X[:, :], X[:, :], ALPHA, Y[:, :], ALU.mult, ALU.add), [kx, ky], [kx])
            dve(lambda e, X=X, st=st: e.reduce_sum(st[:, 0:1], X[:, :], AX.X), [kx], [kst_])
            dve(lambda e, st=st: e.tensor_scalar(st[:, 1:2], st[:, 0:1], -1.0 / D, None, ALU.mult), [kst_], [kst_])
            P.act(Y[:, :], X[:, :], AF.Identity, [kx, kst_], [ky], bias=st[:, 1:2])
            P.act(X[:, :], Y[:, :], AF.Square, [ky], [kx])
            dve(lambda e, X=X, st=st: e.reduce_sum(st[:, 2:3], X[:, :], AX.X), [kx], [B("lnsq%d" % s)])
            dve(lambda e, st=st: e.tensor_scalar(st[:, 3:4], st[:, 2:3], 1.0 / D, LN_EPS, ALU.mult, ALU.add), [B("lnsq%d" % s)], [B("lnr%d" % s)])
            P.act(st[:, 5:6], st[:, 3:4], AF.Sqrt, [B("lnr%d" % s)], [B("lnr%d" % s)])
            dve(lambda e, st=st: e.reciprocal(st[:, 4:5], st[:, 5:6]), [B("lnr%d" % s)], [B("lnr%d" % s)])
            dve(lambda e, X=X, Y=Y, st=st: e.scalar_tensor_tensor(X[:, :], Y[:, :], st[:, 4:5], lng[:, 0:D], ALU.mult, ALU.mult),
                [ky, B("lnr%d" % s), B("stg0"), B("lnsq%d" % s)], [kx])
            pool(lambda e, X=X: e.tensor_tensor(X[:, :], X[:, :], lnb[:, 0:D], ALU.add), [kx, B("stg1")], [kx])
            dst = yout if last else xcur
            P.dma("sp", dst[t * 128:(t + 1) * 128, :], X[:, :], [kx], [B("xcur_%d" % t)], "lnst%d" % s)
            if not last:
                to_xT(t, X[:, :], [kx], s)

    KTkeys = [B("KT_%d" % t) for t in range(NT)]
    Vskeys = [B("Vs_%d" % t) for t in range(NT)]
    Vwkeys = [B("Vw_%d" % t) for t in range(NT)]
    PTckeys = [B("PTc_%d_%d" % (ci, hf)) for ci in range(4) for hf in range(2)]
    GA = KT[:, :, :].rearrange("p a t -> p (a t)").bitcast(F32)
    XTlo = Vs[:, :, :].rearrange("p a b -> p (a b)")[:, 0:4096].rearrange("p (a b) -> p a b", a=32)
    XThi = Vw[:, :, :].rearrange("p a b -> p (a b)")[:, 0:4096].rearrange("p (a b) -> p a b", a=32)
    W1s = PTc[:, :, :].rearrange("p a b -> p (a b)").rearrange("p (k c h) -> p k c h", k=2, c=16)
    FB = [B("fbuf%d" % i) for i in range(4)]

    def load_xsT(layer):
        src = xs_d if layer == 0 else xs_cur
        P.dma("sp", fbuf[0][0:64, :], src[:, :], [B("xs_cur")], [FB[0]], "fbufld0")
        P.act(xbf[0][0:64, :], fbuf[0][0:64, :], AF.Copy, [FB[0]], [B("xbf0")])
        pb = psb[7]
        for c in range(KC):
            P.tr(pb[:, c * 64:(c + 1) * 64], xbf[0][0:64, c * 128:(c + 1) * 128], ident[0:64, 0:64], [B("xbf0"), B("ident")], [PSB[7]])
        dve(lambda e: e.tensor_copy(xsT[:, :], pb[:, 0:512]), [PSB[7]], [B("xsT")])

    def s_proj(col0, ncol, pi):
        for c in range(KC):
            P.mm(ps[pi][0:64, 0:ncol], xsT[:, c * 64:(c + 1) * 64], w_nsa[:, c, col0:col0 + ncol], c == 0, c == KC - 1,
                 [B("xsT"), B("w_nsa")], [PSB[pi]])

    def s_out_tail(layer):
        pq = psb[7]
        for j in range(4):
            P.tr(pq[:, j * 64:(j + 1) * 64], og[0:64, j * 128:(j + 1) * 128], ident[0:64, 0:64], [B("og"), B("ident")], [PSB[7]])
        dve(lambda e: e.tensor_copy(ogT[:, 0:256], pq[:, 0:256]), [PSB[7]], [B("ogT")])
        for hh in range(2):
            for c in range(4):
                P.mm(ps[5 + hh][0:64, :], ogT[:, c * 64:(c + 1) * 64], w_out[:, c, hh * 512:(hh + 1) * 512], c == 0, c == 3,
                     [B("ogT"), B("w_out")], [PSB[5 + hh]])
        ys = fbuf[0]
        P.act(ys[0:64, 0:512], ps[5][0:64, :], AF.Copy, [PSB[5]], [FB[0]])
        dve(lambda e: e.tensor_copy(ys[0:64, 512:1024], ps[6][0:64, :]), [PSB[6]], [FB[0]])
        P.dma("sp", ypart_s[layer][:, :], ys[0:64, :], [FB[0]], [B("ypart_s%d" % layer)], "yst0")

    def s_groupnorm_gate(o_ap, o_bufs, gate_ap, gate_bufs):
        st = lnst[0]
        kst_ = B("lnst0")
        dve(lambda e: e.reduce_sum(st[0:64, 0:1], o_ap, AX.X), o_bufs, [kst_])
        dve(lambda e: e.tensor_scalar(st[0:64, 1:2], st[0:64, 0:1], -1.0 / 512, None, ALU.mult), [kst_], [kst_])
        P.act(otmp[0:64, :], o_ap, AF.Identity, o_bufs + [kst_], [B("otmp0"), B("otmp1")], bias=st[0:64, 1:2])
        P.act(fbuf[3][0:64, 0:512], otmp[0:64, :], AF.Square, [B("otmp0"), B("otmp1")], [FB[3]])
        dve(lambda e: e.reduce_sum(st[0:64, 2:3], fbuf[3][0:64, 0:512], AX.X), [FB[3]], [B("lnsq0")])
        dve(lambda e: e.tensor_scalar(st[0:64, 3:4], st[0:64, 2:3], 1.0 / 512, LN_EPS, ALU.mult, ALU.add), [B("lnsq0")], [B("lnr0")])
        P.act(st[0:64, 5:6], st[0:64, 3:4], AF.Sqrt, [B("lnr0")], [B("lnr0")])
        dve(lambda e: e.reciprocal(st[0:64, 4:5], st[0:64, 5:6]), [B("lnr0")], [B("lnr0")])
        dve(lambda e: e.scalar_tensor_tensor(og[0:64, :], otmp[0:64, :], st[0:64, 4:5], gate_ap, ALU.mult, ALU.mult),
            [B("otmp0"), B("otmp1"), B("lnr0")] + gate_bufs, [B("og")])

    def gelu_to(ps_ap, n, bias_col, out_ap, psbuf, outbuf):
        g0, g1, g2 = fbuf[3][:, 0:n], fbuf[3][:, 512:512 + n], otmp[:, 0:n]
        k0, k1, k2 = FB[3], FB[3], B("otmp0")
        P.act(g0, ps_ap, AF.Identity, [psbuf, B("hb")], [k0], bias=bias_col)
        dve(lambda e: e.tensor_tensor(g1, g0, g0, ALU.mult), [k0], [k1])
        dve(lambda e: e.tensor_scalar(g1, g1, 0.044715, 1.0, ALU.mult, ALU.add), [k1], [k1])
        dve(lambda e: e.tensor_tensor(g1, g1, g0, ALU.mult), [k1, k0], [k1])
        P.act(g2, g1, AF.Sigmoid, [k1], [k2, B("otmp1")], scale=1.5957691216)
        dve(lambda e: e.tensor_tensor(out_ap, g2, g0, ALU.mult), [k2, k0], [outbuf])

    sc3 = fbuf[2][:, :].rearrange("p (h q) -> p h q", h=8)
    pw3 = qbf[:, :].rearrange("p (h q) -> p h q", h=8)
    KSC = B("fbuf2")

    def l_scores(X3, npos, qL3, xbufs, qbufs):
        ch = min(npos, 16)
        for h in range(8):
            for c0 in range(0, npos, ch):
                tmp = otmp[:, :].rearrange("p (a d) -> p a d", d=64)[:, 0:ch, :] if ch <= 8 else fbuf[3][:, :].rearrange("p (a d) -> p a d", d=64)[:, 0:ch, :]
                kt = B("otmp0") if ch <= 8 else FB[3]
                dve(lambda e, tmp=tmp, h=h, c0=c0: e.tensor_tensor(tmp, X3[:, c0:c0 + ch, :], bcast(qL3[:, h, :], 1, ch), ALU.mult), xbufs + qbufs, [kt])
                dve(lambda e, tmp=tmp, h=h, c0=c0: e.reduce_sum(sc3[:, h, c0:c0 + ch], tmp, AX.X), [kt], [KSC])

    def l_norm(npos, gcol0, LA, kLA, self_k_col):
        dve(lambda e: e.reduce_sum(zp[:, :], sc3[:, :, 0:npos], AX.X), [KSC], [B("zp")])
        P.mm(ps[7][:, 0:8], ddsb[:, :], zp[:, :], True, True, [B("ddsb"), B("zp")], [PSB[7]])
        if self_k_col is not None:
            qL3 = LA[:, 0:512].rearrange("p (h d) -> p h d", h=8)
            tmp = otmp[:, :].rearrange("p (h d) -> p h d", h=8)
            dve(lambda e: e.tensor_tensor(tmp, qL3, bcast(LA[:, self_k_col:self_k_col + 64], 1, 8), ALU.mult), [kLA], [B("otmp0"), B("otmp1")])
            dve(lambda e: e.reduce_sum(ssm[:, :], tmp, AX.X), [B("otmp0"), B("otmp1")], [B("ssm")])
            P.act(pself[:, :], ssm[:, :], AF.Exp, [B("ssm")], [B("pself")], scale=0.125)
            dve(lambda e: e.tensor_tensor(rZs[:, :], ps[7][:, 0:8], pself[:, :], ALU.add), [PSB[7], B("pself")], [B("rZs")])
        else:
            dve(lambda e: e.tensor_scalar(rZs[:, :], ps[7][:, 0:8], 1.0e-30, None, ALU.max), [PSB[7]], [B("rZs")])
        dve(lambda e: e.reciprocal(rZs[:, :], rZs[:, :]), [B("rZs")], [B("rZs")])
        dve(lambda e: e.tensor_tensor(Ws[:, :], rZs[:, :], LA[:, 512 + gcol0:512 + gcol0 + 8], ALU.mult), [B("rZs"), kLA], [B("Ws")])

    def l_pv(npos, v_of_pos, vbufs, first_open, last_close):
        if first_open:
            P.mm(ps[5][0:64, 0:64], zeros[0:32, 0:64], zeros[0:32, 0:64], True, False, [B("zeros")], [PSB[5]])
        nch = (npos + 7) // 8
        for pc in range(nch):
            slot = pc % 2
            Pe = QT[:, slot * 512:(slot + 1) * 512].rearrange("p (q s h) -> p q s h", q=8, s=8)
            kq = B("QTs%d" % slot)
            for s_ in range(8):
                dve(lambda e, Pe=Pe, s_=s_, pc=pc: e.tensor_scalar(Pe[:, :, s_, :], pw3[:, :, pc * 8:(pc + 1) * 8].rearrange("p h q -> p q h"), d8[:, s_:s_ + 1], None, ALU.mult),
                    [B("qbf"), B("d8")], [kq, B("QT")])
            for r in range(8):
                pos = pc * 8 + r
                P.mm(ps[5][0:64, 0:64], QT[:, slot * 512 + r * 64:slot * 512 + (r + 1) * 64], v_of_pos(pos), False,
                     last_close and pc == nch - 1 and r == 7, [kq] + vbufs, [PSB[5]])

    def nsa_sample(li, layer):
        PTc2 = PTc[:, :, :].rearrange("p a b -> p (a b)")
        for c in range(4):
            load_bf16(PTc2[:, 1024 * c:1024 * (c + 1)], nsa_w1s[li][:, 1024 * c:1024 * (c + 1)], 128, 1024, PTckeys)
        load_xsT(layer)
        s_proj(0, 512, 0)
        s_proj(512, 384, 1)
        s_proj(896, 24, 2)
        s_proj(920, 512, 3)
        SQ = fbuf[0]
        P.act(SQ[0:64, 0:512], ps[0][0:64, :], AF.Copy, [PSB[0]], [FB[0]])
        P.act(SQ[0:64, 512:536], ps[2][0:64, 0:24], AF.Sigmoid, [PSB[2]], [FB[0]])
        P.act(SQ[0:64, 536:920], ps[1][0:64, 0:384], AF.Copy, [PSB[1]], [FB[0]])
        P.act(szs[0:64, :], ps[3][0:64, :], AF.Silu, [PSB[3]], [B("szs")])
        P.dma("sp", rp2048[0:64, :], bass.AP(rope2048_d.tensor, 0, [[0, 64], [1, 16]]), [], [B("rp2048")], "rp2048")
        q3 = SQ[0:64, 0:512].rearrange("p (a b) -> p a b", a=8)
        rope_apply(q3[:, :, 0:8], q3[:, :, 8:16], 8, 0, ropetmp[0], "rt0", [FB[0], B("rp2048")], [FB[0]],
                   cs_ap=rp2048[0:64, 0:8], sn_ap=rp2048[0:64, 8:16], npart=64)
        k3 = SQ[0:64, 536:920].rearrange("p (a b) -> p a b", a=3)
        rope_apply(k3[:, :, 0:8], k3[:, :, 8:16], 3, 0, ropetmp[1], "rt1", [FB[0], B("rp2048")], [FB[0]],
                   cs_ap=rp2048[0:64, 0:8], sn_ap=rp2048[0:64, 8:16], npart=64)
        P.dma("sp", kvs_out[li][:, :], SQ[0:64, 536:920], [FB[0]], [B("kvs_out%d" % li)], "kvo0")
        for j in range(2):
            src = bass.AP(cwin[li][j].tensor, 64, [[32768, 64], [4672, 7], [1, 4672]])
            dst = bass.AP(wins_out[li][j].tensor, 0, [[32768, 64], [4672, 7], [1, 4672]])
            P.dma("sp", dst, src, [], [B("wins_%d_%d" % (li, j))], "wino%d" % j)
            P.dma("sp", wins_out[li][j][:, 511 * 64:512 * 64], SQ[0:64, 792 + 64 * j:856 + 64 * j], [FB[0]], [B("winsn_%d_%d" % (li, j))], "winn%d" % j)
        P.dma("sp", sq_d[:, :], SQ[0:64, 0:920], [FB[0]], [B("sq_d")], "sqd")

        def load_LA(g8, sl):
            LA = fbuf[sl]
            for s_ in range(8):
                P.dma("sp", LA[16 * s_:16 * s_ + 16, 0:920], bass.AP(sq_d.tensor, (8 * g8 + s_) * 920, [[0, 16], [1, 920]]),
                      [B("sq_d")], [FB[sl]], "fbufld%d" % sl)
            return LA, FB[sl]

        def gather(pool_ap, g8):
            P.add("pool", lambda e: e.indirect_dma_start(out=GA[:, :], out_offset=None, in_=pool_ap,
                                                         in_offset=bass.IndirectOffsetOnAxis(ap=ptL[:, g8:g8 + 1], axis=0)),
                  [B("ptL")], KTkeys, kind="d", semkey="gath")

        for g8 in range(8):
            LA, kLA = load_LA(g8, g8 % 2)
            qL3 = LA[:, 0:512].rearrange("p (h d) -> p h d", h=8)
            for kv_i in range(2):
                gather(pools[li][kv_i][:, :], g8)
                for q4 in range(16):
                    pb = ps[q4 % 2]
                    for r in range(4):
                        p_ = 4 * q4 + r
                        P.tr(pb[:, r * 128:(r + 1) * 128], GA[:, p_ * 128:(p_ + 1) * 128], ident_f[:, :], KTkeys + [B("ident_f")], [PSB[q4 % 2]])
                    dst = (XTlo if q4 < 8 else XThi)[:, (4 * q4) % 32:(4 * q4) % 32 + 4, :]
                    P.act(dst, pb[:, :].rearrange("p (a b) -> p a b", a=4), AF.Copy, [PSB[q4 % 2]], Vskeys if q4 < 8 else Vwkeys)
                xb = Vskeys + Vwkeys + PTckeys
                for ip in range(8):
                    n_ = 127 if ip == 7 else 128
                    for c in range(16):
                        p_ = 8 * ip + c
                        if p_ < 64:
                            rhs = (XTlo if p_ < 32 else XThi)[:, p_ % 32, 0:n_]
                        else:
                            rhs = XTlo[:, p_ - 64, 1:128]
                        P.mm(ps[2 + ip // 4][:, (ip % 4) * 128:(ip % 4) * 128 + n_], W1s[:, kv_i, c, :], rhs, c == 0, c == 15, xb, [PSB[2 + ip // 4]])
                gelu_to(ps[2][:, 0:512], 512, hb[:, kv_i:kv_i + 1], hsb[0][:, 0:512], PSB[2], B("hsb0"))
                gelu_to(ps[3][:, 0:511], 511, hb[:, kv_i:kv_i + 1], hsb[1][:, 0:511], PSB[3], B("hsb1"))
                for ip in range(8):
                    P.mm(ps[4][:, ip * 64:(ip + 1) * 64], hsb[ip // 4][:, (ip % 4) * 128:(ip % 4 + 1) * 128],
                         w2sb[:, 0:64] if kv_i == 0 else w2sb[:, 128:192], True, True, [B("hsb%d" % (ip // 4)), B("w2sb")], [PSB[4]])
                if kv_i == 0:
                    P.act(qf[:, :], ps[4][:, :], AF.Copy, [PSB[4]], [B("qf")])
                else:
                    P.act(vbf[:, :], ps[4][:, :], AF.Copy, [PSB[4]], [B("vbf")])
            l_scores(qf[:, :].rearrange("p (a d) -> p a d", d=64), 8, qL3, [B("qf")], [kLA])
            dve(lambda e: e.tensor_tensor(sc3[:, :, 0:8], sc3[:, :, 0:8], bcast(cmaskL[:, :], 1, 8), ALU.add), [KSC, B("cmaskL")], [KSC])
            P.act(sc3[:, :, 0:8], sc3[:, :, 0:8], AF.Exp, [KSC], [KSC], scale=0.125)
            l_norm(8, 0, LA, kLA, None)
            pn3 = pnL[:, :].rearrange("p (h q) -> p h q", h=8)
            dve(lambda e: e.tensor_tensor(pn3, sc3[:, :, 0:8], bcast(rZs[:, :], 2, 8), ALU.mult), [KSC, B("rZs")], [B("pnL")])
            dve(lambda e: e.reduce_sum(pcs[:, :], pnL[:, :].rearrange("p (h q) -> p q h", h=8), AX.X), [B("pnL")], [B("pcs")])
            PCe = PT[0][:, :].rearrange("p (q s) -> p q s", q=8)
            pool(lambda e: e.memset(PT[0][:, :], 0.0), [], [B("PT0")])
            for ip in range(8):
                dve(lambda e, ip=ip, g8=g8: e.tensor_scalar(PCe[:, ip, 8 * g8:8 * g8 + 8], d8[:, :], pcs[:, ip:ip + 1], None, ALU.mult), [B("pcs"), B("d8")], [B("PT0")])
            for ip in range(8):
                P.mm(ps[6][0:64, 0:33], PCe[:, ip, :], ovt[:, ip, :], g8 == 0 and ip == 0, g8 == 7 and ip == 7, [B("PT0"), B("ovt")], [PSB[6]])
            dve(lambda e: e.tensor_tensor(pw3[:, :, 0:8], sc3[:, :, 0:8], bcast(Ws[:, :], 2, 8), ALU.mult), [KSC, B("Ws")], [B("qbf")])
            l_pv(8, lambda pos: vbf[:, pos * 64:(pos + 1) * 64], [B("vbf")], True, True)
            dve(lambda e, g8=g8: e.tensor_copy(oacc[0:64, g8 * 64:(g8 + 1) * 64], ps[5][0:64, 0:64]), [PSB[5]], [B("oacc0"), B("oacc1")])

        dve(lambda e: e.tensor_tensor(score[0:64, 0:33], ps[6][0:64, 0:33], tbs[0:64, :], ALU.add), [PSB[6], B("tbs")], [B("score")])
        dve(lambda e: e.max(m8[0:64, 0:8], score[0:64, 0:33]), [B("score")], [B("m8a")])
        dve(lambda e: e.match_replace(score2[0:64, 0:33], m8[0:64, 0:8], score[0:64, 0:33], -1.0e30), [B("score"), B("m8a")], [B("score2")])
        dve(lambda e: e.max(m8[0:64, 8:16], score2[0:64, 0:33]), [B("score2")], [B("m8b")])
        dve(lambda e: e.tensor_scalar(imp[0:64, 0:32], score[0:64, 0:32], m8[0:64, 15:16], NEGB, ALU.is_lt, ALU.mult), [B("score"), B("m8b")], [B("imp")])
        P.dma("sp", selb_d[:, :], imp[0:64, 0:32], [B("imp")], [B("selb_d")], "selbd")

        for g8 in range(8):
            LA, kLA = load_LA(g8, g8 % 2)
            qL3 = LA[:, 0:512].rearrange("p (h d) -> p h d", h=8)
            P.dma("sp", selbL[:, :], bass.AP(selb_d.tensor, 256 * g8, [[2, 128], [1, 2]]), [B("selb_d")], [B("selbL")], "selbl")
            gather(pools[li][2][:, :], g8)
            l_scores(GA[:, :].rearrange("p (a d) -> p a d", d=64), 128, qL3, KTkeys, [kLA])
            dve(lambda e: e.tensor_scalar(sc3[:, :, 0:64], sc3[:, :, 0:64], selbL[:, 0:1], None, ALU.add), [KSC, B("selbL")], [KSC])
            dve(lambda e: e.tensor_scalar(sc3[:, :, 64:128], sc3[:, :, 64:128], selbL[:, 1:2], None, ALU.add), [KSC, B("selbL")], [KSC])
            P.act(sc3[:, :, :], sc3[:, :, :], AF.Exp, [KSC], [KSC], scale=0.125)
            l_norm(128, 8, LA, kLA, 664)
            dve(lambda e: e.tensor_tensor(pw3[:, :, :], sc3[:, :, :], bcast(Ws[:, :], 2, 128), ALU.mult), [KSC, B("Ws")], [B("qbf")])
            gather(pools[li][3][:, :], g8)
            P.act(XTlo[:, :, :].rearrange("p a b -> p (a b)"), GA[:, 0:4096], AF.Copy, KTkeys, Vskeys)
            P.act(XThi[:, :, :].rearrange("p a b -> p (a b)"), GA[:, 4096:8192], AF.Copy, KTkeys, Vwkeys)
            XL2 = XTlo[:, :, :].rearrange("p a b -> p (a b)")
            XH2 = XThi[:, :, :].rearrange("p a b -> p (a b)")
            l_pv(128, lambda pos: (XL2 if pos < 64 else XH2)[:, (pos % 64) * 64:(pos % 64 + 1) * 64], Vskeys + Vwkeys, True, False)
            self_pv(LA, kLA, 728)
            P.dma("sp", GA[:, 0:2048], bass.AP(cwin[li][0].tensor, 8 * g8 * 32768, [[2048, 128], [1, 2048]]), [], KTkeys, "gathw")
            l_scores(GA[:, 0:2048].rearrange("p (a d) -> p a d", d=64), 32, qL3, KTkeys, [kLA])
            P.act(sc3[:, :, 0:32], sc3[:, :, 0:32], AF.Exp, [KSC], [KSC], scale=0.125)
            l_norm(32, 16, LA, kLA, 792)
            dve(lambda e: e.tensor_tensor(pw3[:, :, 0:32], sc3[:, :, 0:32], bcast(Ws[:, :], 2, 32), ALU.mult), [KSC, B("Ws")], [B("qbf")])
            P.dma("sp", GA[:, 0:2048], bass.AP(cwin[li][1].tensor, 8 * g8 * 32768, [[2048, 128], [1, 2048]]), [], KTkeys, "gathw")
            P.act(XL2[:, 0:2048], GA[:, 0:2048], AF.Copy, KTkeys, Vskeys)
            l_pv(32, lambda pos: XL2[:, pos * 64:(pos + 1) * 64], Vskeys, False, False)
            self_pv(LA, kLA, 856, close=True)
            osb = ssm64
            dve(lambda e, g8=g8: e.tensor_tensor(osb[0:64, :], ps[5][0:64, 0:64], oacc[0:64, g8 * 64:(g8 + 1) * 64], ALU.add), [PSB[5], B("oacc0"), B("oacc1")], [B("ssm64")])
            P.dma("sp", so_d[64 * g8:64 * g8 + 64, :], osb[0:64, :], [B("ssm64")], [B("so_d_%d" % g8)], "sod")
        P.dma("sp", fbuf[1][0:64, 0:512], bass.AP(so_d.tensor, 0, [[512, 64], [1, 512]]), [B("so_d_%d" % g) for g in range(8)], [FB[1]], "fbufld1")
        dve(lambda e: e.tensor_tensor(og[0:64, :], fbuf[1][0:64, 0:512], szs[0:64, :], ALU.mult), [FB[1], B("szs")], [B("og")])
        s_out_tail(layer)

    def self_pv(LA, kLA, vcol, close=False):
        dve(lambda e: e.tensor_tensor(wself[:, :], pself[:, :], Ws[:, :], ALU.mult), [B("pself"), B("Ws")], [B("wself")])
        for s_ in range(8):
            dve(lambda e, s_=s_: e.tensor_scalar(Wse[:, s_ * 8:(s_ + 1) * 8], wself[:, :], d0[:, s_:s_ + 1], None, ALU.mult), [B("wself"), B("d0")], [B("Wse")])
        P.act(vsn[:, :], LA[:, vcol:vcol + 64], AF.Copy, [kLA], [B("vsn")])
        P.mm(ps[5][0:64, 0:64], Wse[:, :], vsn[:, :], False, close, [B("Wse"), B("vsn")], [PSB[5]])


    def ret_sample(li, layer):
        load_xsT(layer)
        s_proj(0, 512, 0)
        s_proj(512, 512, 1)
        s_proj(1024, 512, 2)
        P.act(vbf[0:64, :], ps[1][0:64, :], AF.Copy, [PSB[1]], [B("vbf")])
        P.act(szs[0:64, :], ps[2][0:64, :], AF.Silu, [PSB[2]], [B("szs")])
        pool(lambda e: e.tensor_tensor(qf[0:64, :], szs[0:64, :], gng[0:64, :], ALU.mult), [B("szs"), B("gng")], [B("qf")])
        P.dma("sp", xcs[1][0:64, :, :].rearrange("p a b -> p (a b)"), bass.AP(xp2048_d.tensor, 0, [[0, 64], [1, 256]]), [], [B("xcs1")], "xcsa1")
        pv4 = ps[0][0:64, :].rearrange("p (a b t) -> p a b t", a=2, b=2)
        E, O_ = pv4[:, :, 0, :], pv4[:, :, 1, :]
        cosb = bcast(xcs[1][0:64, 0, :], 1, 2)
        sinb = bcast(xcs[1][0:64, 1, :], 1, 2)
        rv = [rtmp[0:64, a, :].rearrange("p (a t) -> p a t", a=2) for a in range(4)]
        rb = [PSB[0], B("xcs1")]
        dve(lambda e: e.tensor_tensor(rv[0], E, cosb, ALU.mult), rb, [B("rtmp0")])
        dve(lambda e: e.tensor_tensor(rv[1], O_, sinb, ALU.mult), rb, [B("rtmp1")])
        dve(lambda e: e.tensor_tensor(rv[2], E, sinb, ALU.mult), rb, [B("rtmp2")])
        dve(lambda e: e.tensor_tensor(rv[3], O_, cosb, ALU.mult), rb, [B("rtmp3")])
        qk = fbuf[1]
        qk4 = qk[0:64, 0:512].rearrange("p (a b t) -> p a b t", a=2, b=2)
        dve(lambda e: e.tensor_tensor(qk4[:, :, 0, :], rv[0], rv[1], ALU.subtract), [B("rtmp0"), B("rtmp1")], [FB[1]])
        dve(lambda e: e.tensor_tensor(qk4[:, :, 1, :], rv[2], rv[3], ALU.add), [B("rtmp2"), B("rtmp3"), FB[1]], [FB[1]])
        dve(lambda e: e.tensor_scalar(qk[0:64, 256:512], qk[0:64, 256:512], 0.0625, None, ALU.mult), [FB[1]], [FB[1]])
        pf = ps[0]
        for ch in range(2):
            P.tr(pf[:, ch * 64:(ch + 1) * 64], qk[0:64, ch * 128:(ch + 1) * 128], ident_f[0:64, 0:64], [FB[1], B("ident_f")], [PSB[0]])
        dve(lambda e: e.tensor_copy(QKr[:, 0:2, 0:64], pf[:, 0:128].rearrange("p (a t) -> p a t", a=2)), [PSB[0]], [B("QKr_e"), B("QKr_o")])
        for s_ in range(64):
            sl = s_ % 2
            Sb = fbuf[2 + sl]
            kS = B("fbuf%d" % (2 + sl))
            P.dma("sp", Sb[:, :].rearrange("p (c e) -> p c e", c=2), sret_d[li][s_, :, :].rearrange("(c p) e -> p c e", c=2), [], [kS], "fbufld%d" % (2 + sl))
            km = PT[1 + sl]
            kmk = B("PT%d" % (1 + sl))
            dve(lambda e, km=km, s_=s_: e.tensor_scalar(km[0:64, 0:256], qk[0:64, 256:512], ident_f[0:64, s_:s_ + 1], None, ALU.mult), [FB[1], B("ident_f")], [kmk])
            for ch in range(2):
                P.mm(ps[5 + ch][:, :], km[0:64, ch * 128:(ch + 1) * 128], vbf[0:64, :], True, True, [kmk, B("vbf")], [PSB[5 + ch]])
                dve(lambda e, Sb=Sb, ch=ch: e.scalar_tensor_tensor(Sb[:, ch * 512:(ch + 1) * 512], Sb[:, ch * 512:(ch + 1) * 512], kcdec[:, 2:3], ps[5 + ch][:, :], ALU.mult, ALU.add),
                    [PSB[5 + ch], kS, B("kcdec")], [kS])
            P.dma("sp", rets_out[li][s_, :, :].rearrange("(c p) e -> p c e", c=2), Sb[:, :].rearrange("p (c e) -> p c e", c=2), [kS], [B("rets_%d_%d" % (li, s_))], "reto%d" % sl)
            P.act(Sbf[:, :, :].rearrange("p c e -> p (c e)"), Sb[:, :], AF.Copy, [kS], [B("Sbf")])
            po = ps[3 + sl]
            for ch in range(2):
                P.mm(po[0:64, :], QKr[:, ch, 0:64], Sbf[:, ch, :], ch == 0, ch == 1, [B("QKr_e"), B("QKr_o"), B("Sbf")], [PSB[3 + sl]])
            if s_ == 0:
                dve(lambda e, po=po, s_=s_: e.tensor_scalar(oacc[0:64, :], po[0:64, :], ident_f[0:64, s_:s_ + 1], None, ALU.mult), [PSB[3 + sl], B("ident_f")], [B("oacc0"), B("oacc1")])
            else:
                dve(lambda e, po=po, s_=s_: e.scalar_tensor_tensor(oacc[0:64, :], po[0:64, :], ident_f[0:64, s_:s_ + 1], oacc[0:64, :], ALU.mult, ALU.add),
                    [PSB[3 + sl], B("ident_f"), B("oacc0"), B("oacc1")], [B("oacc0"), B("oacc1")])
        s_groupnorm_gate(oacc[0:64, :], [B("oacc0"), B("oacc1")], qf[0:64, :], [B("qf")])
        s_out_tail(layer)

    def ln_sample(layer, last):
        P.add("pool", lambda e: e.collective_compute("AllReduce", ALU.add, replica_groups=[[0, 1, 2, 3], [4, 5, 6, 7]],
                                                     ins=[ypart_s[layer].ap().opt()], outs=[ysum_s[layer].ap().opt()]),
              [B("ypart_s%d" % layer)], [B("ysum_s%d" % layer)], kind="cc", semkey="ccs%d" % layer)
        X, Y, st = fbuf[0], fbuf[2], lnst[0]
        kx, ky, kst_ = FB[0], FB[2], B("lnst0")
        src = xs_d if layer == 0 else xs_cur
        P.dma("sp", X[0:64, :], src[:, :], [B("xs_cur")], [kx], "fbufld0")
        P.dma("sp", Y[0:64, :], ysum_s[layer][:, :], [B("ysum_s%d" % layer)], [ky], "fbufld2")
        dve(lambda e: e.scalar_tensor_tensor(X[0:64, :], X[0:64, :], ALPHA, Y[0:64, :], ALU.mult, ALU.add), [kx, ky], [kx])
        dve(lambda e: e.reduce_sum(st[0:64, 0:1], X[0:64, :], AX.X), [kx], [kst_])
        dve(lambda e: e.tensor_scalar(st[0:64, 1:2], st[0:64, 0:1], -1.0 / D, None, ALU.mult), [kst_], [kst_])
        P.act(Y[0:64, :], X[0:64, :], AF.Identity, [kx, kst_], [ky], bias=st[0:64, 1:2])
        P.act(X[0:64, :], Y[0:64, :], AF.Square, [ky], [kx])
        dve(lambda e: e.reduce_sum(st[0:64, 2:3], X[0:64, :], AX.X), [kx], [B("lnsq0")])
        dve(lambda e: e.tensor_scalar(st[0:64, 3:4], st[0:64, 2:3], 1.0 / D, LN_EPS, ALU.mult, ALU.add), [B("lnsq0")], [B("lnr0")])
        P.act(st[0:64, 5:6], st[0:64, 3:4], AF.Sqrt, [B("lnr0")], [B("lnr0")])
        dve(lambda e: e.reciprocal(st[0:64, 4:5], st[0:64, 5:6]), [B("lnr0")], [B("lnr0")])
        dve(lambda e: e.scalar_tensor_tensor(X[0:64, :], Y[0:64, :], st[0:64, 4:5], lng[0:64, 0:D], ALU.mult, ALU.mult),
            [ky, B("lnr0"), B("stg0"), B("lnsq0")], [kx])
        pool(lambda e: e.tensor_tensor(X[0:64, :], X[0:64, :], lnb[0:64, 0:D], ALU.add), [kx, B("stg1")], [kx])
        dst = ys_out if last else xs_cur
        P.dma("sp", dst[:, :], X[0:64, :], [kx], [B("xs_cur")], "lnst0")


    for layer in range(NL):
        if layer % 2 == 0:
            nsa_layer(layer // 2, layer)
        else:
            ret_layer(layer // 2, layer)
        if os.environ.get("KNOLN"):
            continue
        ln_pass(layer, layer == NL - 1)
        if SAMPLE:
            ln_sample(layer, layer == NL - 1)

    P.emit(es)
    es.close()
    return nc, P


_CACHE = {}


def _prep_core(c, inp):
    b, k = c // 4, c % 4
    m = {}
    m["xp"] = np.ascontiguousarray(inp["x_prompt"][b])
    m["rope"] = _CACHE["rope"]
    for kk, v in _CACHE["consts"].items():
        m[kk] = v
    m["ln_g"] = np.ascontiguousarray(inp["ln_g"])
    m["ln_b"] = np.ascontiguousarray(inp["ln_b"])
    for l in range(2):
        w = inp["nsa_w_in"][l]
        cols = [w[:, 512 * k:512 * (k + 1)]]
        for j in range(6):
            o = 2048 + 256 * j + 64 * k
            cols.append(w[:, o:o + 64])
        for br in range(3):
            o = 2048 + 1536 + 32 * br + 8 * k
            cols.append(w[:, o:o + 8])
        o = 2048 + 1536 + 96 + 512 * k
        cols.append(w[:, o:o + 512])
        m["nsa_win%d" % l] = np.ascontiguousarray(np.concatenate(cols, axis=1))
        m["nsa_wout%d" % l] = np.ascontiguousarray(inp["nsa_w_out"][l][512 * k:512 * (k + 1), :])
        w1k = inp["nsa_w1_k"][l].reshape(32, 64, 128).transpose(1, 0, 2).reshape(64, 4096)
        w1v = inp["nsa_w1_v"][l].reshape(32, 64, 128).transpose(1, 0, 2).reshape(64, 4096)
        m["nsa_w1_%d" % l] = np.ascontiguousarray(np.concatenate([w1k, w1v], 0))
        m["nsa_pe%d" % l] = np.ascontiguousarray(np.concatenate([inp["nsa_pe_k"][l].T, inp["nsa_pe_v"][l].T], 0))
        m["nsa_w2_%d" % l] = np.ascontiguousarray(np.concatenate([inp["nsa_w2_k"][l], inp["nsa_w2_k"][l], inp["nsa_w2_v"][l]], 1))
        rw = inp["ret_w_in"][l]
        qc = rw[:, 256 * k:256 * (k + 1)]
        kc_ = rw[:, 1024 + 256 * k:1024 + 256 * (k + 1)]
        m["ret_win%d" % l] = np.ascontiguousarray(np.concatenate(
            [qc[:, 0::2], qc[:, 1::2], kc_[:, 0::2], kc_[:, 1::2],
             rw[:, 2048 + 512 * k:2048 + 512 * (k + 1)], rw[:, 4096 + 512 * k:4096 + 512 * (k + 1)]], 1))
        m["ret_wout%d" % l] = np.ascontiguousarray(inp["ret_w_out"][l][512 * k:512 * (k + 1), :])
        m["ret_gn%d" % l] = np.ascontiguousarray(inp["ret_gn_g"][l][512 * k:512 * (k + 1)][None, :])
    m["xcos"], m["xsin"] = _CACHE["xpos"]
    m["xp2048"] = np.ascontiguousarray(np.concatenate([m["xcos"][:, 2048], m["xsin"][:, 2048]])[None, :])
    g = c // 4
    m["xs"] = np.ascontiguousarray(inp["x_sample"][64 * g:64 * g + 64, 0, :])
    m["ptab"] = np.ascontiguousarray(inp["page_table"][64 * g:64 * g + 64].reshape(8, 128).astype(np.int32))
    for kk, v in _CACHE["stab"].items():
        m[kk] = v
    m["rope2048"] = np.ascontiguousarray(_CACHE["rope"][0, 16, :][None, :])
    for l in range(2):
        for j, nm in enumerate(("cache_k_cmp", "cache_v_cmp", "cache_k_sel", "cache_v_sel")):
            m["pool%d_%d" % (l, j)] = _pool_slice(inp, nm, l, k)
        for j, nm in enumerate(("cache_k_win", "cache_v_win")):
            m["cwin%d_%d" % (l, j)] = np.ascontiguousarray(inp[nm][l, 64 * g:64 * g + 64, :, k, :]).reshape(64, 32768)
        w1n = [inp[nm][l].reshape(16, 128, 128).transpose(1, 0, 2).reshape(128, 2048) for nm in ("nsa_w1_k", "nsa_w1_v")]
        m["nsa_w1s%d" % l] = np.ascontiguousarray(np.concatenate(w1n, 1))
    for l in range(2):
        st = inp["state_ret"][l, 64 * g:64 * g + 64, k]
        m["sret%d" % l] = np.ascontiguousarray(np.concatenate([st[:, 0::2, :], st[:, 1::2, :]], 1))
    for kk, v in _CACHE["dec"][k].items():
        m[kk] = v
    return m


def kernel(**inp):
    stage = os.environ.get("KSTAGE", "full")
    inp = {k: np.asarray(v) for k, v in inp.items()}
    if "rope" not in _CACHE:
        _CACHE["rope"] = _rope_table()
        _CACHE["consts"] = _consts_bf16like()
        _CACHE["xpos"] = _xpos_tables()
        _CACHE["dec"] = [_decay_tables(h) for h in range(4)]
        _CACHE["stab"] = _sample_tables()
    _CACHE["pool"] = {}
    nc, P = build(stage)
    in_maps = [_prep_core(c, inp) for c in range(8)]
    res = run_bass_kernel_spmd(nc, in_maps, core_ids=list(range(8))).results
    B_, DEC = 2, 128
    y_prompt = np.zeros((B_, T, D), np.float32)
    y_sample = np.zeros((DEC, 1, D), np.float32)
    kv_p = [np.zeros((2, B_, T, 4, 64), np.float32) for _ in range(4)]
    kv_s = [np.zeros((2, DEC, 1, 4, 64), np.float32) for _ in range(4)]
    win_p = [np.zeros((2, B_, 512, 4, 64), np.float32) for _ in range(2)]
    win_s = [np.zeros((2, DEC, 512, 4, 64), np.float32) for _ in range(2)]
    ret_p = np.zeros((2, B_, 4, 256, 512), np.float32)
    ret_s = np.zeros((2, DEC, 4, 256, 512), np.float32)
    for c in range(8):
        b, k = c // 4, c % 4
        if k == 0:
            y_prompt[b] = res[c]["yout"]
            y_sample[64 * (c // 4):64 * (c // 4) + 64, 0, :] = res[c]["ys_out"]
        for l in range(2):
            kvo = res[c]["kvout%d" % l]
            for j in range(4):
                kv_p[j][l, b, :, k, :] = kvo[:, 64 * j:64 * (j + 1)]
            for j in range(2):
                win_p[j][l, b, :, k, :] = kvo[T - 512:, 256 + 64 * j:256 + 64 * (j + 1)]
            rs = res[c]["rets_out%d" % l]
            g = c // 4
            kvs = res[c]["kvs_out%d" % l]
            for j in range(4):
                kv_s[j][l, 64 * g:64 * g + 64, 0, k, :] = kvs[:, 64 * j:64 * (j + 1)]
            for j in range(2):
                win_s[j][l, 64 * g:64 * g + 64, :, k, :] = res[c]["wins_out%d_%d" % (l, j)].reshape(64, 512, 64)
            ret_s[l, 64 * g:64 * g + 64, k, 0::2, :] = rs[:, 0:128]
            ret_s[l, 64 * g:64 * g + 64, k, 1::2, :] = rs[:, 128:256]
            ro = res[c]["retout%d" % l]
            ret_p[l, b, k, 0::2, :] = ro[0:128]
            ret_p[l, b, k, 1::2, :] = ro[128:256]
    return (y_prompt, y_sample,
            kv_p[0], kv_s[0], kv_p[1], kv_s[1], kv_p[2], kv_s[2], kv_p[3], kv_s[3],
            win_p[0], win_s[0], win_p[1], win_s[1], ret_p, ret_s)
```

```python
import os
from contextlib import ExitStack

import numpy as np
import concourse.bass as bass
import concourse.mybir as mybir
from concourse.bass_utils import run_bass_kernel_spmd

F32 = mybir.dt.float32
BF16 = mybir.dt.bfloat16
I32 = mybir.dt.int32
ALU = mybir.AluOpType
AF = mybir.ActivationFunctionType
AX = mybir.AxisListType

T = 8192
NT = T // 128
D = 1024
KC = D // 128
NEGB = -30000.0
ALPHA = 8.0 ** 0.25
LN_EPS = 1e-5


class Buf:
    __slots__ = ("name", "lw", "rd_eng", "rd_dma")

    def __init__(self, name):
        self.name = name
        self.lw = None
        self.rd_eng = {}
        self.rd_dma = []


class Op:
    __slots__ = ("eng", "fn", "deps", "kind", "sem", "val", "need", "semkey")


class Prog:
    def __init__(self, nc):
        self.nc = nc
        self.ops = []
        self.bufs = {}

    def buf(self, name):
        b = self.bufs.get(name)
        if b is None:
            b = Buf(name)
            self.bufs[name] = b
        return b

    def add(self, eng, fn, r=(), w=(), kind="c", semkey=None):
        op = Op()
        op.eng, op.fn, op.kind, op.semkey = eng, fn, kind, semkey
        op.need = kind != "c"
        op.sem = None
        op.val = 0
        deps = {}
        for b in r:
            if b.lw is not None:
                deps[id(b.lw)] = b.lw
        for b in w:
            if b.lw is not None:
                deps[id(b.lw)] = b.lw
            for o in b.rd_eng.values():
                deps[id(o)] = o
            for o in b.rd_dma:
                deps[id(o)] = o
        for b in r:
            if kind == "c":
                b.rd_eng[eng] = op
            else:
                b.rd_dma.append(op)
        for b in w:
            b.lw = op
            b.rd_eng = {}
            b.rd_dma = []
        dl = []
        for d in deps.values():
            if d is op:
                continue
            if d.kind == "c" and kind == "c" and d.eng == "pe" and eng == "pe":
                continue
            d.need = True
            dl.append(d)
        op.deps = dl
        self.ops.append(op)
        return op

    def mm(self, out, lhsT, rhs, start, stop, r, w, **kw):
        return self.add("pe", lambda e: e.matmul(out, lhsT, rhs, start=start, stop=stop, **kw), r, w)

    def tr(self, out, in_, ident, r, w):
        return self.add("pe", lambda e: e.transpose(out, in_, ident), r, w)

    def act(self, out, in_, func, r, w, **kw):
        return self.add("act", lambda e: e.activation(out, in_, func, **kw), r, w)

    def dma(self, eng, out, in_, r, w, semkey):
        return self.add(eng, lambda e: e.dma_start(out=out, in_=in_), r, w, kind="d", semkey=semkey)

    def emit(self, es):
        nc = self.nc
        engsem = {}
        for e in ("pe", "act", "dve", "pool", "sp"):
            engsem[e] = es.enter_context(nc.semaphore("sem_" + e))
        dsem = {}
        cnt = {e: 0 for e in engsem}
        dcnt = {}
        for op in self.ops:
            if op.kind == "c":
                if op.need:
                    cnt[op.eng] += 1
                    op.sem = engsem[op.eng]
                    op.val = cnt[op.eng]
            else:
                k = op.semkey
                if k not in dsem:
                    dsem[k] = es.enter_context(nc.semaphore("dsem_%d" % len(dsem)))
                    dcnt[k] = 0
                dcnt[k] += 16 if op.kind == "d" else 1
                op.sem = dsem[k]
                op.val = dcnt[k]
        self.n_sems = len(dsem) + 5
        ops = self.ops
        final = [(dsem[k], dcnt[k]) for k in dsem]

        def run(name, is_last=False):
            def f(e):
                waited = {}
                for op in ops:
                    if op.eng != name:
                        continue
                    need = {}
                    for d in op.deps:
                        key = id(d.sem)
                        if need.get(key, (None, 0))[1] < d.val:
                            need[key] = (d.sem, d.val)
                    for key, (sem, val) in need.items():
                        if waited.get(key, 0) < val:
                            e.wait_ge(sem, val)
                            waited[key] = val
                    ins = op.fn(e)
                    if op.kind == "d":
                        ins.then_inc(op.sem, 16)
                    elif op.kind == "cc":
                        ins.then_inc(op.sem)
                    elif op.need:
                        ins.then_inc(op.sem, 1)
                if is_last:
                    for sem, val in final:
                        e.wait_ge(sem, val)
            return f

        with nc.Block() as block:
            block.tensor(run("pe"))
            block.scalar(run("act"))
            block.vector(run("dve"))
            block.gpsimd(run("pool"))
            block.sync(run("sp", True))


def bcast(ap, pos, n):
    a = [list(x) for x in ap.ap]
    a.insert(pos, [0, n])
    return bass.AP(ap.tensor, ap.offset, a)


def _rope_table():
    half = 8
    inv = (np.float32(500000.0) ** (-np.arange(half, dtype=np.float32) / np.float32(half))).astype(np.float32)
    pos = np.arange(T, dtype=np.float32)
    ang = (pos[:, None] * inv[None, :]).astype(np.float32)
    tab = np.concatenate([np.cos(ang), np.sin(ang)], -1).astype(np.float32)
    return np.ascontiguousarray(tab.reshape(NT, 128, 16).transpose(1, 0, 2))


def _consts_bf16like():
    c = {}
    c["ident"] = np.eye(128, dtype=np.float32)
    m = np.arange(128)[:, None]
    r = np.arange(128)[None, :]
    c["triu"] = np.where(m <= r, 0.0, NEGB).astype(np.float32)
    c["tril"] = np.where(m >= r, 0.0, NEGB).astype(np.float32)
    xx = np.arange(4096)[None, :]
    c["ew"] = ((np.arange(128)[:, None] % 64) == (2 * (xx // 128) + (xx % 128) // 64)).astype(np.float32)
    p = np.arange(32)[:, None]
    x = np.arange(272)[None, :]
    c["lw"] = ((p == np.clip(x - 126, 0, 10)) & (p <= 10)).astype(np.float32)
    rr = np.arange(128)[None, :]
    wb = np.where((p - 1) <= ((rr + 1) // 16), 0.0, NEGB)
    wb[11:] = 0.0
    c["wb"] = wb.astype(np.float32)
    cst = np.arange(512)[:, None] * 16
    sst = np.arange(128)[None, :] * 64
    ov = np.clip(np.minimum(cst + 32, sst + 64) - np.maximum(cst, sst), 0, None).astype(np.float32) / 32.0
    ov[511:] = 0.0
    c["ov"] = np.ascontiguousarray(ov.reshape(4, 128, 128).transpose(1, 0, 2))
    r_ = np.arange(128)[:, None]
    sp = np.arange(254)[None, :] - 126
    bq = (r_ >= 64).astype(np.int64)
    forced = (sp == bq) | (sp == bq - 1)
    c["tb"] = (1000.0 * forced + np.where(sp <= bq, 0.0, -1.0e30)).astype(np.float32)
    return c


def _pool_slice(inp, nm, l, k):
    key = (nm, l, k)
    if key not in _CACHE["pool"]:
        a = inp[nm][l][:, :, k, :]
        _CACHE["pool"][key] = np.ascontiguousarray(a).reshape(a.shape[0], 8192)
    return _CACHE["pool"][key]


def _sample_tables():
    t = {}
    p = np.arange(128)
    s8, j16 = p // 16, p % 16
    t["d8"] = (s8[:, None] == np.arange(8)[None, :]).astype(np.float32)
    t["d0"] = ((s8[:, None] == np.arange(8)[None, :]) & (j16[:, None] == 0)).astype(np.float32)
    t["dd"] = (s8[:, None] == s8[None, :]).astype(np.float32)
    cm = np.zeros((128, 8), np.float32)
    cm[j16 == 15, 7] = NEGB
    t["cmaskl"] = cm
    cst = (8 * j16[:, None] + np.arange(8)[None, :]) * 16
    sst = np.arange(33) * 64
    ov = np.clip(np.minimum(cst[:, :, None] + 32, sst[None, None, :] + 64) - np.maximum(cst[:, :, None], sst[None, None, :]), 0, None) / 32.0
    ov[j16 == 15, 7, :] = 0.0
    t["ovt"] = np.ascontiguousarray(ov.astype(np.float32).reshape(128, 264))
    jb = np.arange(33)
    forced = (jb == 0) | (jb == 32) | (jb == 31)
    t["tbs"] = np.ascontiguousarray(np.broadcast_to((1000.0 * forced).astype(np.float32)[None, :], (64, 33)))
    return t


def _xpos_tables():
    half = 128
    inv = (1.0 / (np.float32(10000.0) ** np.linspace(0.0, 1.0, half, dtype=np.float32))).astype(np.float32)
    pos = np.arange(T, dtype=np.float32)
    ang = (pos[None, :] * inv[:, None]).astype(np.float32)
    return np.ascontiguousarray(np.cos(ang).astype(np.float32)), np.ascontiguousarray(np.sin(ang).astype(np.float32))


def _decay_tables(h):
    lg = np.log1p(-np.float64(2.0) ** (-5.0 - h))
    i = np.arange(128, dtype=np.float64)
    diff = i[None, :] - i[:, None]
    dm = np.where(diff >= 0, np.exp(np.maximum(diff, 0.0) * lg), 0.0) / 16.0
    qd = np.broadcast_to(np.exp((i + 1.0) * lg)[None, :], (128, 128))
    kc = np.stack([np.exp((127.0 - i) * lg) / 16.0, np.full(128, np.exp(128.0 * lg)), np.full(128, np.exp(lg))], 1)
    return {"dmaskT": np.ascontiguousarray(dm.astype(np.float32)), "qdec": np.ascontiguousarray(qd.astype(np.float32)),
            "kcdec": np.ascontiguousarray(kc.astype(np.float32))}


def build(stage):
    nc = bass.Bass("TRN2", target_bir_lowering=False)
    P = Prog(nc)
    es = ExitStack()
    NL = int(os.environ.get("KNL", "4"))
    NQT = int(os.environ.get("KNQT", str(NT)))
    NT1 = int(os.environ.get("KNT1", str(NT)))

    def din(name, shape, dt=F32):
        return nc.dram_tensor(name, list(shape), dt, kind="ExternalInput").ap()

    def dout(name, shape, dt=F32):
        return nc.dram_tensor(name, list(shape), dt, kind="ExternalOutput").ap()

    def sb(name, shape, dt):
        return es.enter_context(nc.sbuf_tensor("s_" + name, list(shape), dt))

    xp = din("xp", [T, D])
    rope_d = din("rope", [128, NT, 16])
    ident_d = din("ident", [128, 128])
    triu_d = din("triu", [128, 128])
    tril_d = din("tril", [128, 128])
    ew_d = din("ew", [128, 4096])
    lw_d = din("lw", [32, 272])
    wb_d = din("wb", [32, 128])
    ov_d = din("ov", [128, 4, 128])
    tb_d = din("tb", [128, 254])
    lng_d = din("ln_g", [4, D])
    lnb_d = din("ln_b", [4, D])
    nsa_win = [din("nsa_win%d" % l, [D, 1432]) for l in range(2)]
    nsa_wout = [din("nsa_wout%d" % l, [512, D]) for l in range(2)]
    nsa_w1 = [din("nsa_w1_%d" % l, [128, 32 * 128]) for l in range(2)]
    nsa_pe = [din("nsa_pe%d" % l, [128, 32]) for l in range(2)]
    nsa_w2 = [din("nsa_w2_%d" % l, [128, 192]) for l in range(2)]
    ret_win = [din("ret_win%d" % l, [D, 1536]) for l in range(2)]
    ret_wout = [din("ret_wout%d" % l, [512, D]) for l in range(2)]
    ret_gn = [din("ret_gn%d" % l, [1, 512]) for l in range(2)]
    xcos_d = din("xcos", [128, T])
    xsin_d = din("xsin", [128, T])
    dmask_d = din("dmaskT", [128, 128])
    qdec_d = din("qdec", [128, 128])
    kcdec_d = din("kcdec", [128, 3])
    kvout = [dout("kvout%d" % l, [T, 384]) for l in range(2)]
    retout = [dout("retout%d" % l, [256, 512]) for l in range(2)]
    SAMPLE = int(os.environ.get("KSAMPLE", "1"))
    xs_d = din("xs", [64, D])
    sret_d = [din("sret%d" % l, [64, 256, 512]) for l in range(2)]
    xp2048_d = din("xp2048", [1, 256])
    ys_out = dout("ys_out", [64, D])
    rets_out = [dout("rets_out%d" % l, [64, 256, 512]) for l in range(2)]
    pt_d = din("ptab", [8, 128], I32)
    pools = [[din("pool%d_%d" % (l, j), [2560, 8192]) for j in range(4)] for l in range(2)]
    cwin = [[din("cwin%d_%d" % (l, j), [64, 32768]) for j in range(2)] for l in range(2)]
    nsa_w1s = [din("nsa_w1s%d" % l, [128, 4096]) for l in range(2)]
    d8_d = din("d8", [128, 8]); d0_d = din("d0", [128, 8]); dd_d = din("dd", [128, 128]); cmask_d = din("cmaskl", [128, 8])
    ovt_d = din("ovt", [128, 264]); tbs_d = din("tbs", [64, 33]); rope2048_d = din("rope2048", [1, 16])
    kvs_out = [dout("kvs_out%d" % l, [64, 384]) for l in range(2)]
    wins_out = [[dout("wins_out%d_%d" % (l, j), [64, 32768]) for j in range(2)] for l in range(2)]
    sq_d = nc.dram_tensor("sq_d", [64, 920], F32).ap()
    selb_d = nc.dram_tensor("selb_d", [64, 32], F32).ap()
    so_d = nc.dram_tensor("so_d", [512, 64], F32).ap()
    xs_cur = nc.dram_tensor("xs_cur", [64, D], F32).ap()
    ypart_s = [nc.dram_tensor("ypart_s%d" % l, [64, D], F32) for l in range(4)]
    ysum_s = [nc.dram_tensor("ysum_s%d" % l, [64, D], F32) for l in range(4)]
    yout = dout("yout", [T, D])
    xT = nc.dram_tensor("xT", [KC, 128, T], BF16).ap()
    xcur = nc.dram_tensor("xcur", [T, D], F32).ap()
    ypart = [[nc.dram_tensor("ypart%d_%d" % (l, c), [1024, D], F32) for c in range(8)] for l in range(4)]
    ysum = [[nc.dram_tensor("ysum%d_%d" % (l, c), [1024, D], F32) for c in range(8)] for l in range(4)]

    ident_f = sb("ident_f", [128, 128], F32)
    ident = sb("ident", [128, 128], BF16)
    rope = sb("rope_sb", [128, NT, 16], F32)
    fbuf = [sb("fbuf%d" % i, [128, D], F32) for i in range(4)]
    xin = fbuf[0:2]
    xbf = [sb("xbf%d" % i, [128, D], BF16) for i in range(2)]
    xTt = [sb("xTt%d" % i, [128, KC * 128], BF16) for i in range(2)]
    stg = [sb("stg%d" % i, [128, 1536], F32) for i in range(2)]
    w_nsa = sb("w_in", [128, KC, 1536], BF16)
    w_out = sb("w_out", [128, 4, D], BF16)
    kvsb = [sb("kvsb%d" % i, [128, 384], F32) for i in range(2)]
    ropetmp = [sb("ropetmp%d" % i, [128, 4, 64], F32) for i in range(2)]
    kstage = [sb("kstage%d" % i, [128, 2, 128], BF16) for i in range(2)]
    KT = sb("KT", [128, 2, T], BF16)
    Vs = sb("Vs", [128, NT, 65], BF16)
    Vw = sb("Vw", [128, NT, 65], BF16)
    Vc = sb("Vc", [128, 4, 65], BF16)
    KcT2 = sb("KcT2", [128, 512], BF16)
    w1sb = sb("w1sb", [128, 32, 128], BF16)
    pesb = sb("pesb", [128, 32], BF16)
    w2sb = sb("w2sb", [128, 192], BF16)
    hb = sb("hbias", [128, 2], F32)
    hsb = [sb("hsb%d" % i, [128, 512], BF16) for i in range(2)]
    gtmp = fbuf[0:3]
    ew = sb("ew", [128, 4096], BF16)
    lw = sb("lw", [32, 272], BF16)
    wb4 = sb("wb4", [32, 512], BF16)
    triu4 = sb("triu4", [128, 512], BF16)
    tril4 = sb("tril4", [128, 512], BF16)
    ov = sb("ov", [128, 4, 128], BF16)
    tb = sb("tb", [128, 254], F32)
    lng, lnb = stg[0], stg[1]
    xq = [sb("xq%d" % i, [128, KC, 128], BF16) for i in range(2)]
    qf = sb("qf", [128, 512], F32)
    qbf = sb("qbf", [128, 1024], BF16)
    QT = sb("QT", [128, 1024], BF16)
    gsb = sb("gsb", [128, 24], F32)
    szs = sb("szs", [128, 512], F32)
    PT = [sb("PT%d" % i, [128, 512], BF16) for i in range(3)]
    PTc = sb("PTc", [128, 8, 512], BF16)
    oacc = sb("oacc", [128, 512], F32)
    otmp = sb("otmp", [128, 512], F32)
    rz = sb("rz", [128, 3, 8], F32)
    wgt = sb("wgt", [128, 3, 8], F32)
    imp = sb("imp", [128, 128], F32)
    score = sb("score", [128, 128], F32)
    score2 = sb("score2", [128, 128], F32)
    m8 = sb("m8", [128, 16], F32)
    selb = sb("selb", [128, 128], BF16)
    selT4 = sb("selT4", [128, 512], BF16)
    og = sb("og", [128, 512], BF16)
    ogT = sb("ogT", [128, 512], BF16)
    ysb = fbuf[0:2]
    lnx = fbuf[0:2]
    lny = fbuf[2:4]
    lnst = [sb("lnst%d" % i, [128, 8], F32) for i in range(2)]
    zeros = sb("zeros", [128, 512], BF16)
    xcs = [sb("xcs%d" % i, [128, 2, 128], F32) for i in range(2)]
    dmaskT = sb("dmaskT", [128, 128], F32)
    qdec = sb("qdec", [128, 128], F32)
    kcdec = sb("kcdec", [128, 3], F32)
    xsT = sb("xsT", [128, 512], BF16)
    zp = sb("zp", [128, 8], F32); rZs = sb("rZs", [128, 8], F32); Ws = sb("Ws", [128, 8], F32); pcs = sb("pcs", [128, 8], F32)
    ssm = sb("ssm", [128, 8], F32); pself = sb("pself", [128, 8], F32); wself = sb("wself", [128, 8], F32)
    pnL = sb("pnL", [128, 64], F32); Wse = sb("Wse", [128, 64], BF16); vsn = sb("vsn", [128, 64], BF16)
    ssm64 = sb("ssm64", [128, 64], F32); selbL = sb("selbL", [128, 2], F32); ptL = sb("ptL", [128, 8], I32)
    d8 = sb("d8", [128, 8], F32); d0 = sb("d0", [128, 8], F32); cmaskL = sb("cmaskL", [128, 8], F32)
    ddsb = sb("ddsb", [128, 128], F32); ovt = sb("ovt", [128, 8, 33], BF16); tbs = sb("tbs", [64, 33], F32)
    rp2048 = sb("rp2048", [64, 16], F32)
    rtmp = sb("rtmp", [128, 4, 256], F32)
    QKr = sb("QKr", [128, 4, 128], BF16)
    qdT = sb("qdT", [128, 2, 128], BF16)
    kdsb = sb("kdsb", [128, 256], BF16)
    vbf = sb("vbf", [128, 512], BF16)
    ATs = sb("ATs", [128, 128], BF16)
    Sst = sb("Sst", [128, 2, 512], F32)
    Sbf = sb("Sbf", [128, 2, 512], BF16)
    gng = sb("gng", [128, 512], F32)

    if os.environ.get("KDBG"):
        print("SBUF bytes remaining:", nc.sbuf_bytes_remaining)
    ps = [es.enter_context(nc.psum_tensor("ps%d" % i, [128, 512], F32)) for i in range(8)]
    psb = [p[:, :].bitcast(BF16) for p in ps]

    B = P.buf
    PSB = [B("ps%d" % i) for i in range(8)]

    def dve(fn, r, w):
        return P.add("dve", fn, r, w)

    def pool(fn, r, w):
        return P.add("pool", fn, r, w)

    stg_i = [0]

    def load_bf16(dst_ap, src_ap, npart, ncols, dst_bufs):
        s = stg_i[0] % 2
        stg_i[0] += 1
        P.dma("sp", stg[s][0:npart, 0:ncols], src_ap, [], [B("stg%d" % s)], "stg%d" % s)
        P.act(dst_ap, stg[s][0:npart, 0:ncols], AF.Copy, [B("stg%d" % s)], dst_bufs)

    P.dma("sp", ident_f[:, :], ident_d[:, :], [], [B("ident_f")], "c0")
    dve(lambda e: e.tensor_copy(ident[:, :], ident_f[:, :]), [B("ident_f")], [B("ident")])
    P.dma("sp", rope[:, :, :], rope_d[:, :, :], [], [B("rope")], "c1")
    P.dma("sp", tb[:, :], tb_d[:, :], [], [B("tb")], "c2")
    pool(lambda e: e.memset(Vs[:, :, 64:65], 1.0), [], [B("Vs_ones")])
    pool(lambda e: e.memset(Vw[:, :, 64:65], 1.0), [], [B("Vw_ones")])
    pool(lambda e: e.memset(Vc[:, :, 64:65], 1.0), [], [B("Vc_ones")])
    pool(lambda e: e.memset(zeros[:, :], 0.0), [], [B("zeros")])
    pool(lambda e: e.memset(hsb[1][:, 511:512], 0.0), [], [B("hsb1")])
    for (dst_, src_, nm_) in ((d8, d8_d, "d8"), (d0, d0_d, "d0"), (cmaskL, cmask_d, "cmaskL"), (ddsb, dd_d, "ddsb"), (tbs, tbs_d, "tbs")):
        P.dma("sp", dst_[:, :], src_[:, :], [], [B(nm_)], "c_" + nm_)
    load_bf16(ovt[:, :, :].rearrange("p a b -> p (a b)"), ovt_d[:, :], 128, 264, [B("ovt")])
    for g_ in range(8):
        P.dma("sp", ptL[:, g_:g_ + 1], bass.AP(pt_d.tensor, 128 * g_, [[1, 128], [1, 1]]), [], [B("ptL")], "c_ptL")
    for c in range(4):
        load_bf16(ew[:, c * 1024:(c + 1) * 1024], ew_d[:, c * 1024:(c + 1) * 1024], 128, 1024, [B("ew")])
    load_bf16(lw[:, :], lw_d[:, :], 32, 272, [B("lw")])
    load_bf16(ov[:, :, :].rearrange("p a b -> p (a b)"), ov_d[:, :, :].rearrange("p a b -> p (a b)"), 128, 512, [B("ov")])
    for (dst, src, npart, nm) in ((wb4, wb_d, 32, "wb4"), (triu4, triu_d, 128, "triu4"), (tril4, tril_d, 128, "tril4")):
        s = stg_i[0] % 2
        stg_i[0] += 1
        P.dma("sp", stg[s][0:npart, 0:128], src[:, :], [], [B("stg%d" % s)], "stg%d" % s)
        P.act(dst[0:npart, :].rearrange("p (a b) -> p a b", a=4), bcast(stg[s][0:npart, 0:128], 1, 4), AF.Copy,
              [B("stg%d" % s)], [B(nm)])

    def to_xT(t, src_ap, src_bufs, slot):
        xb = xbf[slot]
        P.act(xb[:, :], src_ap, AF.Copy, src_bufs, [B("xbf%d" % slot)])
        pb = psb[6 + slot]
        for c in range(KC):
            P.tr(pb[:, c * 128:(c + 1) * 128], xb[:, c * 128:(c + 1) * 128], ident[:, :],
                 [B("xbf%d" % slot), B("ident")], [PSB[6 + slot]])
        xt = xTt[slot]
        dve(lambda e: e.tensor_copy(xt[:, :], pb[:, :]), [PSB[6 + slot]], [B("xTt%d" % slot)])
        P.dma("sp", xT[:, :, t * 128:(t + 1) * 128].rearrange("c p t -> p c t"),
              xt[:, :].rearrange("p (c t) -> p c t", c=KC),
              [B("xTt%d" % slot)], [B("xT_%d" % t)], "xTst%d" % slot)

    if NT1 < NT:
        pool(lambda e: e.memset(KT[:, :, :], 0.0), [], [B("KT_%d" % t) for t in range(NT)])
        pool(lambda e: e.memset(Vs[:, :, 0:64], 0.0), [], [B("Vs_%d" % t) for t in range(NT)])
        pool(lambda e: e.memset(Vw[:, :, 0:64], 0.0), [], [B("Vw_%d" % t) for t in range(NT)])
    for t in range(NT1):
        s = t % 2
        P.dma("sp", xin[s][:, :], xp[t * 128:(t + 1) * 128, :], [], [B("fbuf%d" % s)], "fbufld%d" % s)
        to_xT(t, xin[s][:, :], [B("fbuf%d" % s)], s)

    def nsa_layer(li, layer):
        for c in range(KC):
            load_bf16(w_nsa[:, c, 0:1432], nsa_win[li][c * 128:(c + 1) * 128, :], 128, 1432, [B("w_nsa")])
        for c in range(4):
            load_bf16(w_out[:, c, :], nsa_wout[li][c * 128:(c + 1) * 128, :], 128, D, [B("w_out")])
        for c in range(4):
            load_bf16(w1sb[:, 8 * c:8 * c + 8, :].rearrange("p a b -> p (a b)"), nsa_w1[li][:, 1024 * c:1024 * c + 1024], 128, 1024, [B("w1sb")])
        load_bf16(pesb[:, :], nsa_pe[li][:, :], 128, 32, [B("pesb")])
        load_bf16(w2sb[:, :], nsa_w2[li][:, :], 128, 192, [B("w2sb")])
        for kv_i in range(2):
            lo = 64 * kv_i
            pbias = ps[7 - 3 * kv_i]
            for l in range(32):
                P.mm(pbias[:, 0:1], w1sb[lo:lo + 64, l, :], pesb[lo:lo + 64, l:l + 1], l == 0, l == 31,
                     [B("w1sb"), B("pesb")], [PSB[7 - 3 * kv_i]])
            dve(lambda e, pbias=pbias, kv_i=kv_i: e.tensor_copy(hb[:, kv_i:kv_i + 1], pbias[:, 0:1]), [PSB[7 - 3 * kv_i]], [B("hb")])
        if SAMPLE:
            nsa_sample(li, layer)
            pool(lambda e: e.memset(Vs[:, :, 64:65], 1.0), [], Vskeys + [B("Vs_ones")])
            pool(lambda e: e.memset(Vw[:, :, 64:65], 1.0), [], Vwkeys + [B("Vw_ones")])
        if os.environ.get("KNOPROMPT"):
            return

        for u in range(NT1):
            for j in range(1):
                t = u
                s = t % 2
                P.dma("sp", xq[s][:, :, :], xT[:, :, t * 128:(t + 1) * 128].rearrange("c p t -> p c t"),
                      [B("xT_%d" % t)], [B("xq%d" % s)], "xq%d" % s)
                pk = ps[4 + s]
                for c in range(KC):
                    P.mm(pk[:, 0:384], xq[s][:, c, :], w_nsa[:, c, 512:896],
                         c == 0, c == KC - 1, [B("xq%d" % s), B("w_nsa")], [PSB[4 + s]])
                kv = kvsb[s]
                P.act(kv[:, :], pk[:, 0:384], AF.Copy, [PSB[4 + s]], [B("kvsb%d" % s)])
                kv3 = kv[:, :].rearrange("p (a b) -> p a b", a=3)
                rope_apply(kv3[:, :, 0:8], kv3[:, :, 8:16], 3, t, ropetmp[s], "rt%d" % s, [B("kvsb%d" % s)], [B("kvsb%d" % s)])
                P.dma("sp", kvout[li][t * 128:(t + 1) * 128, :], kv[:, :], [B("kvsb%d" % s)],
                      [B("kvout_%d_%d" % (li, t))], "kvo%d" % s)
                kst = kstage[s]
                sbk = [B("kvsb%d" % s)]
                pool(lambda e, kst=kst, kv=kv: e.tensor_copy(kst[:, 0, :], kv[:, 0:128]), sbk, [B("kst%d_0" % s)])
                gs = t // 32
                pool(lambda e, kst=kst, kv=kv, gs=gs: e.tensor_copy(kst[:, 1, 64 * gs:64 * gs + 64], kv[:, 128:192]), sbk, [B("kst%d_1" % s)])
                pool(lambda e, kst=kst, kv=kv, gs=gs: e.tensor_copy(kst[:, 1, 64 - 64 * gs:128 - 64 * gs], kv[:, 256:320]), sbk + [B("kst%d_1" % s)], [B("kst%d_1" % s)])
                pb = psb[6 + s]
                for a in range(2):
                    P.tr(pb[:, a * 128:(a + 1) * 128], kst[:, a, :], ident[:, :], [B("kst%d_%d" % (s, a)), B("ident")], [PSB[6 + s]])
                dve(lambda e, pb=pb, t=t: e.tensor_copy(KT[:, :, t * 128:(t + 1) * 128], pb[:, 0:256].rearrange("p (a t) -> p a t", a=2)),
                    [PSB[6 + s]], [B("KT_%d" % t)])
                pool(lambda e, kv=kv, t=t: e.tensor_copy(Vs[:, t, 0:64], kv[:, 192:256]), sbk, [B("Vs_%d" % t)])
                pool(lambda e, kv=kv, t=t: e.tensor_copy(Vw[:, t, 0:64], kv[:, 320:384]), sbk, [B("Vw_%d" % t)])
        KTall = [B("KT_%d" % t) for t in range(NT)]

        for kv_i in range(2):
            lo = 64 * kv_i
            ph = ps[5 + kv_i]
            for l in range(32):
                rhs = KT[lo:lo + 64, 0, l:l + 16 * 510 + 1:16]
                P.mm(ph[:, 0:511], w1sb[lo:lo + 64, l, :], rhs, l == 0, l == 31, [B("w1sb")] + KTall, [PSB[5 + kv_i]])
            g0, g1, g2 = gtmp
            P.act(g0[:, 0:511], ph[:, 0:511], AF.Identity, [PSB[5 + kv_i], B("hb")], [B("fbuf0")], bias=hb[:, kv_i:kv_i + 1])
            dve(lambda e: e.tensor_tensor(g1[:, 0:511], g0[:, 0:511], g0[:, 0:511], ALU.mult), [B("fbuf0")], [B("fbuf1")])
            dve(lambda e: e.tensor_scalar(g1[:, 0:511], g1[:, 0:511], 0.044715, 1.0, ALU.mult, ALU.add), [B("fbuf1")], [B("fbuf1")])
            dve(lambda e: e.tensor_tensor(g1[:, 0:511], g1[:, 0:511], g0[:, 0:511], ALU.mult), [B("fbuf1"), B("fbuf0")], [B("fbuf1")])
            P.act(g2[:, 0:511], g1[:, 0:511], AF.Sigmoid, [B("fbuf1")], [B("fbuf2")], scale=1.5957691216)
            h = hsb[kv_i]
            dve(lambda e, h=h: e.tensor_tensor(h[:, 0:511], g2[:, 0:511], g0[:, 0:511], ALU.mult), [B("fbuf2"), B("fbuf0")], [B("hsb%d" % kv_i)])
        pk2 = ps[7]
        P.mm(pk2[:, 0:511], w2sb[:, 0:128], hsb[0][:, 0:511], True, True, [B("w2sb"), B("hsb0")], [PSB[7]])
        dve(lambda e: e.tensor_copy(KcT2[:, 0:511], pk2[:, 0:511]), [PSB[7]], [B("KcT2")])
        for ci in range(4):
            n = min(128, 511 - 128 * ci)
            pv = ps[4]
            P.mm(pv[0:n, 0:64], hsb[1][:, 128 * ci:128 * ci + n], w2sb[:, 128:192], True, True, [B("w2sb"), B("hsb1")], [PSB[4]])
            dve(lambda e, ci=ci, n=n, pv=pv: e.tensor_copy(Vc[0:n, ci, 0:64], pv[0:n, 0:64]), [PSB[4]], [B("Vc")])

        srot = [0]

        def unit(lhsT_k, lo, nkeys, half, masks, pt_ap, pt_buf, kbufs):
            si = srot[0] % 3
            srot[0] += 1
            S = ps[si]
            nm = len(masks)
            P.mm(S[0:nkeys, :], lhsT_k, QT[lo:lo + 64, half * 512:(half + 1) * 512], True, nm == 0, kbufs + [B("QT")], [PSB[si]])
            for mi, (ml, mr, mb) in enumerate(masks):
                P.mm(S[0:nkeys, :], ml, mr, False, mi == nm - 1, mb, [PSB[si]])
            P.act(pt_ap, S[0:nkeys, :], AF.Exp, [PSB[si]], [pt_buf], scale=0.125)

        def pv_acc(pt_ap, pt_buf, nkeys, v_ap, vbufs, half, first, last):
            O = ps[3 + half]
            if first:
                P.mm(O[:, 0:260], zeros[0:32, 0:128], zeros[0:32, 0:260], True, False, [B("zeros")], [PSB[3 + half]])
            for jj in range(4):
                P.mm(O[:, jj * 65:(jj + 1) * 65], pt_ap[:, jj * 128:(jj + 1) * 128], v_ap, False, last and jj == 3,
                     [pt_buf] + vbufs, [PSB[3 + half]])

        pend = []

        def defer(f):
            if pend:
                pend.pop()()
            pend.append(f)

        def flush():
            if pend:
                pend.pop()()

        def combine(br, first_branch):
            for half in range(2):
                O3 = ps[3 + half][:, 0:260].rearrange("p (j e) -> p j e", j=4)
                rzs = rz[:, br, 4 * half:4 * half + 4]
                wg = wgt[:, br, 4 * half:4 * half + 4]
                kz = B("rz_%d_%d" % (br, half))
                kw_ = B("wgt_%d_%d" % (br, half))
                dve(lambda e, O3=O3, rzs=rzs: e.tensor_scalar(rzs, O3[:, :, 64], 1.0e-30, None, ALU.max), [PSB[3 + half]], [kz])
                dve(lambda e, rzs=rzs: e.reciprocal(rzs, rzs), [kz], [kz])
                gsl = gsb[:, br * 8 + 4 * half:br * 8 + 4 * half + 4]
                dve(lambda e, wg=wg, rzs=rzs, gsl=gsl: e.tensor_tensor(wg, rzs, gsl, ALU.mult), [kz, B("gsb")], [kw_])
                oa = oacc[:, 256 * half:256 * half + 256].rearrange("p (j e) -> p j e", j=4)
                if first_branch:
                    dve(lambda e, oa=oa, O3=O3, wg=wg: e.tensor_tensor(oa, O3[:, :, 0:64], bcast(wg, 2, 64), ALU.mult),
                        [PSB[3 + half], kw_], [B("oacc%d" % half)])
                else:
                    ot = otmp[:, 256 * half:256 * half + 256].rearrange("p (j e) -> p j e", j=4)
                    dve(lambda e, ot=ot, O3=O3, wg=wg: e.tensor_tensor(ot, O3[:, :, 0:64], bcast(wg, 2, 64), ALU.mult),
                        [PSB[3 + half], kw_], [B("otmp%d" % half)])
                    pool(lambda e, oa=oa, ot=ot: e.tensor_tensor(oa, oa, ot, ALU.add), [B("otmp%d" % half), B("oacc%d" % half)], [B("oacc%d" % half)])

        for i in range(int(os.environ.get("KQ0", "0")), NQT):
            sq = i % 2
            P.dma("sp", xq[sq][:, :, :], xT[:, :, i * 128:(i + 1) * 128].rearrange("c p t -> p c t"),
                  [B("xT_%d" % i)], [B("xq%d" % sq)], "xq%d" % sq)
            for (col0, ncol, pi) in ((0, 512, 5), (920, 512, 6), (896, 24, 7)):
                for c in range(KC):
                    P.mm(ps[pi][:, 0:ncol], xq[sq][:, c, :], w_nsa[:, c, col0:col0 + ncol], c == 0, c == KC - 1,
                         [B("xq%d" % sq), B("w_nsa")], [PSB[pi]])
            P.act(qf[:, :], ps[5][:, :], AF.Copy, [PSB[5]], [B("qf")])
            P.act(szs[:, :], ps[6][:, :], AF.Silu, [PSB[6]], [B("szs")])
            P.act(gsb[:, :], ps[7][:, 0:24], AF.Sigmoid, [PSB[7]], [B("gsb")])
            q3 = qf[:, :].rearrange("p (a b) -> p a b", a=8)
            rope_apply(q3[:, :, 0:8], q3[:, :, 8:16], 8, i, ropetmp[0], "rt0", [B("qf")], [B("qf")])
            pool(lambda e: e.tensor_copy(qbf[:, :].rearrange("p (h a d) -> p h a d", h=8, a=2), bcast(qf[:, :].rearrange("p (h d) -> p h d", h=8), 2, 2)), [B("qf")], [B("qbf")])
            pq = psb[7]
            for j in range(8):
                P.tr(pq[:, j * 128:(j + 1) * 128], qbf[:, j * 128:(j + 1) * 128], ident[:, :], [B("qbf"), B("ident")], [PSB[7]])
            dve(lambda e, pq=pq: e.tensor_copy(QT[:, :], pq[:, :]), [PSB[7]], [B("QT")])

            chunks = [ci for ci in range(4) if 128 * ci <= 8 * i + 6]
            for half in range(2):
                for k_, ci in enumerate(chunks):
                    cs = 128 * ci
                    n = min(128, 511 - cs)
                    delta = 8 * i - cs
                    masks = []
                    if delta <= 129:
                        off = 129 - delta
                        masks.append((lw[0:32, off:off + n], wb4[0:32, :], [B("lw"), B("wb4")]))
                    pt = PTc[0:n, 2 * ci + half, :]
                    ptb = B("PTc_%d_%d" % (ci, half))
                    unit(KcT2[0:64, cs:cs + n], 0, n, half, masks, pt, ptb, [B("KcT2")])
                    defer(lambda pt=pt, ptb=ptb, n=n, ci=ci, half=half, k_=k_: pv_acc(pt, ptb, n, Vc[0:n, ci, :], [B("Vc"), B("Vc_ones")], half, k_ == 0, k_ == len(chunks) - 1))
            flush()
            combine(0, True)
            for half in range(2):
                for jj in range(4):
                    for k_, ci in enumerate(chunks):
                        n = min(128, 511 - 128 * ci)
                        P.mm(ps[5 + half][:, jj * 128:(jj + 1) * 128], PTc[0:n, 2 * ci + half, jj * 128:(jj + 1) * 128], ov[0:n, ci, :],
                             k_ == 0, k_ == len(chunks) - 1, [B("PTc_%d_%d" % (ci, half)), B("ov")], [PSB[5 + half]])
            first = True
            for half in range(2):
                for jj in range(4):
                    A = ps[5 + half][:, jj * 128:(jj + 1) * 128]
                    sc = rz[:, 0, 4 * half + jj:4 * half + jj + 1]
                    rb = [PSB[5 + half], B("rz_0_%d" % half)]
                    if first:
                        dve(lambda e, A=A, sc=sc: e.tensor_scalar(imp[:, :], A, sc, None, ALU.mult), rb, [B("imp")])
                        first = False
                    else:
                        dve(lambda e, A=A, sc=sc: e.scalar_tensor_tensor(imp[:, :], A, sc, imp[:, :], ALU.mult, ALU.add), rb + [B("imp")], [B("imp")])
            toff = 126 - 2 * i
            dve(lambda e, toff=toff: e.tensor_tensor(score[:, :], imp[:, :], tb[:, toff:toff + 128], ALU.add), [B("imp"), B("tb")], [B("score")])
            if i >= 1:
                dve(lambda e: e.tensor_scalar(score[:, 0:1], score[:, 0:1], 1000.0, None, ALU.add), [B("score")], [B("score")])
            dve(lambda e: e.max(m8[:, 0:8], score[:, :]), [B("score")], [B("m8a")])
            dve(lambda e: e.match_replace(score2[:, :], m8[:, 0:8], score[:, :], -1.0e30), [B("score"), B("m8a")], [B("score2")])
            dve(lambda e: e.max(m8[:, 8:16], score2[:, :]), [B("score2")], [B("m8b")])
            dve(lambda e: e.tensor_scalar(selb[:, :], score[:, :], m8[:, 15:16], NEGB, ALU.is_lt, ALU.mult), [B("score"), B("m8b")], [B("selb")])
            P.tr(pq[:, 512:640], selb[:, :], ident[:, :], [B("selb"), B("ident")], [PSB[7]])
            dve(lambda e, pq=pq: e.tensor_copy(selT4[:, :].rearrange("p (a b) -> p a b", a=4), bcast(pq[:, 512:640], 1, 4)), [PSB[7]], [B("selT4")])

            for half in range(2):
                for t in range(i + 1):
                    gg, tm = t // 32, t % 32
                    masks = [(ew[64 * gg:64 * gg + 64, tm * 128:(tm + 1) * 128], selT4[64 * gg:64 * gg + 64, :], [B("ew"), B("selT4")])]
                    if t == i:
                        masks.append((ident[:, :], triu4[:, :], [B("ident"), B("triu4")]))
                    si = srot[0] % 3
                    pt = PT[si][:, :]
                    ptb = B("PT%d" % si)
                    unit(KT[64 * gg:64 * gg + 64, 1, t * 128:(t + 1) * 128], 64 * gg, 128, half, masks, pt, ptb, [B("KT_%d" % t)])
                    defer(lambda pt=pt, ptb=ptb, t=t, half=half: pv_acc(pt, ptb, 128, Vs[:, t, :], [B("Vs_%d" % t), B("Vs_ones")], half, t == 0, t == i))
            flush()
            combine(1, False)
            t0 = max(0, i - 4)
            for half in range(2):
                for t in range(t0, i + 1):
                    masks = []
                    if t == i:
                        masks.append((ident[:, :], triu4[:, :], [B("ident"), B("triu4")]))
                    elif t == i - 4:
                        masks.append((ident[:, :], tril4[:, :], [B("ident"), B("tril4")]))
                    si = srot[0] % 3
                    pt = PT[si][:, :]
                    ptb = B("PT%d" % si)
                    low = 64 - 64 * (t // 32)
                    unit(KT[low:low + 64, 1, t * 128:(t + 1) * 128], low, 128, half, masks, pt, ptb, [B("KT_%d" % t)])
                    defer(lambda pt=pt, ptb=ptb, t=t, half=half: pv_acc(pt, ptb, 128, Vw[:, t, :], [B("Vw_%d" % t), B("Vw_ones")], half, t == t0, t == i))
            flush()
            combine(2, False)

            dve(lambda e: e.tensor_tensor(og[:, :], oacc[:, :], szs[:, :], ALU.mult), [B("oacc0"), B("oacc1"), B("szs")], [B("og")])
            out_tail(i, layer)

    def out_tail(i, layer):
        sq = i % 2
        pq = psb[7]
        for j in range(4):
            P.tr(pq[:, j * 128:(j + 1) * 128], og[:, j * 128:(j + 1) * 128], ident[:, :], [B("og"), B("ident")], [PSB[7]])
        dve(lambda e: e.tensor_copy(ogT[:, :], pq[:, 0:512]), [PSB[7]], [B("ogT")])
        for hh in range(2):
            for c in range(4):
                P.mm(ps[5 + hh][:, :], ogT[:, c * 128:(c + 1) * 128], w_out[:, c, hh * 512:(hh + 1) * 512], c == 0, c == 3,
                     [B("ogT"), B("w_out")], [PSB[5 + hh]])
        ys = ysb[sq]
        P.act(ys[:, 0:512], ps[5][:, :], AF.Copy, [PSB[5]], [B("fbuf%d" % sq)])
        dve(lambda e: e.tensor_copy(ys[:, 512:1024], ps[6][:, :]), [PSB[6]], [B("fbuf%d" % sq)])
        P.dma("sp", ypart[layer][i // 8][(i % 8) * 128:(i % 8 + 1) * 128, :], ys[:, :], [B("fbuf%d" % sq)], [B("ypart_%d_%d" % (layer, i))], "yst%d" % sq)

    def ret_layer(li, layer):
        for c in range(KC):
            load_bf16(w_nsa[:, c, :], ret_win[li][c * 128:(c + 1) * 128, :], 128, 1536, [B("w_nsa")])
        for c in range(4):
            load_bf16(w_out[:, c, :], ret_wout[li][c * 128:(c + 1) * 128, :], 128, D, [B("w_out")])
        P.dma("sp", gng[:, :], bass.AP(ret_gn[li].tensor, 0, [[0, 128], [1, 512]]), [], [B("gng")], "gng")
        P.dma("sp", dmaskT[:, :], dmask_d[:, :], [], [B("dmaskT")], "rc0")
        P.dma("sp", qdec[:, :], qdec_d[:, :], [], [B("qdec")], "rc1")
        P.dma("sp", kcdec[:, :], kcdec_d[:, :], [], [B("kcdec")], "rc2")
        if SAMPLE:
            ret_sample(li, layer)
        nch = NQT
        for n in range(nch):
            sq = n % 2
            P.dma("sp", xq[sq][:, :, :], xT[:, :, n * 128:(n + 1) * 128].rearrange("c p t -> p c t"),
                  [B("xT_%d" % n)], [B("xq%d" % sq)], "xq%d" % sq)
            cs_t = xcs[sq]
            P.dma("sp", cs_t[:, 0, :], xcos_d[:, n * 128:(n + 1) * 128], [], [B("xcs%d" % sq)], "xcsa%d" % sq)
            P.dma("sp", cs_t[:, 1, :], xsin_d[:, n * 128:(n + 1) * 128], [], [B("xcs%d" % sq)], "xcsb%d" % sq)
            for col in range(4):
                for c in range(KC):
                    P.mm(ps[0][:, col * 128:(col + 1) * 128], w_nsa[:, c, col * 128:(col + 1) * 128], xq[sq][:, c, :], c == 0, c == KC - 1,
                         [B("xq%d" % sq), B("w_nsa")], [PSB[0]])
            for (col0, pi) in ((512, 1), (1024, 2)):
                for c in range(KC):
                    P.mm(ps[pi][:, :], xq[sq][:, c, :], w_nsa[:, c, col0:col0 + 512], c == 0, c == KC - 1,
                         [B("xq%d" % sq), B("w_nsa")], [PSB[pi]])
            P.act(vbf[:, :], ps[1][:, :], AF.Copy, [PSB[1]], [B("vbf")])
            P.act(szs[:, :], ps[2][:, :], AF.Silu, [PSB[2]], [B("szs")])
            pool(lambda e: e.tensor_tensor(qf[:, :], szs[:, :], gng[:, :], ALU.mult), [B("szs"), B("gng")], [B("qf")])
            pv4 = ps[0][:, :].rearrange("p (a b t) -> p a b t", a=2, b=2)
            E, O_ = pv4[:, :, 0, :], pv4[:, :, 1, :]
            cosb = bcast(cs_t[:, 0, :], 1, 2)
            sinb = bcast(cs_t[:, 1, :], 1, 2)
            rv = [rtmp[:, a, :].rearrange("p (a t) -> p a t", a=2) for a in range(4)]
            rb = [PSB[0], B("xcs%d" % sq)]
            dve(lambda e, E=E, cosb=cosb: e.tensor_tensor(rv[0], E, cosb, ALU.mult), rb, [B("rtmp0")])
            dve(lambda e, O_=O_, sinb=sinb: e.tensor_tensor(rv[1], O_, sinb, ALU.mult), rb, [B("rtmp1")])
            dve(lambda e, E=E, sinb=sinb: e.tensor_tensor(rv[2], E, sinb, ALU.mult), rb, [B("rtmp2")])
            dve(lambda e, O_=O_, cosb=cosb: e.tensor_tensor(rv[3], O_, cosb, ALU.mult), rb, [B("rtmp3")])
            qk4 = QKr[:, :, :].rearrange("p (a b) t -> p a b t", a=2)
            dve(lambda e: e.tensor_tensor(qk4[:, :, 0, :], rv[0], rv[1], ALU.subtract), [B("rtmp0"), B("rtmp1")], [B("QKr_e")])
            dve(lambda e: e.tensor_tensor(qk4[:, :, 1, :], rv[2], rv[3], ALU.add), [B("rtmp2"), B("rtmp3")], [B("QKr_o")])
            qkb = [B("QKr_e"), B("QKr_o")]
            pool(lambda e: e.tensor_tensor(qdT[:, :, :], QKr[:, 0:2, :], bcast(qdec[:, :], 1, 2), ALU.mult), qkb + [B("qdec")], [B("qdT")])
            pq = psb[7]
            for ch in range(2):
                P.tr(pq[:, ch * 128:(ch + 1) * 128], QKr[:, 2 + ch, :], ident[:, :], qkb + [B("ident")], [PSB[7]])
            dve(lambda e: e.tensor_scalar(kdsb[:, :], pq[:, 0:256], kcdec[:, 0:1], None, ALU.mult), [PSB[7], B("kcdec")], [B("kdsb")])
            for ch in range(2):
                P.mm(ps[3][:, 0:128], QKr[:, 2 + ch, :], QKr[:, ch, :], ch == 0, ch == 1, qkb, [PSB[3]])
            dve(lambda e: e.tensor_tensor(ATs[:, :], ps[3][:, 0:128], dmaskT[:, :], ALU.mult), [PSB[3], B("dmaskT")], [B("ATs")])
            P.mm(ps[4][:, :], ATs[:, :], vbf[:, :], True, n == 0, [B("ATs"), B("vbf")], [PSB[4]])
            if n > 0:
                for ch in range(2):
                    P.mm(ps[4][:, :], qdT[:, ch, :], Sbf[:, ch, :], False, ch == 1, [B("qdT"), B("Sbf")], [PSB[4]])
            for ch in range(2):
                P.mm(ps[5 + ch][:, :], kdsb[:, ch * 128:(ch + 1) * 128], vbf[:, :], True, True, [B("kdsb"), B("vbf")], [PSB[5 + ch]])
                if n == 0:
                    dve(lambda e, ch=ch: e.tensor_copy(Sst[:, ch, :], ps[5 + ch][:, :]), [PSB[5 + ch]], [B("Sst%d" % ch)])
                else:
                    dve(lambda e, ch=ch: e.scalar_tensor_tensor(Sst[:, ch, :], Sst[:, ch, :], kcdec[:, 1:2], ps[5 + ch][:, :], ALU.mult, ALU.add),
                        [PSB[5 + ch], B("Sst%d" % ch), B("kcdec")], [B("Sst%d" % ch)])
                pool(lambda e, ch=ch: e.tensor_copy(Sbf[:, ch, :], Sst[:, ch, :]), [B("Sst%d" % ch)], [B("Sbf")])
            st = lnst[sq]
            kst_ = B("lnst%d" % sq)
            dve(lambda e, st=st: e.reduce_sum(st[:, 0:1], ps[4][:, :], AX.X), [PSB[4]], [kst_])
            dve(lambda e, st=st: e.tensor_scalar(st[:, 1:2], st[:, 0:1], -1.0 / 512, None, ALU.mult), [kst_], [kst_])
            P.act(oacc[:, :], ps[4][:, :], AF.Identity, [PSB[4], kst_], [B("oacc0"), B("oacc1")], bias=st[:, 1:2])
            P.act(otmp[:, :], oacc[:, :], AF.Square, [B("oacc0"), B("oacc1")], [B("otmp0"), B("otmp1")])
            dve(lambda e, st=st: e.reduce_sum(st[:, 2:3], otmp[:, :], AX.X), [B("otmp0"), B("otmp1")], [B("lnsq%d" % sq)])
            dve(lambda e, st=st: e.tensor_scalar(st[:, 3:4], st[:, 2:3], 1.0 / 512, LN_EPS, ALU.mult, ALU.add), [B("lnsq%d" % sq)], [B("lnr%d" % sq)])
            P.act(st[:, 5:6], st[:, 3:4], AF.Sqrt, [B("lnr%d" % sq)], [B("lnr%d" % sq)])
            dve(lambda e, st=st: e.reciprocal(st[:, 4:5], st[:, 5:6]), [B("lnr%d" % sq)], [B("lnr%d" % sq)])
            dve(lambda e, st=st: e.scalar_tensor_tensor(og[:, :], oacc[:, :], st[:, 4:5], qf[:, :], ALU.mult, ALU.mult),
                [B("oacc0"), B("oacc1"), B("lnr%d" % sq), B("qf")], [B("og")])
            out_tail(n, layer)
        for ch in range(2):
            P.dma("sp", retout[li][ch * 128:(ch + 1) * 128, :], Sst[:, ch, :], [B("Sst%d" % ch)], [B("retout_%d_%d" % (li, ch))], "reto%d" % ch)

    def rope_apply(x1, x2, nrep, t, rt, rtname, rbufs, wbufs, cs_ap=None, sn_ap=None, npart=128):
        cs = bcast(rope[:, t, 0:8] if cs_ap is None else cs_ap, 1, nrep)
        sn = bcast(rope[:, t, 8:16] if sn_ap is None else sn_ap, 1, nrep)
        n8 = nrep * 8
        v = [rt[0:npart, a, 0:n8].rearrange("p (a b) -> p a b", a=nrep) for a in range(4)]
        rb = list(rbufs) + ([B("rope")] if cs_ap is None else [])
        keys = [B("%s_%d" % (rtname, a)) for a in range(4)]
        dve(lambda e: e.tensor_tensor(v[0], x1, cs, ALU.mult), rb, [keys[0]])
        dve(lambda e: e.tensor_tensor(v[1], x2, sn, ALU.mult), rb, [keys[1]])
        dve(lambda e: e.tensor_tensor(v[2], x1, sn, ALU.mult), rb, [keys[2]])
        dve(lambda e: e.tensor_tensor(v[3], x2, cs, ALU.mult), rb, [keys[3]])
        dve(lambda e: e.tensor_tensor(x1, v[0], v[1], ALU.subtract), [keys[0], keys[1], keys[2]], wbufs)
        dve(lambda e: e.tensor_tensor(x2, v[2], v[3], ALU.add), [keys[2], keys[3]], wbufs)

    def ln_pass(layer, last):
        ntile = NQT if NQT < NT else NT
        for c in range((ntile + 7) // 8):
            P.add("pool", lambda e, c=c: e.collective_compute("AllReduce", ALU.add, replica_groups=[[0, 1, 2, 3], [4, 5, 6, 7]],
                                                              ins=[ypart[layer][c].ap().opt()], outs=[ysum[layer][c].ap().opt()]),
                  [B("ypart_%d_%d" % (layer, i)) for i in range(8 * c, min(8 * c + 8, ntile))], [B("ysum_%d_%d" % (layer, c))],
                  kind="cc", semkey="cc%d_%d" % (layer, c))
        P.dma("sp", lng[:, 0:D], bass.AP(lng_d.tensor, layer * D, [[0, 128], [1, D]]), [], [B("stg0")], "stg0")
        P.dma("sp", lnb[:, 0:D], bass.AP(lnb_d.tensor, layer * D, [[0, 128], [1, D]]), [], [B("stg1")], "stg1")
        for t in range(ntile):
            s = t % 2
            src = xp if layer == 0 else xcur
            X, Y, st = lnx[s], lny[s], lnst[s]
            kx, ky, kst_ = B("fbuf%d" % s), B("fbuf%d" % (2 + s)), B("lnst%d" % s)
            P.dma("sp", X[:, :], src[t * 128:(t + 1) * 128, :], [B("xcur_%d" % t)], [kx], "fbufld%d" % s)
            P.dma("sp", Y[:, :], ysum[layer][t // 8][(t % 8) * 128:(t % 8 + 1) * 128, :], [B("ysum_%d_%d" % (layer, t // 8))], [ky], "fbufld%d" % (2 + s))
            dve(lambda e, X=X, Y=Y: e.scalar_tensor_tensor(X[:, :], X[:, :], ALPHA, Y[:, :], ALU.mult, ALU.add), [kx, ky], [kx])
            dve(lambda e, X=X, st=st: e.reduce_sum(st[:, 0:1], X[:, :], AX.X), [kx], [kst_])
            dve(lambda e, st=st: e.tensor_scalar(st[:, 1:2], st[:, 0:1], -1.0 / D, None, ALU.mult), [kst_], [kst_])
            P.act(Y[:, :], X[:, :], AF.Identity, [kx, kst_], [ky], bias=st[:, 1:2])
            P.act(X[:, :], Y[:, :], AF.Square, [ky], [kx])
            dve(lambda e, X=X, st=st: e.reduce_sum(st[:, 2:3], X[:, :], AX.X), [kx], [B("lnsq%d" % s)])
            dve(lambda e, st=st: e.tensor_scalar(st[:, 3:4], st[:, 2:3], 1.0 / D, LN_EPS, ALU.mult, ALU.add), [B("lnsq%d" % s)], [B("lnr%d" % s)])
            P.act(st[:, 5:6], st[:, 3:4], AF.Sqrt, [B("lnr%d" % s)], [B("lnr%d" % s)])
            dve(lambda e, st=st: e.reciprocal(st[:, 4:5], st[:, 5:6]), [B("lnr%d" % s)], [B("lnr%d" % s)])
            dve(lambda e, X=X, Y=Y, st=st: e.scalar_tensor_tensor(X[:, :], Y[:, :], st[:, 4:5], lng[:, 0:D], ALU.mult, ALU.mult),
                [ky, B("lnr%d" % s), B("stg0"), B("lnsq%d" % s)], [kx])
            pool(lambda e, X=X: e.tensor_tensor(X[:, :], X[:, :], lnb[:, 0:D], ALU.add), [kx, B("stg1")], [kx])
            dst = yout if last else xcur
            P.dma("sp", dst[t * 128:(t + 1) * 128, :], X[:, :], [kx], [B("xcur_%d" % t)], "lnst%d" % s)
            if not last:
                to_xT(t, X[:, :], [kx], s)

    KTkeys = [B("KT_%d" % t) for t in range(NT)]
    Vskeys = [B("Vs_%d" % t) for t in range(NT)]
    Vwkeys = [B("Vw_%d" % t) for t in range(NT)]
    PTckeys = [B("PTc_%d_%d" % (ci, hf)) for ci in range(4) for hf in range(2)]
    GA = KT[:, :, :].rearrange("p a t -> p (a t)").bitcast(F32)
    XTlo = Vs[:, :, :].rearrange("p a b -> p (a b)")[:, 0:4096].rearrange("p (a b) -> p a b", a=32)
    XThi = Vw[:, :, :].rearrange("p a b -> p (a b)")[:, 0:4096].rearrange("p (a b) -> p a b", a=32)
    W1s = PTc[:, :, :].rearrange("p a b -> p (a b)").rearrange("p (k c h) -> p k c h", k=2, c=16)
    FB = [B("fbuf%d" % i) for i in range(4)]

    def load_xsT(layer):
        src = xs_d if layer == 0 else xs_cur
        P.dma("sp", fbuf[0][0:64, :], src[:, :], [B("xs_cur")], [FB[0]], "fbufld0")
        P.act(xbf[0][0:64, :], fbuf[0][0:64, :], AF.Copy, [FB[0]], [B("xbf0")])
        pb = psb[7]
        for c in range(KC):
            P.tr(pb[:, c * 64:(c + 1) * 64], xbf[0][0:64, c * 128:(c + 1) * 128], ident[0:64, 0:64], [B("xbf0"), B("ident")], [PSB[7]])
        dve(lambda e: e.tensor_copy(xsT[:, :], pb[:, 0:512]), [PSB[7]], [B("xsT")])

    def s_proj(col0, ncol, pi):
        for c in range(KC):
            P.mm(ps[pi][0:64, 0:ncol], xsT[:, c * 64:(c + 1) * 64], w_nsa[:, c, col0:col0 + ncol], c == 0, c == KC - 1,
                 [B("xsT"), B("w_nsa")], [PSB[pi]])

    def s_out_tail(layer):
        pq = psb[7]
        for j in range(4):
            P.tr(pq[:, j * 64:(j + 1) * 64], og[0:64, j * 128:(j + 1) * 128], ident[0:64, 0:64], [B("og"), B("ident")], [PSB[7]])
        dve(lambda e: e.tensor_copy(ogT[:, 0:256], pq[:, 0:256]), [PSB[7]], [B("ogT")])
        for hh in range(2):
            for c in range(4):
                P.mm(ps[5 + hh][0:64, :], ogT[:, c * 64:(c + 1) * 64], w_out[:, c, hh * 512:(hh + 1) * 512], c == 0, c == 3,
                     [B("ogT"), B("w_out")], [PSB[5 + hh]])
        ys = fbuf[0]
        P.act(ys[0:64, 0:512], ps[5][0:64, :], AF.Copy, [PSB[5]], [FB[0]])
        dve(lambda e: e.tensor_copy(ys[0:64, 512:1024], ps[6][0:64, :]), [PSB[6]], [FB[0]])
        P.dma("sp", ypart_s[layer][:, :], ys[0:64, :], [FB[0]], [B("ypart_s%d" % layer)], "yst0")

    def s_groupnorm_gate(o_ap, o_bufs, gate_ap, gate_bufs):
        st = lnst[0]
        kst_ = B("lnst0")
        dve(lambda e: e.reduce_sum(st[0:64, 0:1], o_ap, AX.X), o_bufs, [kst_])
        dve(lambda e: e.tensor_scalar(st[0:64, 1:2], st[0:64, 0:1], -1.0 / 512, None, ALU.mult), [kst_], [kst_])
        P.act(otmp[0:64, :], o_ap, AF.Identity, o_bufs + [kst_], [B("otmp0"), B("otmp1")], bias=st[0:64, 1:2])
        P.act(fbuf[3][0:64, 0:512], otmp[0:64, :], AF.Square, [B("otmp0"), B("otmp1")], [FB[3]])
        dve(lambda e: e.reduce_sum(st[0:64, 2:3], fbuf[3][0:64, 0:512], AX.X), [FB[3]], [B("lnsq0")])
        dve(lambda e: e.tensor_scalar(st[0:64, 3:4], st[0:64, 2:3], 1.0 / 512, LN_EPS, ALU.mult, ALU.add), [B("lnsq0")], [B("lnr0")])
        P.act(st[0:64, 5:6], st[0:64, 3:4], AF.Sqrt, [B("lnr0")], [B("lnr0")])
        dve(lambda e: e.reciprocal(st[0:64, 4:5], st[0:64, 5:6]), [B("lnr0")], [B("lnr0")])
        dve(lambda e: e.scalar_tensor_tensor(og[0:64, :], otmp[0:64, :], st[0:64, 4:5], gate_ap, ALU.mult, ALU.mult),
            [B("otmp0"), B("otmp1"), B("lnr0")] + gate_bufs, [B("og")])

    def gelu_to(ps_ap, n, bias_col, out_ap, psbuf, outbuf):
        g0, g1, g2 = fbuf[3][:, 0:n], fbuf[3][:, 512:512 + n], otmp[:, 0:n]
        k0, k1, k2 = FB[3], FB[3], B("otmp0")
        P.act(g0, ps_ap, AF.Identity, [psbuf, B("hb")], [k0], bias=bias_col)
        dve(lambda e: e.tensor_tensor(g1, g0, g0, ALU.mult), [k0], [k1])
        dve(lambda e: e.tensor_scalar(g1, g1, 0.044715, 1.0, ALU.mult, ALU.add), [k1], [k1])
        dve(lambda e: e.tensor_tensor(g1, g1, g0, ALU.mult), [k1, k0], [k1])
        P.act(g2, g1, AF.Sigmoid, [k1], [k2, B("otmp1")], scale=1.5957691216)
        dve(lambda e: e.tensor_tensor(out_ap, g2, g0, ALU.mult), [k2, k0], [outbuf])

    sc3 = fbuf[2][:, :].rearrange("p (h q) -> p h q", h=8)
    pw3 = qbf[:, :].rearrange("p (h q) -> p h q", h=8)
    KSC = B("fbuf2")

    def l_scores(X3, npos, qL3, xbufs, qbufs):
        ch = min(npos, 16)
        for h in range(8):
            for c0 in range(0, npos, ch):
                tmp = otmp[:, :].rearrange("p (a d) -> p a d", d=64)[:, 0:ch, :] if ch <= 8 else fbuf[3][:, :].rearrange("p (a d) -> p a d", d=64)[:, 0:ch, :]
                kt = B("otmp0") if ch <= 8 else FB[3]
                dve(lambda e, tmp=tmp, h=h, c0=c0: e.tensor_tensor(tmp, X3[:, c0:c0 + ch, :], bcast(qL3[:, h, :], 1, ch), ALU.mult), xbufs + qbufs, [kt])
                dve(lambda e, tmp=tmp, h=h, c0=c0: e.reduce_sum(sc3[:, h, c0:c0 + ch], tmp, AX.X), [kt], [KSC])

    def l_norm(npos, gcol0, LA, kLA, self_k_col):
        dve(lambda e: e.reduce_sum(zp[:, :], sc3[:, :, 0:npos], AX.X), [KSC], [B("zp")])
        P.mm(ps[7][:, 0:8], ddsb[:, :], zp[:, :], True, True, [B("ddsb"), B("zp")], [PSB[7]])
        if self_k_col is not None:
            qL3 = LA[:, 0:512].rearrange("p (h d) -> p h d", h=8)
            tmp = otmp[:, :].rearrange("p (h d) -> p h d", h=8)
            dve(lambda e: e.tensor_tensor(tmp, qL3, bcast(LA[:, self_k_col:self_k_col + 64], 1, 8), ALU.mult), [kLA], [B("otmp0"), B("otmp1")])
            dve(lambda e: e.reduce_sum(ssm[:, :], tmp, AX.X), [B("otmp0"), B("otmp1")], [B("ssm")])
            P.act(pself[:, :], ssm[:, :], AF.Exp, [B("ssm")], [B("pself")], scale=0.125)
            dve(lambda e: e.tensor_tensor(rZs[:, :], ps[7][:, 0:8], pself[:, :], ALU.add), [PSB[7], B("pself")], [B("rZs")])
        else:
            dve(lambda e: e.tensor_scalar(rZs[:, :], ps[7][:, 0:8], 1.0e-30, None, ALU.max), [PSB[7]], [B("rZs")])
        dve(lambda e: e.reciprocal(rZs[:, :], rZs[:, :]), [B("rZs")], [B("rZs")])
        dve(lambda e: e.tensor_tensor(Ws[:, :], rZs[:, :], LA[:, 512 + gcol0:512 + gcol0 + 8], ALU.mult), [B("rZs"), kLA], [B("Ws")])

    def l_pv(npos, v_of_pos, vbufs, first_open, last_close):
        if first_open:
            P.mm(ps[5][0:64, 0:64], zeros[0:32, 0:64], zeros[0:32, 0:64], True, False, [B("zeros")], [PSB[5]])
        nch = (npos + 7) // 8
        for pc in range(nch):
            slot = pc % 2
            Pe = QT[:, slot * 512:(slot + 1) * 512].rearrange("p (q s h) -> p q s h", q=8, s=8)
            kq = B("QTs%d" % slot)
            for s_ in range(8):
                dve(lambda e, Pe=Pe, s_=s_, pc=pc: e.tensor_scalar(Pe[:, :, s_, :], pw3[:, :, pc * 8:(pc + 1) * 8].rearrange("p h q -> p q h"), d8[:, s_:s_ + 1], None, ALU.mult),
                    [B("qbf"), B("d8")], [kq, B("QT")])
            for r in range(8):
                pos = pc * 8 + r
                P.mm(ps[5][0:64, 0:64], QT[:, slot * 512 + r * 64:slot * 512 + (r + 1) * 64], v_of_pos(pos), False,
                     last_close and pc == nch - 1 and r == 7, [kq] + vbufs, [PSB[5]])

    def nsa_sample(li, layer):
        PTc2 = PTc[:, :, :].rearrange("p a b -> p (a b)")
        for c in range(4):
            load_bf16(PTc2[:, 1024 * c:1024 * (c + 1)], nsa_w1s[li][:, 1024 * c:1024 * (c + 1)], 128, 1024, PTckeys)
        load_xsT(layer)
        s_proj(0, 512, 0)
        s_proj(512, 384, 1)
        s_proj(896, 24, 2)
        s_proj(920, 512, 3)
        SQ = fbuf[0]
        P.act(SQ[0:64, 0:512], ps[0][0:64, :], AF.Copy, [PSB[0]], [FB[0]])
        P.act(SQ[0:64, 512:536], ps[2][0:64, 0:24], AF.Sigmoid, [PSB[2]], [FB[0]])
        P.act(SQ[0:64, 536:920], ps[1][0:64, 0:384], AF.Copy, [PSB[1]], [FB[0]])
        P.act(szs[0:64, :], ps[3][0:64, :], AF.Silu, [PSB[3]], [B("szs")])
        P.dma("sp", rp2048[0:64, :], bass.AP(rope2048_d.tensor, 0, [[0, 64], [1, 16]]), [], [B("rp2048")], "rp2048")
        q3 = SQ[0:64, 0:512].rearrange("p (a b) -> p a b", a=8)
        rope_apply(q3[:, :, 0:8], q3[:, :, 8:16], 8, 0, ropetmp[0], "rt0", [FB[0], B("rp2048")], [FB[0]],
                   cs_ap=rp2048[0:64, 0:8], sn_ap=rp2048[0:64, 8:16], npart=64)
        k3 = SQ[0:64, 536:920].rearrange("p (a b) -> p a b", a=3)
        rope_apply(k3[:, :, 0:8], k3[:, :, 8:16], 3, 0, ropetmp[1], "rt1", [FB[0], B("rp2048")], [FB[0]],
                   cs_ap=rp2048[0:64, 0:8], sn_ap=rp2048[0:64, 8:16], npart=64)
        P.dma("sp", kvs_out[li][:, :], SQ[0:64, 536:920], [FB[0]], [B("kvs_out%d" % li)], "kvo0")
        for j in range(2):
            src = bass.AP(cwin[li][j].tensor, 64, [[32768, 64], [4672, 7], [1, 4672]])
            dst = bass.AP(wins_out[li][j].tensor, 0, [[32768, 64], [4672, 7], [1, 4672]])
            P.dma("sp", dst, src, [], [B("wins_%d_%d" % (li, j))], "wino%d" % j)
            P.dma("sp", wins_out[li][j][:, 511 * 64:512 * 64], SQ[0:64, 792 + 64 * j:856 + 64 * j], [FB[0]], [B("winsn_%d_%d" % (li, j))], "winn%d" % j)
        P.dma("sp", sq_d[:, :], SQ[0:64, 0:920], [FB[0]], [B("sq_d")], "sqd")

        def load_LA(g8, sl):
            LA = fbuf[sl]
            for s_ in range(8):
                P.dma("sp", LA[16 * s_:16 * s_ + 16, 0:920], bass.AP(sq_d.tensor, (8 * g8 + s_) * 920, [[0, 16], [1, 920]]),
                      [B("sq_d")], [FB[sl]], "fbufld%d" % sl)
            return LA, FB[sl]

        def gather(pool_ap, g8):
            P.add("pool", lambda e: e.indirect_dma_start(out=GA[:, :], out_offset=None, in_=pool_ap,
                                                         in_offset=bass.IndirectOffsetOnAxis(ap=ptL[:, g8:g8 + 1], axis=0)),
                  [B("ptL")], KTkeys, kind="d", semkey="gath")

        for g8 in range(8):
            LA, kLA = load_LA(g8, g8 % 2)
            qL3 = LA[:, 0:512].rearrange("p (h d) -> p h d", h=8)
            for kv_i in range(2):
                gather(pools[li][kv_i][:, :], g8)
                for q4 in range(16):
                    pb = ps[q4 % 2]
                    for r in range(4):
                        p_ = 4 * q4 + r
                        P.tr(pb[:, r * 128:(r + 1) * 128], GA[:, p_ * 128:(p_ + 1) * 128], ident_f[:, :], KTkeys + [B("ident_f")], [PSB[q4 % 2]])
                    dst = (XTlo if q4 < 8 else XThi)[:, (4 * q4) % 32:(4 * q4) % 32 + 4, :]
                    P.act(dst, pb[:, :].rearrange("p (a b) -> p a b", a=4), AF.Copy, [PSB[q4 % 2]], Vskeys if q4 < 8 else Vwkeys)
                xb = Vskeys + Vwkeys + PTckeys
                for ip in range(8):
                    n_ = 127 if ip == 7 else 128
                    for c in range(16):
                        p_ = 8 * ip + c
                        if p_ < 64:
                            rhs = (XTlo if p_ < 32 else XThi)[:, p_ % 32, 0:n_]
                        else:
                            rhs = XTlo[:, p_ - 64, 1:128]
                        P.mm(ps[2 + ip // 4][:, (ip % 4) * 128:(ip % 4) * 128 + n_], W1s[:, kv_i, c, :], rhs, c == 0, c == 15, xb, [PSB[2 + ip // 4]])
                gelu_to(ps[2][:, 0:512], 512, hb[:, kv_i:kv_i + 1], hsb[0][:, 0:512], PSB[2], B("hsb0"))
                gelu_to(ps[3][:, 0:511], 511, hb[:, kv_i:kv_i + 1], hsb[1][:, 0:511], PSB[3], B("hsb1"))
                for ip in range(8):
                    P.mm(ps[4][:, ip * 64:(ip + 1) * 64], hsb[ip // 4][:, (ip % 4) * 128:(ip % 4 + 1) * 128],
                         w2sb[:, 0:64] if kv_i == 0 else w2sb[:, 128:192], True, True, [B("hsb%d" % (ip // 4)), B("w2sb")], [PSB[4]])
                if kv_i == 0:
                    P.act(qf[:, :], ps[4][:, :], AF.Copy, [PSB[4]], [B("qf")])
                else:
                    P.act(vbf[:, :], ps[4][:, :], AF.Copy, [PSB[4]], [B("vbf")])
            l_scores(qf[:, :].rearrange("p (a d) -> p a d", d=64), 8, qL3, [B("qf")], [kLA])
            dve(lambda e: e.tensor_tensor(sc3[:, :, 0:8], sc3[:, :, 0:8], bcast(cmaskL[:, :], 1, 8), ALU.add), [KSC, B("cmaskL")], [KSC])
            P.act(sc3[:, :, 0:8], sc3[:, :, 0:8], AF.Exp, [KSC], [KSC], scale=0.125)
            l_norm(8, 0, LA, kLA, None)
            pn3 = pnL[:, :].rearrange("p (h q) -> p h q", h=8)
            dve(lambda e: e.tensor_tensor(pn3, sc3[:, :, 0:8], bcast(rZs[:, :], 2, 8), ALU.mult), [KSC, B("rZs")], [B("pnL")])
            dve(lambda e: e.reduce_sum(pcs[:, :], pnL[:, :].rearrange("p (h q) -> p q h", h=8), AX.X), [B("pnL")], [B("pcs")])
            PCe = PT[0][:, :].rearrange("p (q s) -> p q s", q=8)
            pool(lambda e: e.memset(PT[0][:, :], 0.0), [], [B("PT0")])
            for ip in range(8):
                dve(lambda e, ip=ip, g8=g8: e.tensor_scalar(PCe[:, ip, 8 * g8:8 * g8 + 8], d8[:, :], pcs[:, ip:ip + 1], None, ALU.mult), [B("pcs"), B("d8")], [B("PT0")])
            for ip in range(8):
                P.mm(ps[6][0:64, 0:33], PCe[:, ip, :], ovt[:, ip, :], g8 == 0 and ip == 0, g8 == 7 and ip == 7, [B("PT0"), B("ovt")], [PSB[6]])
            dve(lambda e: e.tensor_tensor(pw3[:, :, 0:8], sc3[:, :, 0:8], bcast(Ws[:, :], 2, 8), ALU.mult), [KSC, B("Ws")], [B("qbf")])
            l_pv(8, lambda pos: vbf[:, pos * 64:(pos + 1) * 64], [B("vbf")], True, True)
            dve(lambda e, g8=g8: e.tensor_copy(oacc[0:64, g8 * 64:(g8 + 1) * 64], ps[5][0:64, 0:64]), [PSB[5]], [B("oacc0"), B("oacc1")])

        dve(lambda e: e.tensor_tensor(score[0:64, 0:33], ps[6][0:64, 0:33], tbs[0:64, :], ALU.add), [PSB[6], B("tbs")], [B("score")])
        dve(lambda e: e.max(m8[0:64, 0:8], score[0:64, 0:33]), [B("score")], [B("m8a")])
        dve(lambda e: e.match_replace(score2[0:64, 0:33], m8[0:64, 0:8], score[0:64, 0:33], -1.0e30), [B("score"), B("m8a")], [B("score2")])
        dve(lambda e: e.max(m8[0:64, 8:16], score2[0:64, 0:33]), [B("score2")], [B("m8b")])
        dve(lambda e: e.tensor_scalar(imp[0:64, 0:32], score[0:64, 0:32], m8[0:64, 15:16], NEGB, ALU.is_lt, ALU.mult), [B("score"), B("m8b")], [B("imp")])
        P.dma("sp", selb_d[:, :], imp[0:64, 0:32], [B("imp")], [B("selb_d")], "selbd")

        for g8 in range(8):
            LA, kLA = load_LA(g8, g8 % 2)
            qL3 = LA[:, 0:512].rearrange("p (h d) -> p h d", h=8)
            P.dma("sp", selbL[:, :], bass.AP(selb_d.tensor, 256 * g8, [[2, 128], [1, 2]]), [B("selb_d")], [B("selbL")], "selbl")
            gather(pools[li][2][:, :], g8)
            l_scores(GA[:, :].rearrange("p (a d) -> p a d", d=64), 128, qL3, KTkeys, [kLA])
            dve(lambda e: e.tensor_scalar(sc3[:, :, 0:64], sc3[:, :, 0:64], selbL[:, 0:1], None, ALU.add), [KSC, B("selbL")], [KSC])
            dve(lambda e: e.tensor_scalar(sc3[:, :, 64:128], sc3[:, :, 64:128], selbL[:, 1:2], None, ALU.add), [KSC, B("selbL")], [KSC])
            P.act(sc3[:, :, :], sc3[:, :, :], AF.Exp, [KSC], [KSC], scale=0.125)
            l_norm(128, 8, LA, kLA, 664)
            dve(lambda e: e.tensor_tensor(pw3[:, :, :], sc3[:, :, :], bcast(Ws[:, :], 2, 128), ALU.mult), [KSC, B("Ws")], [B("qbf")])
            gather(pools[li][3][:, :], g8)
            P.act(XTlo[:, :, :].rearrange("p a b -> p (a b)"), GA[:, 0:4096], AF.Copy, KTkeys, Vskeys)
            P.act(XThi[:, :, :].rearrange("p a b -> p (a b)"), GA[:, 4096:8192], AF.Copy, KTkeys, Vwkeys)
            XL2 = XTlo[:, :, :].rearrange("p a b -> p (a b)")
            XH2 = XThi[:, :, :].rearrange("p a b -> p (a b)")
            l_pv(128, lambda pos: (XL2 if pos < 64 else XH2)[:, (pos % 64) * 64:(pos % 64 + 1) * 64], Vskeys + Vwkeys, True, False)
            self_pv(LA, kLA, 728)
            P.dma("sp", GA[:, 0:2048], bass.AP(cwin[li][0].tensor, 8 * g8 * 32768, [[2048, 128], [1, 2048]]), [], KTkeys, "gathw")
            l_scores(GA[:, 0:2048].rearrange("p (a d) -> p a d", d=64), 32, qL3, KTkeys, [kLA])
            P.act(sc3[:, :, 0:32], sc3[:, :, 0:32], AF.Exp, [KSC], [KSC], scale=0.125)
            l_norm(32, 16, LA, kLA, 792)
            dve(lambda e: e.tensor_tensor(pw3[:, :, 0:32], sc3[:, :, 0:32], bcast(Ws[:, :], 2, 32), ALU.mult), [KSC, B("Ws")], [B("qbf")])
            P.dma("sp", GA[:, 0:2048], bass.AP(cwin[li][1].tensor, 8 * g8 * 32768, [[2048, 128], [1, 2048]]), [], KTkeys, "gathw")
            P.act(XL2[:, 0:2048], GA[:, 0:2048], AF.Copy, KTkeys, Vskeys)
            l_pv(32, lambda pos: XL2[:, pos * 64:(pos + 1) * 64], Vskeys, False, False)
            self_pv(LA, kLA, 856, close=True)
            osb = ssm64
            dve(lambda e, g8=g8: e.tensor_tensor(osb[0:64, :], ps[5][0:64, 0:64], oacc[0:64, g8 * 64:(g8 + 1) * 64], ALU.add), [PSB[5], B("oacc0"), B("oacc1")], [B("ssm64")])
            P.dma("sp", so_d[64 * g8:64 * g8 + 64, :], osb[0:64, :], [B("ssm64")], [B("so_d_%d" % g8)], "sod")
        P.dma("sp", fbuf[1][0:64, 0:512], bass.AP(so_d.tensor, 0, [[512, 64], [1, 512]]), [B("so_d_%d" % g) for g in range(8)], [FB[1]], "fbufld1")
        dve(lambda e: e.tensor_tensor(og[0:64, :], fbuf[1][0:64, 0:512], szs[0:64, :], ALU.mult), [FB[1], B("szs")], [B("og")])
        s_out_tail(layer)

    def self_pv(LA, kLA, vcol, close=False):
        dve(lambda e: e.tensor_tensor(wself[:, :], pself[:, :], Ws[:, :], ALU.mult), [B("pself"), B("Ws")], [B("wself")])
        for s_ in range(8):
            dve(lambda e, s_=s_: e.tensor_scalar(Wse[:, s_ * 8:(s_ + 1) * 8], wself[:, :], d0[:, s_:s_ + 1], None, ALU.mult), [B("wself"), B("d0")], [B("Wse")])
        P.act(vsn[:, :], LA[:, vcol:vcol + 64], AF.Copy, [kLA], [B("vsn")])
        P.mm(ps[5][0:64, 0:64], Wse[:, :], vsn[:, :], False, close, [B("Wse"), B("vsn")], [PSB[5]])


    def ret_sample(li, layer):
        load_xsT(layer)
        s_proj(0, 512, 0)
        s_proj(512, 512, 1)
        s_proj(1024, 512, 2)
        P.act(vbf[0:64, :], ps[1][0:64, :], AF.Copy, [PSB[1]], [B("vbf")])
        P.act(szs[0:64, :], ps[2][0:64, :], AF.Silu, [PSB[2]], [B("szs")])
        pool(lambda e: e.tensor_tensor(qf[0:64, :], szs[0:64, :], gng[0:64, :], ALU.mult), [B("szs"), B("gng")], [B("qf")])
        P.dma("sp", xcs[1][0:64, :, :].rearrange("p a b -> p (a b)"), bass.AP(xp2048_d.tensor, 0, [[0, 64], [1, 256]]), [], [B("xcs1")], "xcsa1")
        pv4 = ps[0][0:64, :].rearrange("p (a b t) -> p a b t", a=2, b=2)
        E, O_ = pv4[:, :, 0, :], pv4[:, :, 1, :]
        cosb = bcast(xcs[1][0:64, 0, :], 1, 2)
        sinb = bcast(xcs[1][0:64, 1, :], 1, 2)
        rv = [rtmp[0:64, a, :].rearrange("p (a t) -> p a t", a=2) for a in range(4)]
        rb = [PSB[0], B("xcs1")]
        dve(lambda e: e.tensor_tensor(rv[0], E, cosb, ALU.mult), rb, [B("rtmp0")])
        dve(lambda e: e.tensor_tensor(rv[1], O_, sinb, ALU.mult), rb, [B("rtmp1")])
        dve(lambda e: e.tensor_tensor(rv[2], E, sinb, ALU.mult), rb, [B("rtmp2")])
        dve(lambda e: e.tensor_tensor(rv[3], O_, cosb, ALU.mult), rb, [B("rtmp3")])
        qk = fbuf[1]
        qk4 = qk[0:64, 0:512].rearrange("p (a b t) -> p a b t", a=2, b=2)
        dve(lambda e: e.tensor_tensor(qk4[:, :, 0, :], rv[0], rv[1], ALU.subtract), [B("rtmp0"), B("rtmp1")], [FB[1]])
        dve(lambda e: e.tensor_tensor(qk4[:, :, 1, :], rv[2], rv[3], ALU.add), [B("rtmp2"), B("rtmp3"), FB[1]], [FB[1]])
        dve(lambda e: e.tensor_scalar(qk[0:64, 256:512], qk[0:64, 256:512], 0.0625, None, ALU.mult), [FB[1]], [FB[1]])
        pf = ps[0]
        for ch in range(2):
            P.tr(pf[:, ch * 64:(ch + 1) * 64], qk[0:64, ch * 128:(ch + 1) * 128], ident_f[0:64, 0:64], [FB[1], B("ident_f")], [PSB[0]])
        dve(lambda e: e.tensor_copy(QKr[:, 0:2, 0:64], pf[:, 0:128].rearrange("p (a t) -> p a t", a=2)), [PSB[0]], [B("QKr_e"), B("QKr_o")])
        for s_ in range(64):
            sl = s_ % 2
            Sb = fbuf[2 + sl]
            kS = B("fbuf%d" % (2 + sl))
            P.dma("sp", Sb[:, :].rearrange("p (c e) -> p c e", c=2), sret_d[li][s_, :, :].rearrange("(c p) e -> p c e", c=2), [], [kS], "fbufld%d" % (2 + sl))
            km = PT[1 + sl]
            kmk = B("PT%d" % (1 + sl))
            dve(lambda e, km=km, s_=s_: e.tensor_scalar(km[0:64, 0:256], qk[0:64, 256:512], ident_f[0:64, s_:s_ + 1], None, ALU.mult), [FB[1], B("ident_f")], [kmk])
            for ch in range(2):
                P.mm(ps[5 + ch][:, :], km[0:64, ch * 128:(ch + 1) * 128], vbf[0:64, :], True, True, [kmk, B("vbf")], [PSB[5 + ch]])
                dve(lambda e, Sb=Sb, ch=ch: e.scalar_tensor_tensor(Sb[:, ch * 512:(ch + 1) * 512], Sb[:, ch * 512:(ch + 1) * 512], kcdec[:, 2:3], ps[5 + ch][:, :], ALU.mult, ALU.add),
                    [PSB[5 + ch], kS, B("kcdec")], [kS])
            P.dma("sp", rets_out[li][s_, :, :].rearrange("(c p) e -> p c e", c=2), Sb[:, :].rearrange("p (c e) -> p c e", c=2), [kS], [B("rets_%d_%d" % (li, s_))], "reto%d" % sl)
            P.act(Sbf[:, :, :].rearrange("p c e -> p (c e)"), Sb[:, :], AF.Copy, [kS], [B("Sbf")])
            po = ps[3 + sl]
            for ch in range(2):
                P.mm(po[0:64, :], QKr[:, ch, 0:64], Sbf[:, ch, :], ch == 0, ch == 1, [B("QKr_e"), B("QKr_o"), B("Sbf")], [PSB[3 + sl]])
            if s_ == 0:
                dve(lambda e, po=po, s_=s_: e.tensor_scalar(oacc[0:64, :], po[0:64, :], ident_f[0:64, s_:s_ + 1], None, ALU.mult), [PSB[3 + sl], B("ident_f")], [B("oacc0"), B("oacc1")])
            else:
                dve(lambda e, po=po, s_=s_: e.scalar_tensor_tensor(oacc[0:64, :], po[0:64, :], ident_f[0:64, s_:s_ + 1], oacc[0:64, :], ALU.mult, ALU.add),
                    [PSB[3 + sl], B("ident_f"), B("oacc0"), B("oacc1")], [B("oacc0"), B("oacc1")])
        s_groupnorm_gate(oacc[0:64, :], [B("oacc0"), B("oacc1")], qf[0:64, :], [B("qf")])
        s_out_tail(layer)

    def ln_sample(layer, last):
        P.add("pool", lambda e: e.collective_compute("AllReduce", ALU.add, replica_groups=[[0, 1, 2, 3], [4, 5, 6, 7]],
                                                     ins=[ypart_s[layer].ap().opt()], outs=[ysum_s[layer].ap().opt()]),
              [B("ypart_s%d" % layer)], [B("ysum_s%d" % layer)], kind="cc", semkey="ccs%d" % layer)
        X, Y, st = fbuf[0], fbuf[2], lnst[0]
        kx, ky, kst_ = FB[0], FB[2], B("lnst0")
        src = xs_d if layer == 0 else xs_cur
        P.dma("sp", X[0:64, :], src[:, :], [B("xs_cur")], [kx], "fbufld0")
        P.dma("sp", Y[0:64, :], ysum_s[layer][:, :], [B("ysum_s%d" % layer)], [ky], "fbufld2")
        dve(lambda e: e.scalar_tensor_tensor(X[0:64, :], X[0:64, :], ALPHA, Y[0:64, :], ALU.mult, ALU.add), [kx, ky], [kx])
        dve(lambda e: e.reduce_sum(st[0:64, 0:1], X[0:64, :], AX.X), [kx], [kst_])
        dve(lambda e: e.tensor_scalar(st[0:64, 1:2], st[0:64, 0:1], -1.0 / D, None, ALU.mult), [kst_], [kst_])
        P.act(Y[0:64, :], X[0:64, :], AF.Identity, [kx, kst_], [ky], bias=st[0:64, 1:2])
        P.act(X[0:64, :], Y[0:64, :], AF.Square, [ky], [kx])
        dve(lambda e: e.reduce_sum(st[0:64, 2:3], X[0:64, :], AX.X), [kx], [B("lnsq0")])
        dve(lambda e: e.tensor_scalar(st[0:64, 3:4], st[0:64, 2:3], 1.0 / D, LN_EPS, ALU.mult, ALU.add), [B("lnsq0")], [B("lnr0")])
        P.act(st[0:64, 5:6], st[0:64, 3:4], AF.Sqrt, [B("lnr0")], [B("lnr0")])
        dve(lambda e: e.reciprocal(st[0:64, 4:5], st[0:64, 5:6]), [B("lnr0")], [B("lnr0")])
        dve(lambda e: e.scalar_tensor_tensor(X[0:64, :], Y[0:64, :], st[0:64, 4:5], lng[0:64, 0:D], ALU.mult, ALU.mult),
            [ky, B("lnr0"), B("stg0"), B("lnsq0")], [kx])
        pool(lambda e: e.tensor_tensor(X[0:64, :], X[0:64, :], lnb[0:64, 0:D], ALU.add), [kx, B("stg1")], [kx])
        dst = ys_out if last else xs_cur
        P.dma("sp", dst[:, :], X[0:64, :], [kx], [B("xs_cur")], "lnst0")


    for layer in range(NL):
        if layer % 2 == 0:
            nsa_layer(layer // 2, layer)
        else:
            ret_layer(layer // 2, layer)
        if os.environ.get("KNOLN"):
            continue
        ln_pass(layer, layer == NL - 1)
        if SAMPLE:
            ln_sample(layer, layer == NL - 1)

    P.emit(es)
    es.close()
    return nc, P


_CACHE = {}


def _prep_core(c, inp):
    b, k = c // 4, c % 4
    m = {}
    m["xp"] = np.ascontiguousarray(inp["x_prompt"][b])
    m["rope"] = _CACHE["rope"]
    for kk, v in _CACHE["consts"].items():
        m[kk] = v
    m["ln_g"] = np.ascontiguousarray(inp["ln_g"])
    m["ln_b"] = np.ascontiguousarray(inp["ln_b"])
    for l in range(2):
        w = inp["nsa_w_in"][l]
        cols = [w[:, 512 * k:512 * (k + 1)]]
        for j in range(6):
            o = 2048 + 256 * j + 64 * k
            cols.append(w[:, o:o + 64])
        for br in range(3):
            o = 2048 + 1536 + 32 * br + 8 * k
            cols.append(w[:, o:o + 8])
        o = 2048 + 1536 + 96 + 512 * k
        cols.append(w[:, o:o + 512])
        m["nsa_win%d" % l] = np.ascontiguousarray(np.concatenate(cols, axis=1))
        m["nsa_wout%d" % l] = np.ascontiguousarray(inp["nsa_w_out"][l][512 * k:512 * (k + 1), :])
        w1k = inp["nsa_w1_k"][l].reshape(32, 64, 128).transpose(1, 0, 2).reshape(64, 4096)
        w1v = inp["nsa_w1_v"][l].reshape(32, 64, 128).transpose(1, 0, 2).reshape(64, 4096)
        m["nsa_w1_%d" % l] = np.ascontiguousarray(np.concatenate([w1k, w1v], 0))
        m["nsa_pe%d" % l] = np.ascontiguousarray(np.concatenate([inp["nsa_pe_k"][l].T, inp["nsa_pe_v"][l].T], 0))
        m["nsa_w2_%d" % l] = np.ascontiguousarray(np.concatenate([inp["nsa_w2_k"][l], inp["nsa_w2_k"][l], inp["nsa_w2_v"][l]], 1))
        rw = inp["ret_w_in"][l]
        qc = rw[:, 256 * k:256 * (k + 1)]
        kc_ = rw[:, 1024 + 256 * k:1024 + 256 * (k + 1)]
        m["ret_win%d" % l] = np.ascontiguousarray(np.concatenate(
            [qc[:, 0::2], qc[:, 1::2], kc_[:, 0::2], kc_[:, 1::2],
             rw[:, 2048 + 512 * k:2048 + 512 * (k + 1)], rw[:, 4096 + 512 * k:4096 + 512 * (k + 1)]], 1))
        m["ret_wout%d" % l] = np.ascontiguousarray(inp["ret_w_out"][l][512 * k:512 * (k + 1), :])
        m["ret_gn%d" % l] = np.ascontiguousarray(inp["ret_gn_g"][l][512 * k:512 * (k + 1)][None, :])
    m["xcos"], m["xsin"] = _CACHE["xpos"]
    m["xp2048"] = np.ascontiguousarray(np.concatenate([m["xcos"][:, 2048], m["xsin"][:, 2048]])[None, :])
    g = c // 4
    m["xs"] = np.ascontiguousarray(inp["x_sample"][64 * g:64 * g + 64, 0, :])
    m["ptab"] = np.ascontiguousarray(inp["page_table"][64 * g:64 * g + 64].reshape(8, 128).astype(np.int32))
    for kk, v in _CACHE["stab"].items():
        m[kk] = v
    m["rope2048"] = np.ascontiguousarray(_CACHE["rope"][0, 16, :][None, :])
    for l in range(2):
        for j, nm in enumerate(("cache_k_cmp", "cache_v_cmp", "cache_k_sel", "cache_v_sel")):
            m["pool%d_%d" % (l, j)] = _pool_slice(inp, nm, l, k)
        for j, nm in enumerate(("cache_k_win", "cache_v_win")):
            m["cwin%d_%d" % (l, j)] = np.ascontiguousarray(inp[nm][l, 64 * g:64 * g + 64, :, k, :]).reshape(64, 32768)
        w1n = [inp[nm][l].reshape(16, 128, 128).transpose(1, 0, 2).reshape(128, 2048) for nm in ("nsa_w1_k", "nsa_w1_v")]
        m["nsa_w1s%d" % l] = np.ascontiguousarray(np.concatenate(w1n, 1))
    for l in range(2):
        st = inp["state_ret"][l, 64 * g:64 * g + 64, k]
        m["sret%d" % l] = np.ascontiguousarray(np.concatenate([st[:, 0::2, :], st[:, 1::2, :]], 1))
    for kk, v in _CACHE["dec"][k].items():
        m[kk] = v
    return m


def kernel(**inp):
    stage = os.environ.get("KSTAGE", "full")
    inp = {k: np.asarray(v) for k, v in inp.items()}
    if "rope" not in _CACHE:
        _CACHE["rope"] = _rope_table()
        _CACHE["consts"] = _consts_bf16like()
        _CACHE["xpos"] = _xpos_tables()
        _CACHE["dec"] = [_decay_tables(h) for h in range(4)]
        _CACHE["stab"] = _sample_tables()
    _CACHE["pool"] = {}
    nc, P = build(stage)
    in_maps = [_prep_core(c, inp) for c in range(8)]
    res = run_bass_kernel_spmd(nc, in_maps, core_ids=list(range(8))).results
    B_, DEC = 2, 128
    y_prompt = np.zeros((B_, T, D), np.float32)
    y_sample = np.zeros((DEC, 1, D), np.float32)
    kv_p = [np.zeros((2, B_, T, 4, 64), np.float32) for _ in range(4)]
    kv_s = [np.zeros((2, DEC, 1, 4, 64), np.float32) for _ in range(4)]
    win_p = [np.zeros((2, B_, 512, 4, 64), np.float32) for _ in range(2)]
    win_s = [np.zeros((2, DEC, 512, 4, 64), np.float32) for _ in range(2)]
    ret_p = np.zeros((2, B_, 4, 256, 512), np.float32)
    ret_s = np.zeros((2, DEC, 4, 256, 512), np.float32)
    for c in range(8):
        b, k = c // 4, c % 4
        if k == 0:
            y_prompt[b] = res[c]["yout"]
            y_sample[64 * (c // 4):64 * (c // 4) + 64, 0, :] = res[c]["ys_out"]
        for l in range(2):
            kvo = res[c]["kvout%d" % l]
            for j in range(4):
                kv_p[j][l, b, :, k, :] = kvo[:, 64 * j:64 * (j + 1)]
            for j in range(2):
                win_p[j][l, b, :, k, :] = kvo[T - 512:, 256 + 64 * j:256 + 64 * (j + 1)]
            rs = res[c]["rets_out%d" % l]
            g = c // 4
            kvs = res[c]["kvs_out%d" % l]
            for j in range(4):
                kv_s[j][l, 64 * g:64 * g + 64, 0, k, :] = kvs[:, 64 * j:64 * (j + 1)]
            for j in range(2):
                win_s[j][l, 64 * g:64 * g + 64, :, k, :] = res[c]["wins_out%d_%d" % (l, j)].reshape(64, 512, 64)
            ret_s[l, 64 * g:64 * g + 64, k, 0::2, :] = rs[:, 0:128]
            ret_s[l, 64 * g:64 * g + 64, k, 1::2, :] = rs[:, 128:256]
            ro = res[c]["retout%d" % l]
            ret_p[l, b, k, 0::2, :] = ro[0:128]
            ret_p[l, b, k, 1::2, :] = ro[128:256]
    return (y_prompt, y_sample,
            kv_p[0], kv_s[0], kv_p[1], kv_s[1], kv_p[2], kv_s[2], kv_p[3], kv_s[3],
            win_p[0], win_s[0], win_p[1], win_s[1], ret_p, ret_s)
```

```python
import os
from contextlib import ExitStack

import numpy as np
import concourse.bass as bass
import concourse.mybir as mybir
from concourse.bass_utils import run_bass_kernel_spmd

F32 = mybir.dt.float32
BF16 = mybir.dt.bfloat16
I32 = mybir.dt.int32
ALU = mybir.AluOpType
AF = mybir.ActivationFunctionType
AX = mybir.AxisListType

T = 8192
NT = T // 128
D = 1024
KC = D // 128
NEGB = -30000.0
ALPHA = 8.0 ** 0.25
LN_EPS = 1e-5


class Buf:
    __slots__ = ("name", "lw", "rd_eng", "rd_dma")

    def __init__(self, name):
        self.name = name
        self.lw = None
        self.rd_eng = {}
        self.rd_dma = []


class Op:
    __slots__ = ("eng", "fn", "deps", "kind", "sem", "val", "need", "semkey")


class Prog:
    def __init__(self, nc):
        self.nc = nc
        self.ops = []
        self.bufs = {}

    def buf(self, name):
        b = self.bufs.get(name)
        if b is None:
            b = Buf(name)
            self.bufs[name] = b
        return b

    def add(self, eng, fn, r=(), w=(), kind="c", semkey=None):
        op = Op()
        op.eng, op.fn, op.kind, op.semkey = eng, fn, kind, semkey
        op.need = kind != "c"
        op.sem = None
        op.val = 0
        deps = {}
        for b in r:
            if b.lw is not None:
                deps[id(b.lw)] = b.lw
        for b in w:
            if b.lw is not None:
                deps[id(b.lw)] = b.lw
            for o in b.rd_eng.values():
                deps[id(o)] = o
            for o in b.rd_dma:
                deps[id(o)] = o
        for b in r:
            if kind == "c":
                b.rd_eng[eng] = op
            else:
                b.rd_dma.append(op)
        for b in w:
            b.lw = op
            b.rd_eng = {}
            b.rd_dma = []
        dl = []
        for d in deps.values():
            if d is op:
                continue
            if d.kind == "c" and kind == "c" and d.eng == "pe" and eng == "pe":
                continue
            d.need = True
            dl.append(d)
        op.deps = dl
        self.ops.append(op)
        return op

    def mm(self, out, lhsT, rhs, start, stop, r, w, **kw):
        return self.add("pe", lambda e: e.matmul(out, lhsT, rhs, start=start, stop=stop, **kw), r, w)

    def tr(self, out, in_, ident, r, w):
        return self.add("pe", lambda e: e.transpose(out, in_, ident), r, w)

    def act(self, out, in_, func, r, w, **kw):
        return self.add("act", lambda e: e.activation(out, in_, func, **kw), r, w)

    def dma(self, eng, out, in_, r, w, semkey):
        return self.add(eng, lambda e: e.dma_start(out=out, in_=in_), r, w, kind="d", semkey=semkey)

    def emit(self, es):
        nc = self.nc
        engsem = {}
        for e in ("pe", "act", "dve", "pool", "sp"):
            engsem[e] = es.enter_context(nc.semaphore("sem_" + e))
        dsem = {}
        cnt = {e: 0 for e in engsem}
        dcnt = {}
        for op in self.ops:
            if op.kind == "c":
                if op.need:
                    cnt[op.eng] += 1
                    op.sem = engsem[op.eng]
                    op.val = cnt[op.eng]
            else:
                k = op.semkey
                if k not in dsem:
                    dsem[k] = es.enter_context(nc.semaphore("dsem_%d" % len(dsem)))
                    dcnt[k] = 0
                dcnt[k] += 16 if op.kind == "d" else 1
                op.sem = dsem[k]
                op.val = dcnt[k]
        self.n_sems = len(dsem) + 5
        ops = self.ops
        final = [(dsem[k], dcnt[k]) for k in dsem]

        def run(name, is_last=False):
            def f(e):
                waited = {}
                for op in ops:
                    if op.eng != name:
                        continue
                    need = {}
                    for d in op.deps:
                        key = id(d.sem)
                        if need.get(key, (None, 0))[1] < d.val:
                            need[key] = (d.sem, d.val)
                    for key, (sem, val) in need.items():
                        if waited.get(key, 0) < val:
                            e.wait_ge(sem, val)
                            waited[key] = val
                    ins = op.fn(e)
                    if op.kind == "d":
                        ins.then_inc(op.sem, 16)
                    elif op.kind == "cc":
                        ins.then_inc(op.sem)
                    elif op.need:
                        ins.then_inc(op.sem, 1)
                if is_last:
                    for sem, val in final:
                        e.wait_ge(sem, val)
            return f

        with nc.Block() as block:
            block.tensor(run("pe"))
            block.scalar(run("act"))
            block.vector(run("dve"))
            block.gpsimd(run("pool"))
            block.sync(run("sp", True))


def bcast(ap, pos, n):
    a = [list(x) for x in ap.ap]
    a.insert(pos, [0, n])
    return bass.AP(ap.tensor, ap.offset, a)


def _rope_table():
    half = 8
    inv = (np.float32(500000.0) ** (-np.arange(half, dtype=np.float32) / np.float32(half))).astype(np.float32)
    pos = np.arange(T, dtype=np.float32)
    ang = (pos[:, None] * inv[None, :]).astype(np.float32)
    tab = np.concatenate([np.cos(ang), np.sin(ang)], -1).astype(np.float32)
    return np.ascontiguousarray(tab.reshape(NT, 128, 16).transpose(1, 0, 2))


def _consts_bf16like():
    c = {}
    c["ident"] = np.eye(128, dtype=np.float32)
    m = np.arange(128)[:, None]
    r = np.arange(128)[None, :]
    c["triu"] = np.where(m <= r, 0.0, NEGB).astype(np.float32)
    c["tril"] = np.where(m >= r, 0.0, NEGB).astype(np.float32)
    xx = np.arange(4096)[None, :]
    c["ew"] = ((np.arange(128)[:, None] % 64) == (2 * (xx // 128) + (xx % 128) // 64)).astype(np.float32)
    p = np.arange(32)[:, None]
    x = np.arange(272)[None, :]
    c["lw"] = ((p == np.clip(x - 126, 0, 10)) & (p <= 10)).astype(np.float32)
    rr = np.arange(128)[None, :]
    wb = np.where((p - 1) <= ((rr + 1) // 16), 0.0, NEGB)
    wb[11:] = 0.0
    c["wb"] = wb.astype(np.float32)
    cst = np.arange(512)[:, None] * 16
    sst = np.arange(128)[None, :] * 64
    ov = np.clip(np.minimum(cst + 32, sst + 64) - np.maximum(cst, sst), 0, None).astype(np.float32) / 32.0
    ov[511:] = 0.0
    c["ov"] = np.ascontiguousarray(ov.reshape(4, 128, 128).transpose(1, 0, 2))
    r_ = np.arange(128)[:, None]
    sp = np.arange(254)[None, :] - 126
    bq = (r_ >= 64).astype(np.int64)
    forced = (sp == bq) | (sp == bq - 1)
    c["tb"] = (1000.0 * forced + np.where(sp <= bq, 0.0, -1.0e30)).astype(np.float32)
    return c


def _pool_slice(inp, nm, l, k):
    key = (nm, l, k)
    if key not in _CACHE["pool"]:
        a = inp[nm][l][:, :, k, :]
        _CACHE["pool"][key] = np.ascontiguousarray(a).reshape(a.shape[0], 8192)
    return _CACHE["pool"][key]


def _sample_tables():
    t = {}
    p = np.arange(128)
    s8, j16 = p // 16, p % 16
    t["d8"] = (s8[:, None] == np.arange(8)[None, :]).astype(np.float32)
    t["d0"] = ((s8[:, None] == np.arange(8)[None, :]) & (j16[:, None] == 0)).astype(np.float32)
    t["dd"] = (s8[:, None] == s8[None, :]).astype(np.float32)
    cm = np.zeros((128, 8), np.float32)
    cm[j16 == 15, 7] = NEGB
    t["cmaskl"] = cm
    cst = (8 * j16[:, None] + np.arange(8)[None, :]) * 16
    sst = np.arange(33) * 64
    ov = np.clip(np.minimum(cst[:, :, None] + 32, sst[None, None, :] + 64) - np.maximum(cst[:, :, None], sst[None, None, :]), 0, None) / 32.0
    ov[j16 == 15, 7, :] = 0.0
    t["ovt"] = np.ascontiguousarray(ov.astype(np.float32).reshape(128, 264))
    jb = np.arange(33)
    forced = (jb == 0) | (jb == 32) | (jb == 31)
    t["tbs"] = np.ascontiguousarray(np.broadcast_to((1000.0 * forced).astype(np.float32)[None, :], (64, 33)))
    return t


def _xpos_tables():
    half = 128
    inv = (1.0 / (np.float32(10000.0) ** np.linspace(0.0, 1.0, half, dtype=np.float32))).astype(np.float32)
    pos = np.arange(T, dtype=np.float32)
    ang = (pos[None, :] * inv[:, None]).astype(np.float32)
    return np.ascontiguousarray(np.cos(ang).astype(np.float32)), np.ascontiguousarray(np.sin(ang).astype(np.float32))


def _decay_tables(h):
    lg = np.log1p(-np.float64(2.0) ** (-5.0 - h))
    i = np.arange(128, dtype=np.float64)
    diff = i[None, :] - i[:, None]
    dm = np.where(diff >= 0, np.exp(np.maximum(diff, 0.0) * lg), 0.0) / 16.0
    qd = np.broadcast_to(np.exp((i + 1.0) * lg)[None, :], (128, 128))
    kc = np.stack([np.exp((127.0 - i) * lg) / 16.0, np.full(128, np.exp(128.0 * lg)), np.full(128, np.exp(lg))], 1)
    return {"dmaskT": np.ascontiguousarray(dm.astype(np.float32)), "qdec": np.ascontiguousarray(qd.astype(np.float32)),
            "kcdec": np.ascontiguousarray(kc.astype(np.float32))}


def build(stage):
    nc = bass.Bass("TRN2", target_bir_lowering=False)
    P = Prog(nc)
    es = ExitStack()
    NL = int(os.environ.get("KNL", "4"))
    NQT = int(os.environ.get("KNQT", str(NT)))
    NT1 = int(os.environ.get("KNT1", str(NT)))

    def din(name, shape, dt=F32):
        return nc.dram_tensor(name, list(shape), dt, kind="ExternalInput").ap()

    def dout(name, shape, dt=F32):
        return nc.dram_tensor(name, list(shape), dt, kind="ExternalOutput").ap()

    def sb(name, shape, dt):
        return es.enter_context(nc.sbuf_tensor("s_" + name, list(shape), dt))

    xp = din("xp", [T, D])
    rope_d = din("rope", [128, NT, 16])
    ident_d = din("ident", [128, 128])
    triu_d = din("triu", [128, 128])
    tril_d = din("tril", [128, 128])
    ew_d = din("ew", [128, 4096])
    lw_d = din("lw", [32, 272])
    wb_d = din("wb", [32, 128])
    ov_d = din("ov", [128, 4, 128])
    tb_d = din("tb", [128, 254])
    lng_d = din("ln_g", [4, D])
    lnb_d = din("ln_b", [4, D])
    nsa_win = [din("nsa_win%d" % l, [D, 1432]) for l in range(2)]
    nsa_wout = [din("nsa_wout%d" % l, [512, D]) for l in range(2)]
    nsa_w1 = [din("nsa_w1_%d" % l, [128, 32 * 128]) for l in range(2)]
    nsa_pe = [din("nsa_pe%d" % l, [128, 32]) for l in range(2)]
    nsa_w2 = [din("nsa_w2_%d" % l, [128, 192]) for l in range(2)]
    ret_win = [din("ret_win%d" % l, [D, 1536]) for l in range(2)]
    ret_wout = [din("ret_wout%d" % l, [512, D]) for l in range(2)]
    ret_gn = [din("ret_gn%d" % l, [1, 512]) for l in range(2)]
    xcos_d = din("xcos", [128, T])
    xsin_d = din("xsin", [128, T])
    dmask_d = din("dmaskT", [128, 128])
    qdec_d = din("qdec", [128, 128])
    kcdec_d = din("kcdec", [128, 3])
    kvout = [dout("kvout%d" % l, [T, 384]) for l in range(2)]
    retout = [dout("retout%d" % l, [256, 512]) for l in range(2)]
    SAMPLE = int(os.environ.get("KSAMPLE", "1"))
    xs_d = din("xs", [64, D])
    sret_d = [din("sret%d" % l, [64, 256, 512]) for l in range(2)]
    xp2048_d = din("xp2048", [1, 256])
    ys_out = dout("ys_out", [64, D])
    rets_out = [dout("rets_out%d" % l, [64, 256, 512]) for l in range(2)]
    pt_d = din("ptab", [8, 128], I32)
    pools = [[din("pool%d_%d" % (l, j), [2560, 8192]) for j in range(4)] for l in range(2)]
    cwin = [[din("cwin%d_%d" % (l, j), [64, 32768]) for j in range(2)] for l in range(2)]
    nsa_w1s = [din("nsa_w1s%d" % l, [128, 4096]) for l in range(2)]
    d8_d = din("d8", [128, 8]); d0_d = din("d0", [128, 8]); dd_d = din("dd", [128, 128]); cmask_d = din("cmaskl", [128, 8])
    ovt_d = din("ovt", [128, 264]); tbs_d = din("tbs", [64, 33]); rope2048_d = din("rope2048", [1, 16])
    kvs_out = [dout("kvs_out%d" % l, [64, 384]) for l in range(2)]
    wins_out = [[dout("wins_out%d_%d" % (l, j), [64, 32768]) for j in range(2)] for l in range(2)]
    sq_d = nc.dram_tensor("sq_d", [64, 920], F32).ap()
    selb_d = nc.dram_tensor("selb_d", [64, 32], F32).ap()
    so_d = nc.dram_tensor("so_d", [512, 64], F32).ap()
    xs_cur = nc.dram_tensor("xs_cur", [64, D], F32).ap()
    ypart_s = [nc.dram_tensor("ypart_s%d" % l, [64, D], F32) for l in range(4)]
    ysum_s = [nc.dram_tensor("ysum_s%d" % l, [64, D], F32) for l in range(4)]
    yout = dout("yout", [T, D])
    xT = nc.dram_tensor("xT", [KC, 128, T], BF16).ap()
    xcur = nc.dram_tensor("xcur", [T, D], F32).ap()
    ypart = [[nc.dram_tensor("ypart%d_%d" % (l, c), [1024, D], F32) for c in range(8)] for l in range(4)]
    ysum = [[nc.dram_tensor("ysum%d_%d" % (l, c), [1024, D], F32) for c in range(8)] for l in range(4)]

    ident_f = sb("ident_f", [128, 128], F32)
    ident = sb("ident", [128, 128], BF16)
    rope = sb("rope_sb", [128, NT, 16], F32)
    fbuf = [sb("fbuf%d" % i, [128, D], F32) for i in range(4)]
    xin = fbuf[0:2]
    xbf = [sb("xbf%d" % i, [128, D], BF16) for i in range(2)]
    xTt = [sb("xTt%d" % i, [128, KC * 128], BF16) for i in range(2)]
    stg = [sb("stg%d" % i, [128, 1536], F32) for i in range(2)]
    w_nsa = sb("w_in", [128, KC, 1536], BF16)
    w_out = sb("w_out", [128, 4, D], BF16)
    kvsb = [sb("kvsb%d" % i, [128, 384], F32) for i in range(2)]
    ropetmp = [sb("ropetmp%d" % i, [128, 4, 64], F32) for i in range(2)]
    kstage = [sb("kstage%d" % i, [128, 2, 128], BF16) for i in range(2)]
    KT = sb("KT", [128, 2, T], BF16)
    Vs = sb("Vs", [128, NT, 65], BF16)
    Vw = sb("Vw", [128, NT, 65], BF16)
    Vc = sb("Vc", [128, 4, 65], BF16)
    KcT2 = sb("KcT2", [128, 512], BF16)
    w1sb = sb("w1sb", [128, 32, 128], BF16)
    pesb = sb("pesb", [128, 32], BF16)
    w2sb = sb("w2sb", [128, 192], BF16)
    hb = sb("hbias", [128, 2], F32)
    hsb = [sb("hsb%d" % i, [128, 512], BF16) for i in range(2)]
    gtmp = fbuf[0:3]
    ew = sb("ew", [128, 4096], BF16)
    lw = sb("lw", [32, 272], BF16)
    wb4 = sb("wb4", [32, 512], BF16)
    triu4 = sb("triu4", [128, 512], BF16)
    tril4 = sb("tril4", [128, 512], BF16)
    ov = sb("ov", [128, 4, 128], BF16)
    tb = sb("tb", [128, 254], F32)
    lng, lnb = stg[0], stg[1]
    xq = [sb("xq%d" % i, [128, KC, 128], BF16) for i in range(2)]
    qf = sb("qf", [128, 512], F32)
    qbf = sb("qbf", [128, 1024], BF16)
    QT = sb("QT", [128, 1024], BF16)
    gsb = sb("gsb", [128, 24], F32)
    szs = sb("szs", [128, 512], F32)
    PT = [sb("PT%d" % i, [128, 512], BF16) for i in range(3)]
    PTc = sb("PTc", [128, 8, 512], BF16)
    oacc = sb("oacc", [128, 512], F32)
    otmp = sb("otmp", [128, 512], F32)
    rz = sb("rz", [128, 3, 8], F32)
    wgt = sb("wgt", [128, 3, 8], F32)
    imp = sb("imp", [128, 128], F32)
    score = sb("score", [128, 128], F32)
    score2 = sb("score2", [128, 128], F32)
    m8 = sb("m8", [128, 16], F32)
    selb = sb("selb", [128, 128], BF16)
    selT4 = sb("selT4", [128, 512], BF16)
    og = sb("og", [128, 512], BF16)
    ogT = sb("ogT", [128, 512], BF16)
    ysb = fbuf[0:2]
    lnx = fbuf[0:2]
    lny = fbuf[2:4]
    lnst = [sb("lnst%d" % i, [128, 8], F32) for i in range(2)]
    zeros = sb("zeros", [128, 512], BF16)
    xcs = [sb("xcs%d" % i, [128, 2, 128], F32) for i in range(2)]
    dmaskT = sb("dmaskT", [128, 128], F32)
    qdec = sb("qdec", [128, 128], F32)
    kcdec = sb("kcdec", [128, 3], F32)
    xsT = sb("xsT", [128, 512], BF16)
    zp = sb("zp", [128, 8], F32); rZs = sb("rZs", [128, 8], F32); Ws = sb("Ws", [128, 8], F32); pcs = sb("pcs", [128, 8], F32)
    ssm = sb("ssm", [128, 8], F32); pself = sb("pself", [128, 8], F32); wself = sb("wself", [128, 8], F32)
    pnL = sb("pnL", [128, 64], F32); Wse = sb("Wse", [128, 64], BF16); vsn = sb("vsn", [128, 64], BF16)
    ssm64 = sb("ssm64", [128, 64], F32); selbL = sb("selbL", [128, 2], F32); ptL = sb("ptL", [128, 8], I32)
    d8 = sb("d8", [128, 8], F32); d0 = sb("d0", [128, 8], F32); cmaskL = sb("cmaskL", [128, 8], F32)
    ddsb = sb("ddsb", [128, 128], F32); ovt = sb("ovt", [128, 8, 33], BF16); tbs = sb("tbs", [64, 33], F32)
    rp2048 = sb("rp2048", [64, 16], F32)
    rtmp = sb("rtmp", [128, 4, 256], F32)
    QKr = sb("QKr", [128, 4, 128], BF16)
    qdT = sb("qdT", [128, 2, 128], BF16)
    kdsb = sb("kdsb", [128, 256], BF16)
    vbf = sb("vbf", [128, 512], BF16)
    ATs = sb("ATs", [128, 128], BF16)
    Sst = sb("Sst", [128, 2, 512], F32)
    Sbf = sb("Sbf", [128, 2, 512], BF16)
    gng = sb("gng", [128, 512], F32)

    if os.environ.get("KDBG"):
        print("SBUF bytes remaining:", nc.sbuf_bytes_remaining)
    ps = [es.enter_context(nc.psum_tensor("ps%d" % i, [128, 512], F32)) for i in range(8)]
    psb = [p[:, :].bitcast(BF16) for p in ps]

    B = P.buf
    PSB = [B("ps%d" % i) for i in range(8)]

    def dve(fn, r, w):
        return P.add("dve", fn, r, w)

    def pool(fn, r, w):
        return P.add("pool", fn, r, w)

    stg_i = [0]

    def load_bf16(dst_ap, src_ap, npart, ncols, dst_bufs):
        s = stg_i[0] % 2
        stg_i[0] += 1
        P.dma("sp", stg[s][0:npart, 0:ncols], src_ap, [], [B("stg%d" % s)], "stg%d" % s)
        P.act(dst_ap, stg[s][0:npart, 0:ncols], AF.Copy, [B("stg%d" % s)], dst_bufs)

    P.dma("sp", ident_f[:, :], ident_d[:, :], [], [B("ident_f")], "c0")
    dve(lambda e: e.tensor_copy(ident[:, :], ident_f[:, :]), [B("ident_f")], [B("ident")])
    P.dma("sp", rope[:, :, :], rope_d[:, :, :], [], [B("rope")], "c1")
    P.dma("sp", tb[:, :], tb_d[:, :], [], [B("tb")], "c2")
    pool(lambda e: e.memset(Vs[:, :, 64:65], 1.0), [], [B("Vs_ones")])
    pool(lambda e: e.memset(Vw[:, :, 64:65], 1.0), [], [B("Vw_ones")])
    pool(lambda e: e.memset(Vc[:, :, 64:65], 1.0), [], [B("Vc_ones")])
    pool(lambda e: e.memset(zeros[:, :], 0.0), [], [B("zeros")])
    pool(lambda e: e.memset(hsb[1][:, 511:512], 0.0), [], [B("hsb1")])
    for (dst_, src_, nm_) in ((d8, d8_d, "d8"), (d0, d0_d, "d0"), (cmaskL, cmask_d, "cmaskL"), (ddsb, dd_d, "ddsb"), (tbs, tbs_d, "tbs")):
        P.dma("sp", dst_[:, :], src_[:, :], [], [B(nm_)], "c_" + nm_)
    load_bf16(ovt[:, :, :].rearrange("p a b -> p (a b)"), ovt_d[:, :], 128, 264, [B("ovt")])
    for g_ in range(8):
        P.dma("sp", ptL[:, g_:g_ + 1], bass.AP(pt_d.tensor, 128 * g_, [[1, 128], [1, 1]]), [], [B("ptL")], "c_ptL")
    for c in range(4):
        load_bf16(ew[:, c * 1024:(c + 1) * 1024], ew_d[:, c * 1024:(c + 1) * 1024], 128, 1024, [B("ew")])
    load_bf16(lw[:, :], lw_d[:, :], 32, 272, [B("lw")])
    load_bf16(ov[:, :, :].rearrange("p a b -> p (a b)"), ov_d[:, :, :].rearrange("p a b -> p (a b)"), 128, 512, [B("ov")])
    for (dst, src, npart, nm) in ((wb4, wb_d, 32, "wb4"), (triu4, triu_d, 128, "triu4"), (tril4, tril_d, 128, "tril4")):
        s = stg_i[0] % 2
        stg_i[0] += 1
        P.dma("sp", stg[s][0:npart, 0:128], src[:, :], [], [B("stg%d" % s)], "stg%d" % s)
        P.act(dst[0:npart, :].rearrange("p (a b) -> p a b", a=4), bcast(stg[s][0:npart, 0:128], 1, 4), AF.Copy,
              [B("stg%d" % s)], [B(nm)])

    def to_xT(t, src_ap, src_bufs, slot):
        xb = xbf[slot]
        P.act(xb[:, :], src_ap, AF.Copy, src_bufs, [B("xbf%d" % slot)])
        pb = psb[6 + slot]
        for c in range(KC):
            P.tr(pb[:, c * 128:(c + 1) * 128], xb[:, c * 128:(c + 1) * 128], ident[:, :],
                 [B("xbf%d" % slot), B("ident")], [PSB[6 + slot]])
        xt = xTt[slot]
        dve(lambda e: e.tensor_copy(xt[:, :], pb[:, :]), [PSB[6 + slot]], [B("xTt%d" % slot)])
        P.dma("sp", xT[:, :, t * 128:(t + 1) * 128].rearrange("c p t -> p c t"),
              xt[:, :].rearrange("p (c t) -> p c t", c=KC),
              [B("xTt%d" % slot)], [B("xT_%d" % t)], "xTst%d" % slot)

    if NT1 < NT:
        pool(lambda e: e.memset(KT[:, :, :], 0.0), [], [B("KT_%d" % t) for t in range(NT)])
        pool(lambda e: e.memset(Vs[:, :, 0:64], 0.0), [], [B("Vs_%d" % t) for t in range(NT)])
        pool(lambda e: e.memset(Vw[:, :, 0:64], 0.0), [], [B("Vw_%d" % t) for t in range(NT)])
    for t in range(NT1):
        s = t % 2
        P.dma("sp", xin[s][:, :], xp[t * 128:(t + 1) * 128, :], [], [B("fbuf%d" % s)], "fbufld%d" % s)
        to_xT(t, xin[s][:, :], [B("fbuf%d" % s)], s)

    def nsa_layer(li, layer):
        for c in range(KC):
            load_bf16(w_nsa[:, c, 0:1432], nsa_win[li][c * 128:(c + 1) * 128, :], 128, 1432, [B("w_nsa")])
        for c in range(4):
            load_bf16(w_out[:, c, :], nsa_wout[li][c * 128:(c + 1) * 128, :], 128, D, [B("w_out")])
        for c in range(4):
            load_bf16(w1sb[:, 8 * c:8 * c + 8, :].rearrange("p a b -> p (a b)"), nsa_w1[li][:, 1024 * c:1024 * c + 1024], 128, 1024, [B("w1sb")])
        load_bf16(pesb[:, :], nsa_pe[li][:, :], 128, 32, [B("pesb")])
        load_bf16(w2sb[:, :], nsa_w2[li][:, :], 128, 192, [B("w2sb")])
        for kv_i in range(2):
            lo = 64 * kv_i
            pbias = ps[7 - 3 * kv_i]
            for l in range(32):
                P.mm(pbias[:, 0:1], w1sb[lo:lo + 64, l, :], pesb[lo:lo + 64, l:l + 1], l == 0, l == 31,
                     [B("w1sb"), B("pesb")], [PSB[7 - 3 * kv_i]])
            dve(lambda e, pbias=pbias, kv_i=kv_i: e.tensor_copy(hb[:, kv_i:kv_i + 1], pbias[:, 0:1]), [PSB[7 - 3 * kv_i]], [B("hb")])
        if SAMPLE:
            nsa_sample(li, layer)
            pool(lambda e: e.memset(Vs[:, :, 64:65], 1.0), [], Vskeys + [B("Vs_ones")])
            pool(lambda e: e.memset(Vw[:, :, 64:65], 1.0), [], Vwkeys + [B("Vw_ones")])
        if os.environ.get("KNOPROMPT"):
            return

        for u in range(NT1):
            for j in range(1):
                t = u
                s = t % 2
                P.dma("sp", xq[s][:, :, :], xT[:, :, t * 128:(t + 1) * 128].rearrange("c p t -> p c t"),
                      [B("xT_%d" % t)], [B("xq%d" % s)], "xq%d" % s)
                pk = ps[4 + s]
                for c in range(KC):
                    P.mm(pk[:, 0:384], xq[s][:, c, :], w_nsa[:, c, 512:896],
                         c == 0, c == KC - 1, [B("xq%d" % s), B("w_nsa")], [PSB[4 + s]])
                kv = kvsb[s]
                P.act(kv[:, :], pk[:, 0:384], AF.Copy, [PSB[4 + s]], [B("kvsb%d" % s)])
                kv3 = kv[:, :].rearrange("p (a b) -> p a b", a=3)
                rope_apply(kv3[:, :, 0:8], kv3[:, :, 8:16], 3, t, ropetmp[s], "rt%d" % s, [B("kvsb%d" % s)], [B("kvsb%d" % s)])
                P.dma("sp", kvout[li][t * 128:(t + 1) * 128, :], kv[:, :], [B("kvsb%d" % s)],
                      [B("kvout_%d_%d" % (li, t))], "kvo%d" % s)
                kst = kstage[s]
                sbk = [B("kvsb%d" % s)]
                pool(lambda e, kst=kst, kv=kv: e.tensor_copy(kst[:, 0, :], kv[:, 0:128]), sbk, [B("kst%d_0" % s)])
                gs = t // 32
                pool(lambda e, kst=kst, kv=kv, gs=gs: e.tensor_copy(kst[:, 1, 64 * gs:64 * gs + 64], kv[:, 128:192]), sbk, [B("kst%d_1" % s)])
                pool(lambda e, kst=kst, kv=kv, gs=gs: e.tensor_copy(kst[:, 1, 64 - 64 * gs:128 - 64 * gs], kv[:, 256:320]), sbk + [B("kst%d_1" % s)], [B("kst%d_1" % s)])
                pb = psb[6 + s]
                for a in range(2):
                    P.tr(pb[:, a * 128:(a + 1) * 128], kst[:, a, :], ident[:, :], [B("kst%d_%d" % (s, a)), B("ident")], [PSB[6 + s]])
                dve(lambda e, pb=pb, t=t: e.tensor_copy(KT[:, :, t * 128:(t + 1) * 128], pb[:, 0:256].rearrange("p (a t) -> p a t", a=2)),
                    [PSB[6 + s]], [B("KT_%d" % t)])
                pool(lambda e, kv=kv, t=t: e.tensor_copy(Vs[:, t, 0:64], kv[:, 192:256]), sbk, [B("Vs_%d" % t)])
                pool(lambda e, kv=kv, t=t: e.tensor_copy(Vw[:, t, 0:64], kv[:, 320:384]), sbk, [B("Vw_%d" % t)])
        KTall = [B("KT_%d" % t) for t in range(NT)]

        for kv_i in range(2):
            lo = 64 * kv_i
            ph = ps[5 + kv_i]
            for l in range(32):
                rhs = KT[lo:lo + 64, 0, l:l + 16 * 510 + 1:16]
                P.mm(ph[:, 0:511], w1sb[lo:lo + 64, l, :], rhs, l == 0, l == 31, [B("w1sb")] + KTall, [PSB[5 + kv_i]])
            g0, g1, g2 = gtmp
            P.act(g0[:, 0:511], ph[:, 0:511], AF.Identity, [PSB[5 + kv_i], B("hb")], [B("fbuf0")], bias=hb[:, kv_i:kv_i + 1])
            dve(lambda e: e.tensor_tensor(g1[:, 0:511], g0[:, 0:511], g0[:, 0:511], ALU.mult), [B("fbuf0")], [B("fbuf1")])
            dve(lambda e: e.tensor_scalar(g1[:, 0:511], g1[:, 0:511], 0.044715, 1.0, ALU.mult, ALU.add), [B("fbuf1")], [B("fbuf1")])
            dve(lambda e: e.tensor_tensor(g1[:, 0:511], g1[:, 0:511], g0[:, 0:511], ALU.mult), [B("fbuf1"), B("fbuf0")], [B("fbuf1")])
            P.act(g2[:, 0:511], g1[:, 0:511], AF.Sigmoid, [B("fbuf1")], [B("fbuf2")], scale=1.5957691216)
            h = hsb[kv_i]
            dve(lambda e, h=h: e.tensor_tensor(h[:, 0:511], g2[:, 0:511], g0[:, 0:511], ALU.mult), [B("fbuf2"), B("fbuf0")], [B("hsb%d" % kv_i)])
        pk2 = ps[7]
        P.mm(pk2[:, 0:511], w2sb[:, 0:128], hsb[0][:, 0:511], True, True, [B("w2sb"), B("hsb0")], [PSB[7]])
        dve(lambda e: e.tensor_copy(KcT2[:, 0:511], pk2[:, 0:511]), [PSB[7]], [B("KcT2")])
        for ci in range(4):
            n = min(128, 511 - 128 * ci)
            pv = ps[4]
            P.mm(pv[0:n, 0:64], hsb[1][:, 128 * ci:128 * ci + n], w2sb[:, 128:192], True, True, [B("w2sb"), B("hsb1")], [PSB[4]])
            dve(lambda e, ci=ci, n=n, pv=pv: e.tensor_copy(Vc[0:n, ci, 0:64], pv[0:n, 0:64]), [PSB[4]], [B("Vc")])

        srot = [0]

        def unit(lhsT_k, lo, nkeys, half, masks, pt_ap, pt_buf, kbufs):
            si = srot[0] % 3
            srot[0] += 1
            S = ps[si]
            nm = len(masks)
            P.mm(S[0:nkeys, :], lhsT_k, QT[lo:lo + 64, half * 512:(half + 1) * 512], True, nm == 0, kbufs + [B("QT")], [PSB[si]])
            for mi, (ml, mr, mb) in enumerate(masks):
                P.mm(S[0:nkeys, :], ml, mr, False, mi == nm - 1, mb, [PSB[si]])
            P.act(pt_ap, S[0:nkeys, :], AF.Exp, [PSB[si]], [pt_buf], scale=0.125)

        def pv_acc(pt_ap, pt_buf, nkeys, v_ap, vbufs, half, first, last):
            O = ps[3 + half]
            if first:
                P.mm(O[:, 0:260], zeros[0:32, 0:128], zeros[0:32, 0:260], True, False, [B("zeros")], [PSB[3 + half]])
            for jj in range(4):
                P.mm(O[:, jj * 65:(jj + 1) * 65], pt_ap[:, jj * 128:(jj + 1) * 128], v_ap, False, last and jj == 3,
                     [pt_buf] + vbufs, [PSB[3 + half]])

        pend = []

        def defer(f):
            if len(pend) >= 2:
                pend.pop(0)()
            pend.append(f)

        def flush():
            while pend:
                pend.pop(0)()

        def combine(br, first_branch):
            for half in range(2):
                O3 = ps[3 + half][:, 0:260].rearrange("p (j e) -> p j e", j=4)
                rzs = rz[:, br, 4 * half:4 * half + 4]
                wg = wgt[:, br, 4 * half:4 * half + 4]
                kz = B("rz_%d_%d" % (br, half))
                kw_ = B("wgt_%d_%d" % (br, half))
                dve(lambda e, O3=O3, rzs=rzs: e.tensor_scalar(rzs, O3[:, :, 64], 1.0e-30, None, ALU.max), [PSB[3 + half]], [kz])
                dve(lambda e, rzs=rzs: e.reciprocal(rzs, rzs), [kz], [kz])
                gsl = gsb[:, br * 8 + 4 * half:br * 8 + 4 * half + 4]
                dve(lambda e, wg=wg, rzs=rzs, gsl=gsl: e.tensor_tensor(wg, rzs, gsl, ALU.mult), [kz, B("gsb")], [kw_])
                oa = oacc[:, 256 * half:256 * half + 256].rearrange("p (j e) -> p j e", j=4)
                if first_branch:
                    dve(lambda e, oa=oa, O3=O3, wg=wg: e.tensor_tensor(oa, O3[:, :, 0:64], bcast(wg, 2, 64), ALU.mult),
                        [PSB[3 + half], kw_], [B("oacc%d" % half)])
                else:
                    ot = otmp[:, 256 * half:256 * half + 256].rearrange("p (j e) -> p j e", j=4)
                    dve(lambda e, ot=ot, O3=O3, wg=wg: e.tensor_tensor(ot, O3[:, :, 0:64], bcast(wg, 2, 64), ALU.mult),
                        [PSB[3 + half], kw_], [B("otmp%d" % half)])
                    pool(lambda e, oa=oa, ot=ot: e.tensor_tensor(oa, oa, ot, ALU.add), [B("otmp%d" % half), B("oacc%d" % half)], [B("oacc%d" % half)])

        for i in range(int(os.environ.get("KQ0", "0")), NQT):
            sq = i % 2
            P.dma("sp", xq[sq][:, :, :], xT[:, :, i * 128:(i + 1) * 128].rearrange("c p t -> p c t"),
                  [B("xT_%d" % i)], [B("xq%d" % sq)], "xq%d" % sq)
            for (col0, ncol, pi) in ((0, 512, 5), (920, 512, 6), (896, 24, 7)):
                for c in range(KC):
                    P.mm(ps[pi][:, 0:ncol], xq[sq][:, c, :], w_nsa[:, c, col0:col0 + ncol], c == 0, c == KC - 1,
                         [B("xq%d" % sq), B("w_nsa")], [PSB[pi]])
            P.act(qf[:, :], ps[5][:, :], AF.Copy, [PSB[5]], [B("qf")])
            P.act(szs[:, :], ps[6][:, :], AF.Silu, [PSB[6]], [B("szs")])
            P.act(gsb[:, :], ps[7][:, 0:24], AF.Sigmoid, [PSB[7]], [B("gsb")])
            q3 = qf[:, :].rearrange("p (a b) -> p a b", a=8)
            rope_apply(q3[:, :, 0:8], q3[:, :, 8:16], 8, i, ropetmp[0], "rt0", [B("qf")], [B("qf")])
            pool(lambda e: e.tensor_copy(qbf[:, :].rearrange("p (h a d) -> p h a d", h=8, a=2), bcast(qf[:, :].rearrange("p (h d) -> p h d", h=8), 2, 2)), [B("qf")], [B("qbf")])
            pq = psb[7]
            for j in range(8):
                P.tr(pq[:, j * 128:(j + 1) * 128], qbf[:, j * 128:(j + 1) * 128], ident[:, :], [B("qbf"), B("ident")], [PSB[7]])
            dve(lambda e, pq=pq: e.tensor_copy(QT[:, :], pq[:, :]), [PSB[7]], [B("QT")])

            chunks = [ci for ci in range(4) if 128 * ci <= 8 * i + 6]
            for half in range(2):
                for k_, ci in enumerate(chunks):
                    cs = 128 * ci
                    n = min(128, 511 - cs)
                    delta = 8 * i - cs
                    masks = []
                    if delta <= 129:
                        off = 129 - delta
                        masks.append((lw[0:32, off:off + n], wb4[0:32, :], [B("lw"), B("wb4")]))
                    pt = PTc[0:n, 2 * ci + half, :]
                    ptb = B("PTc_%d_%d" % (ci, half))
                    unit(KcT2[0:64, cs:cs + n], 0, n, half, masks, pt, ptb, [B("KcT2")])
                    defer(lambda pt=pt, ptb=ptb, n=n, ci=ci, half=half, k_=k_: pv_acc(pt, ptb, n, Vc[0:n, ci, :], [B("Vc"), B("Vc_ones")], half, k_ == 0, k_ == len(chunks) - 1))
            flush()
            combine(0, True)
            for half in range(2):
                for jj in range(4):
                    for k_, ci in enumerate(chunks):
                        n = min(128, 511 - 128 * ci)
                        P.mm(ps[5 + half][:, jj * 128:(jj + 1) * 128], PTc[0:n, 2 * ci + half, jj * 128:(jj + 1) * 128], ov[0:n, ci, :],
                             k_ == 0, k_ == len(chunks) - 1, [B("PTc_%d_%d" % (ci, half)), B("ov")], [PSB[5 + half]])
            first = True
            for half in range(2):
                for jj in range(4):
                    A = ps[5 + half][:, jj * 128:(jj + 1) * 128]
                    sc = rz[:, 0, 4 * half + jj:4 * half + jj + 1]
                    rb = [PSB[5 + half], B("rz_0_%d" % half)]
                    if first:
                        dve(lambda e, A=A, sc=sc: e.tensor_scalar(imp[:, :], A, sc, None, ALU.mult), rb, [B("imp")])
                        first = False
                    else:
                        dve(lambda e, A=A, sc=sc: e.scalar_tensor_tensor(imp[:, :], A, sc, imp[:, :], ALU.mult, ALU.add), rb + [B("imp")], [B("imp")])
            toff = 126 - 2 * i
            dve(lambda e, toff=toff: e.tensor_tensor(score[:, :], imp[:, :], tb[:, toff:toff + 128], ALU.add), [B("imp"), B("tb")], [B("score")])
            if i >= 1:
                dve(lambda e: e.tensor_scalar(score[:, 0:1], score[:, 0:1], 1000.0, None, ALU.add), [B("score")], [B("score")])
            dve(lambda e: e.max(m8[:, 0:8], score[:, :]), [B("score")], [B("m8a")])
            dve(lambda e: e.match_replace(score2[:, :], m8[:, 0:8], score[:, :], -1.0e30), [B("score"), B("m8a")], [B("score2")])
            dve(lambda e: e.max(m8[:, 8:16], score2[:, :]), [B("score2")], [B("m8b")])
            dve(lambda e: e.tensor_scalar(selb[:, :], score[:, :], m8[:, 15:16], NEGB, ALU.is_lt, ALU.mult), [B("score"), B("m8b")], [B("selb")])
            P.tr(pq[:, 512:640], selb[:, :], ident[:, :], [B("selb"), B("ident")], [PSB[7]])
            dve(lambda e, pq=pq: e.tensor_copy(selT4[:, :].rearrange("p (a b) -> p a b", a=4), bcast(pq[:, 512:640], 1, 4)), [PSB[7]], [B("selT4")])

            for half in range(2):
                for t in range(i + 1):
                    gg, tm = t // 32, t % 32
                    masks = [(ew[64 * gg:64 * gg + 64, tm * 128:(tm + 1) * 128], selT4[64 * gg:64 * gg + 64, :], [B("ew"), B("selT4")])]
                    if t == i:
                        masks.append((ident[:, :], triu4[:, :], [B("ident"), B("triu4")]))
                    si = srot[0] % 3
                    pt = PT[si][:, :]
                    ptb = B("PT%d" % si)
                    unit(KT[64 * gg:64 * gg + 64, 1, t * 128:(t + 1) * 128], 64 * gg, 128, half, masks, pt, ptb, [B("KT_%d" % t)])
                    defer(lambda pt=pt, ptb=ptb, t=t, half=half: pv_acc(pt, ptb, 128, Vs[:, t, :], [B("Vs_%d" % t), B("Vs_ones")], half, t == 0, t == i))
            flush()
            combine(1, False)
            t0 = max(0, i - 4)
            for half in range(2):
                for t in range(t0, i + 1):
                    masks = []
                    if t == i:
                        masks.append((ident[:, :], triu4[:, :], [B("ident"), B("triu4")]))
                    elif t == i - 4:
                        masks.append((ident[:, :], tril4[:, :], [B("ident"), B("tril4")]))
                    si = srot[0] % 3
                    pt = PT[si][:, :]
                    ptb = B("PT%d" % si)
                    low = 64 - 64 * (t // 32)
                    unit(KT[low:low + 64, 1, t * 128:(t + 1) * 128], low, 128, half, masks, pt, ptb, [B("KT_%d" % t)])
                    defer(lambda pt=pt, ptb=ptb, t=t, half=half: pv_acc(pt, ptb, 128, Vw[:, t, :], [B("Vw_%d" % t), B("Vw_ones")], half, t == t0, t == i))
            flush()
            combine(2, False)

            dve(lambda e: e.tensor_tensor(og[:, :], oacc[:, :], szs[:, :], ALU.mult), [B("oacc0"), B("oacc1"), B("szs")], [B("og")])
            out_tail(i, layer)

    def out_tail(i, layer):
        sq = i % 2
        pq = psb[7]
        for j in range(4):
            P.tr(pq[:, j * 128:(j + 1) * 128], og[:, j * 128:(j + 1) * 128], ident[:, :], [B("og"), B("ident")], [PSB[7]])
        dve(lambda e: e.tensor_copy(ogT[:, :], pq[:, 0:512]), [PSB[7]], [B("ogT")])
        for hh in range(2):
            for c in range(4):
                P.mm(ps[5 + hh][:, :], ogT[:, c * 128:(c + 1) * 128], w_out[:, c, hh * 512:(hh + 1) * 512], c == 0, c == 3,
                     [B("ogT"), B("w_out")], [PSB[5 + hh]])
        ys = ysb[sq]
        P.act(ys[:, 0:512], ps[5][:, :], AF.Copy, [PSB[5]], [B("fbuf%d" % sq)])
        dve(lambda e: e.tensor_copy(ys[:, 512:1024], ps[6][:, :]), [PSB[6]], [B("fbuf%d" % sq)])
        P.dma("sp", ypart[layer][i // 8][(i % 8) * 128:(i % 8 + 1) * 128, :], ys[:, :], [B("fbuf%d" % sq)], [B("ypart_%d_%d" % (layer, i))], "yst%d" % sq)

    def ret_layer(li, layer):
        for c in range(KC):
            load_bf16(w_nsa[:, c, :], ret_win[li][c * 128:(c + 1) * 128, :], 128, 1536, [B("w_nsa")])
        for c in range(4):
            load_bf16(w_out[:, c, :], ret_wout[li][c * 128:(c + 1) * 128, :], 128, D, [B("w_out")])
        P.dma("sp", gng[:, :], bass.AP(ret_gn[li].tensor, 0, [[0, 128], [1, 512]]), [], [B("gng")], "gng")
        P.dma("sp", dmaskT[:, :], dmask_d[:, :], [], [B("dmaskT")], "rc0")
        P.dma("sp", qdec[:, :], qdec_d[:, :], [], [B("qdec")], "rc1")
        P.dma("sp", kcdec[:, :], kcdec_d[:, :], [], [B("kcdec")], "rc2")
        if SAMPLE:
            ret_sample(li, layer)
        nch = NQT
        for n in range(nch):
            sq = n % 2
            P.dma("sp", xq[sq][:, :, :], xT[:, :, n * 128:(n + 1) * 128].rearrange("c p t -> p c t"),
                  [B("xT_%d" % n)], [B("xq%d" % sq)], "xq%d" % sq)
            cs_t = xcs[sq]
            P.dma("sp", cs_t[:, 0, :], xcos_d[:, n * 128:(n + 1) * 128], [], [B("xcs%d" % sq)], "xcsa%d" % sq)
            P.dma("sp", cs_t[:, 1, :], xsin_d[:, n * 128:(n + 1) * 128], [], [B("xcs%d" % sq)], "xcsb%d" % sq)
            for col in range(4):
                for c in range(KC):
                    P.mm(ps[0][:, col * 128:(col + 1) * 128], w_nsa[:, c, col * 128:(col + 1) * 128], xq[sq][:, c, :], c == 0, c == KC - 1,
                         [B("xq%d" % sq), B("w_nsa")], [PSB[0]])
            for (col0, pi) in ((512, 1), (1024, 2)):
                for c in range(KC):
                    P.mm(ps[pi][:, :], xq[sq][:, c, :], w_nsa[:, c, col0:col0 + 512], c == 0, c == KC - 1,
                         [B("xq%d" % sq), B("w_nsa")], [PSB[pi]])
            P.act(vbf[:, :], ps[1][:, :], AF.Copy, [PSB[1]], [B("vbf")])
            P.act(szs[:, :], ps[2][:, :], AF.Silu, [PSB[2]], [B("szs")])
            pool(lambda e: e.tensor_tensor(qf[:, :], szs[:, :], gng[:, :], ALU.mult), [B("szs"), B("gng")], [B("qf")])
            pv4 = ps[0][:, :].rearrange("p (a b t) -> p a b t", a=2, b=2)
            E, O_ = pv4[:, :, 0, :], pv4[:, :, 1, :]
            cosb = bcast(cs_t[:, 0, :], 1, 2)
            sinb = bcast(cs_t[:, 1, :], 1, 2)
            rv = [rtmp[:, a, :].rearrange("p (a t) -> p a t", a=2) for a in range(4)]
            rb = [PSB[0], B("xcs%d" % sq)]
            dve(lambda e, E=E, cosb=cosb: e.tensor_tensor(rv[0], E, cosb, ALU.mult), rb, [B("rtmp0")])
            dve(lambda e, O_=O_, sinb=sinb: e.tensor_tensor(rv[1], O_, sinb, ALU.mult), rb, [B("rtmp1")])
            dve(lambda e, E=E, sinb=sinb: e.tensor_tensor(rv[2], E, sinb, ALU.mult), rb, [B("rtmp2")])
            dve(lambda e, O_=O_, cosb=cosb: e.tensor_tensor(rv[3], O_, cosb, ALU.mult), rb, [B("rtmp3")])
            qk4 = QKr[:, :, :].rearrange("p (a b) t -> p a b t", a=2)
            dve(lambda e: e.tensor_tensor(qk4[:, :, 0, :], rv[0], rv[1], ALU.subtract), [B("rtmp0"), B("rtmp1")], [B("QKr_e")])
            dve(lambda e: e.tensor_tensor(qk4[:, :, 1, :], rv[2], rv[3], ALU.add), [B("rtmp2"), B("rtmp3")], [B("QKr_o")])
            qkb = [B("QKr_e"), B("QKr_o")]
            pool(lambda e: e.tensor_tensor(qdT[:, :, :], QKr[:, 0:2, :], bcast(qdec[:, :], 1, 2), ALU.mult), qkb + [B("qdec")], [B("qdT")])
            pq = psb[7]
            for ch in range(2):
                P.tr(pq[:, ch * 128:(ch + 1) * 128], QKr[:, 2 + ch, :], ident[:, :], qkb + [B("ident")], [PSB[7]])
            dve(lambda e: e.tensor_scalar(kdsb[:, :], pq[:, 0:256], kcdec[:, 0:1], None, ALU.mult), [PSB[7], B("kcdec")], [B("kdsb")])
            for ch in range(2):
                P.mm(ps[3][:, 0:128], QKr[:, 2 + ch, :], QKr[:, ch, :], ch == 0, ch == 1, qkb, [PSB[3]])
            dve(lambda e: e.tensor_tensor(ATs[:, :], ps[3][:, 0:128], dmaskT[:, :], ALU.mult), [PSB[3], B("dmaskT")], [B("ATs")])
            P.mm(ps[4][:, :], ATs[:, :], vbf[:, :], True, n == 0, [B("ATs"), B("vbf")], [PSB[4]])
            if n > 0:
                for ch in range(2):
                    P.mm(ps[4][:, :], qdT[:, ch, :], Sbf[:, ch, :], False, ch == 1, [B("qdT"), B("Sbf")], [PSB[4]])
            for ch in range(2):
                P.mm(ps[5 + ch][:, :], kdsb[:, ch * 128:(ch + 1) * 128], vbf[:, :], True, True, [B("kdsb"), B("vbf")], [PSB[5 + ch]])
                if n == 0:
                    dve(lambda e, ch=ch: e.tensor_copy(Sst[:, ch, :], ps[5 + ch][:, :]), [PSB[5 + ch]], [B("Sst%d" % ch)])
                else:
                    dve(lambda e, ch=ch: e.scalar_tensor_tensor(Sst[:, ch, :], Sst[:, ch, :], kcdec[:, 1:2], ps[5 + ch][:, :], ALU.mult, ALU.add),
                        [PSB[5 + ch], B("Sst%d" % ch), B("kcdec")], [B("Sst%d" % ch)])
                pool(lambda e, ch=ch: e.tensor_copy(Sbf[:, ch, :], Sst[:, ch, :]), [B("Sst%d" % ch)], [B("Sbf")])
            st = lnst[sq]
            kst_ = B("lnst%d" % sq)
            dve(lambda e, st=st: e.reduce_sum(st[:, 0:1], ps[4][:, :], AX.X), [PSB[4]], [kst_])
            dve(lambda e, st=st: e.tensor_scalar(st[:, 1:2], st[:, 0:1], -1.0 / 512, None, ALU.mult), [kst_], [kst_])
            P.act(oacc[:, :], ps[4][:, :], AF.Identity, [PSB[4], kst_], [B("oacc0"), B("oacc1")], bias=st[:, 1:2])
            P.act(otmp[:, :], oacc[:, :], AF.Square, [B("oacc0"), B("oacc1")], [B("otmp0"), B("otmp1")])
            dve(lambda e, st=st: e.reduce_sum(st[:, 2:3], otmp[:, :], AX.X), [B("otmp0"), B("otmp1")], [B("lnsq%d" % sq)])
            dve(lambda e, st=st: e.tensor_scalar(st[:, 3:4], st[:, 2:3], 1.0 / 512, LN_EPS, ALU.mult, ALU.add), [B("lnsq%d" % sq)], [B("lnr%d" % sq)])
            P.act(st[:, 5:6], st[:, 3:4], AF.Sqrt, [B("lnr%d" % sq)], [B("lnr%d" % sq)])
            dve(lambda e, st=st: e.reciprocal(st[:, 4:5], st[:, 5:6]), [B("lnr%d" % sq)], [B("lnr%d" % sq)])
            dve(lambda e, st=st: e.scalar_tensor_tensor(og[:, :], oacc[:, :], st[:, 4:5], qf[:, :], ALU.mult, ALU.mult),
                [B("oacc0"), B("oacc1"), B("lnr%d" % sq), B("qf")], [B("og")])
            out_tail(n, layer)
        for ch in range(2):
            P.dma("sp", retout[li][ch * 128:(ch + 1) * 128, :], Sst[:, ch, :], [B("Sst%d" % ch)], [B("retout_%d_%d" % (li, ch))], "reto%d" % ch)

    def rope_apply(x1, x2, nrep, t, rt, rtname, rbufs, wbufs, cs_ap=None, sn_ap=None, npart=128):
        cs = bcast(rope[:, t, 0:8] if cs_ap is None else cs_ap, 1, nrep)
        sn = bcast(rope[:, t, 8:16] if sn_ap is None else sn_ap, 1, nrep)
        n8 = nrep * 8
        v = [rt[0:npart, a, 0:n8].rearrange("p (a b) -> p a b", a=nrep) for a in range(4)]
        rb = list(rbufs) + ([B("rope")] if cs_ap is None else [])
        keys = [B("%s_%d" % (rtname, a)) for a in range(4)]
        dve(lambda e: e.tensor_tensor(v[0], x1, cs, ALU.mult), rb, [keys[0]])
        dve(lambda e: e.tensor_tensor(v[1], x2, sn, ALU.mult), rb, [keys[1]])
        dve(lambda e: e.tensor_tensor(v[2], x1, sn, ALU.mult), rb, [keys[2]])
        dve(lambda e: e.tensor_tensor(v[3], x2, cs, ALU.mult), rb, [keys[3]])
        dve(lambda e: e.tensor_tensor(x1, v[0], v[1], ALU.subtract), [keys[0], keys[1], keys[2]], wbufs)
        dve(lambda e: e.tensor_tensor(x2, v[2], v[3], ALU.add), [keys[2], keys[3]], wbufs)

    def ln_pass(layer, last):
        ntile = NQT if NQT < NT else NT
        for c in range((ntile + 7) // 8):
            P.add("pool", lambda e, c=c: e.collective_compute("AllReduce", ALU.add, replica_groups=[[0, 1, 2, 3], [4, 5, 6, 7]],
                                                              ins=[ypart[layer][c].ap().opt()], outs=[ysum[layer][c].ap().opt()]),
                  [B("ypart_%d_%d" % (layer, i)) for i in range(8 * c, min(8 * c + 8, ntile))], [B("ysum_%d_%d" % (layer, c))],
                  kind="cc", semkey="cc%d_%d" % (layer, c))
        P.dma("sp", lng[:, 0:D], bass.AP(lng_d.tensor, layer * D, [[0, 128], [1, D]]), [], [B("stg0")], "stg0")
        P.dma("sp", lnb[:, 0:D], bass.AP(lnb_d.tensor, layer * D, [[0, 128], [1, D]]), [], [B("stg1")], "stg1")
        for t in range(ntile):
            s = t % 2
            src = xp if layer == 0 else xcur
            X, Y, st = lnx[s], lny[s], lnst[s]
            kx, ky, kst_ = B("fbuf%d" % s), B("fbuf%d" % (2 + s)), B("lnst%d" % s)
            P.dma("sp", X[:, :], src[t * 128:(t + 1) * 128, :], [B("xcur_%d" % t)], [kx], "fbufld%d" % s)
            P.dma("sp", Y[:, :], ysum[layer][t // 8][(t % 8) * 128:(t % 8 + 1) * 128, :], [B("ysum_%d_%d" % (layer, t // 8))], [ky], "fbufld%d" % (2 + s))
            dve(lambda e, X=X, Y=Y: e.scalar_tensor_tensor(X[:, :], X[:, :], ALPHA, Y[:, :], ALU.mult, ALU.add), [kx, ky], [kx])
            dve(lambda e, X=X, st=st: e.reduce_sum(st[:, 0:1], X[:, :], AX.X), [kx], [kst_])
            dve(lambda e, st=st: e.tensor_scalar(st[:, 1:2], st[:, 0:1], -1.0 / D, None, ALU.mult), [kst_], [kst_])
            P.act(Y[:, :], X[:, :], AF.Identity, [kx, kst_], [ky], bias=st[:, 1:2])
            P.act(X[:, :], Y[:, :], AF.Square, [ky], [kx])
            dve(lambda e, X=X, st=st: e.reduce_sum(st[:, 2:3], X[:, :], AX.X), [kx], [B("lnsq%d" % s)])
            dve(lambda e, st=st: e.tensor_scalar(st[:, 3:4], st[:, 2:3], 1.0 / D, LN_EPS, ALU.mult, ALU.add), [B("lnsq%d" % s)], [B("lnr%d" % s)])
            P.act(st[:, 5:6], st[:, 3:4], AF.Sqrt, [B("lnr%d" % s)], [B("lnr%d" % s)])
            dve(lambda e, st=st: e.reciprocal(st[:, 4:5], st[:, 5:6]), [B("lnr%d" % s)], [B("lnr%d" % s)])
            dve(lambda e, X=X, Y=Y, st=st: e.scalar_tensor_tensor(X[:, :], Y[:, :], st[:, 4:5], lng[:, 0:D], ALU.mult, ALU.mult),
                [ky, B("lnr%d" % s), B("stg0"), B("lnsq%d" % s)], [kx])
            pool(lambda e, X=X: e.tensor_tensor(X[:, :], X[:, :], lnb[:, 0:D], ALU.add), [kx, B("stg1")], [kx])
            dst = yout if last else xcur
            P.dma("sp", dst[t * 128:(t + 1) * 128, :], X[:, :], [kx], [B("xcur_%d" % t)], "lnst%d" % s)
            if not last:
                to_xT(t, X[:, :], [kx], s)

    KTkeys = [B("KT_%d" % t) for t in range(NT)]
    Vskeys = [B("Vs_%d" % t) for t in range(NT)]
    Vwkeys = [B("Vw_%d" % t) for t in range(NT)]
    PTckeys = [B("PTc_%d_%d" % (ci, hf)) for ci in range(4) for hf in range(2)]
    GA = KT[:, :, :].rearrange("p a t -> p (a t)").bitcast(F32)
    XTlo = Vs[:, :, :].rearrange("p a b -> p (a b)")[:, 0:4096].rearrange("p (a b) -> p a b", a=32)
    XThi = Vw[:, :, :].rearrange("p a b -> p (a b)")[:, 0:4096].rearrange("p (a b) -> p a b", a=32)
    W1s = PTc[:, :, :].rearrange("p a b -> p (a b)").rearrange("p (k c h) -> p k c h", k=2, c=16)
    FB = [B("fbuf%d" % i) for i in range(4)]

    def load_xsT(layer):
        src = xs_d if layer == 0 else xs_cur
        P.dma("sp", fbuf[0][0:64, :], src[:, :], [B("xs_cur")], [FB[0]], "fbufld0")
        P.act(xbf[0][0:64, :], fbuf[0][0:64, :], AF.Copy, [FB[0]], [B("xbf0")])
        pb = psb[7]
        for c in range(KC):
            P.tr(pb[:, c * 64:(c + 1) * 64], xbf[0][0:64, c * 128:(c + 1) * 128], ident[0:64, 0:64], [B("xbf0"), B("ident")], [PSB[7]])
        dve(lambda e: e.tensor_copy(xsT[:, :], pb[:, 0:512]), [PSB[7]], [B("xsT")])

    def s_proj(col0, ncol, pi):
        for c in range(KC):
            P.mm(ps[pi][0:64, 0:ncol], xsT[:, c * 64:(c + 1) * 64], w_nsa[:, c, col0:col0 + ncol], c == 0, c == KC - 1,
                 [B("xsT"), B("w_nsa")], [PSB[pi]])

    def s_out_tail(layer):
        pq = psb[7]
        for j in range(4):
            P.tr(pq[:, j * 64:(j + 1) * 64], og[0:64, j * 128:(j + 1) * 128], ident[0:64, 0:64], [B("og"), B("ident")], [PSB[7]])
        dve(lambda e: e.tensor_copy(ogT[:, 0:256], pq[:, 0:256]), [PSB[7]], [B("ogT")])
        for hh in range(2):
            for c in range(4):
                P.mm(ps[5 + hh][0:64, :], ogT[:, c * 64:(c + 1) * 64], w_out[:, c, hh * 512:(hh + 1) * 512], c == 0, c == 3,
                     [B("ogT"), B("w_out")], [PSB[5 + hh]])
        ys = fbuf[0]
        P.act(ys[0:64, 0:512], ps[5][0:64, :], AF.Copy, [PSB[5]], [FB[0]])
        dve(lambda e: e.tensor_copy(ys[0:64, 512:1024], ps[6][0:64, :]), [PSB[6]], [FB[0]])
        P.dma("sp", ypart_s[layer][:, :], ys[0:64, :], [FB[0]], [B("ypart_s%d" % layer)], "yst0")

    def s_groupnorm_gate(o_ap, o_bufs, gate_ap, gate_bufs):
        st = lnst[0]
        kst_ = B("lnst0")
        dve(lambda e: e.reduce_sum(st[0:64, 0:1], o_ap, AX.X), o_bufs, [kst_])
        dve(lambda e: e.tensor_scalar(st[0:64, 1:2], st[0:64, 0:1], -1.0 / 512, None, ALU.mult), [kst_], [kst_])
        P.act(otmp[0:64, :], o_ap, AF.Identity, o_bufs + [kst_], [B("otmp0"), B("otmp1")], bias=st[0:64, 1:2])
        P.act(fbuf[3][0:64, 0:512], otmp[0:64, :], AF.Square, [B("otmp0"), B("otmp1")], [FB[3]])
        dve(lambda e: e.reduce_sum(st[0:64, 2:3], fbuf[3][0:64, 0:512], AX.X), [FB[3]], [B("lnsq0")])
        dve(lambda e: e.tensor_scalar(st[0:64, 3:4], st[0:64, 2:3], 1.0 / 512, LN_EPS, ALU.mult, ALU.add), [B("lnsq0")], [B("lnr0")])
        P.act(st[0:64, 5:6], st[0:64, 3:4], AF.Sqrt, [B("lnr0")], [B("lnr0")])
        dve(lambda e: e.reciprocal(st[0:64, 4:5], st[0:64, 5:6]), [B("lnr0")], [B("lnr0")])
        dve(lambda e: e.scalar_tensor_tensor(og[0:64, :], otmp[0:64, :], st[0:64, 4:5], gate_ap, ALU.mult, ALU.mult),
            [B("otmp0"), B("otmp1"), B("lnr0")] + gate_bufs, [B("og")])

    def gelu_to(ps_ap, n, bias_col, out_ap, psbuf, outbuf):
        g0, g1, g2 = fbuf[3][:, 0:n], fbuf[3][:, 512:512 + n], otmp[:, 0:n]
        k0, k1, k2 = FB[3], FB[3], B("otmp0")
        P.act(g0, ps_ap, AF.Identity, [psbuf, B("hb")], [k0], bias=bias_col)
        dve(lambda e: e.tensor_tensor(g1, g0, g0, ALU.mult), [k0], [k1])
        dve(lambda e: e.tensor_scalar(g1, g1, 0.044715, 1.0, ALU.mult, ALU.add), [k1], [k1])
        dve(lambda e: e.tensor_tensor(g1, g1, g0, ALU.mult), [k1, k0], [k1])
        P.act(g2, g1, AF.Sigmoid, [k1], [k2, B("otmp1")], scale=1.5957691216)
        dve(lambda e: e.tensor_tensor(out_ap, g2, g0, ALU.mult), [k2, k0], [outbuf])

    sc3 = fbuf[2][:, :].rearrange("p (h q) -> p h q", h=8)
    pw3 = qbf[:, :].rearrange("p (h q) -> p h q", h=8)
    KSC = B("fbuf2")

    def l_scores(X3, npos, qL3, xbufs, qbufs):
        ch = min(npos, 16)
        for h in range(8):
            for c0 in range(0, npos, ch):
                tmp = otmp[:, :].rearrange("p (a d) -> p a d", d=64)[:, 0:ch, :] if ch <= 8 else fbuf[3][:, :].rearrange("p (a d) -> p a d", d=64)[:, 0:ch, :]
                kt = B("otmp0") if ch <= 8 else FB[3]
                dve(lambda e, tmp=tmp, h=h, c0=c0: e.tensor_tensor(tmp, X3[:, c0:c0 + ch, :], bcast(qL3[:, h, :], 1, ch), ALU.mult), xbufs + qbufs, [kt])
                dve(lambda e, tmp=tmp, h=h, c0=c0: e.reduce_sum(sc3[:, h, c0:c0 + ch], tmp, AX.X), [kt], [KSC])

    def l_norm(npos, gcol0, LA, kLA, self_k_col):
        dve(lambda e: e.reduce_sum(zp[:, :], sc3[:, :, 0:npos], AX.X), [KSC], [B("zp")])
        P.mm(ps[7][:, 0:8], ddsb[:, :], zp[:, :], True, True, [B("ddsb"), B("zp")], [PSB[7]])
        if self_k_col is not None:
            qL3 = LA[:, 0:512].rearrange("p (h d) -> p h d", h=8)
            tmp = otmp[:, :].rearrange("p (h d) -> p h d", h=8)
            dve(lambda e: e.tensor_tensor(tmp, qL3, bcast(LA[:, self_k_col:self_k_col + 64], 1, 8), ALU.mult), [kLA], [B("otmp0"), B("otmp1")])
            dve(lambda e: e.reduce_sum(ssm[:, :], tmp, AX.X), [B("otmp0"), B("otmp1")], [B("ssm")])
            P.act(pself[:, :], ssm[:, :], AF.Exp, [B("ssm")], [B("pself")], scale=0.125)
            dve(lambda e: e.tensor_tensor(rZs[:, :], ps[7][:, 0:8], pself[:, :], ALU.add), [PSB[7], B("pself")], [B("rZs")])
        else:
            dve(lambda e: e.tensor_scalar(rZs[:, :], ps[7][:, 0:8], 1.0e-30, None, ALU.max), [PSB[7]], [B("rZs")])
        dve(lambda e: e.reciprocal(rZs[:, :], rZs[:, :]), [B("rZs")], [B("rZs")])
        dve(lambda e: e.tensor_tensor(Ws[:, :], rZs[:, :], LA[:, 512 + gcol0:512 + gcol0 + 8], ALU.mult), [B("rZs"), kLA], [B("Ws")])

    def l_pv(npos, v_of_pos, vbufs, first_open, last_close):
        if first_open:
            P.mm(ps[5][0:64, 0:64], zeros[0:32, 0:64], zeros[0:32, 0:64], True, False, [B("zeros")], [PSB[5]])
        nch = (npos + 7) // 8
        for pc in range(nch):
            slot = pc % 2
            Pe = QT[:, slot * 512:(slot + 1) * 512].rearrange("p (q s h) -> p q s h", q=8, s=8)
            kq = B("QTs%d" % slot)
            for s_ in range(8):
                dve(lambda e, Pe=Pe, s_=s_, pc=pc: e.tensor_scalar(Pe[:, :, s_, :], pw3[:, :, pc * 8:(pc + 1) * 8].rearrange("p h q -> p q h"), d8[:, s_:s_ + 1], None, ALU.mult),
                    [B("qbf"), B("d8")], [kq, B("QT")])
            for r in range(8):
                pos = pc * 8 + r
                P.mm(ps[5][0:64, 0:64], QT[:, slot * 512 + r * 64:slot * 512 + (r + 1) * 64], v_of_pos(pos), False,
                     last_close and pc == nch - 1 and r == 7, [kq] + vbufs, [PSB[5]])

    def nsa_sample(li, layer):
        PTc2 = PTc[:, :, :].rearrange("p a b -> p (a b)")
        for c in range(4):
            load_bf16(PTc2[:, 1024 * c:1024 * (c + 1)], nsa_w1s[li][:, 1024 * c:1024 * (c + 1)], 128, 1024, PTckeys)
        load_xsT(layer)
        s_proj(0, 512, 0)
        s_proj(512, 384, 1)
        s_proj(896, 24, 2)
        s_proj(920, 512, 3)
        SQ = fbuf[0]
        P.act(SQ[0:64, 0:512], ps[0][0:64, :], AF.Copy, [PSB[0]], [FB[0]])
        P.act(SQ[0:64, 512:536], ps[2][0:64, 0:24], AF.Sigmoid, [PSB[2]], [FB[0]])
        P.act(SQ[0:64, 536:920], ps[1][0:64, 0:384], AF.Copy, [PSB[1]], [FB[0]])
        P.act(szs[0:64, :], ps[3][0:64, :], AF.Silu, [PSB[3]], [B("szs")])
        P.dma("sp", rp2048[0:64, :], bass.AP(rope2048_d.tensor, 0, [[0, 64], [1, 16]]), [], [B("rp2048")], "rp2048")
        q3 = SQ[0:64, 0:512].rearrange("p (a b) -> p a b", a=8)
        rope_apply(q3[:, :, 0:8], q3[:, :, 8:16], 8, 0, ropetmp[0], "rt0", [FB[0], B("rp2048")], [FB[0]],
                   cs_ap=rp2048[0:64, 0:8], sn_ap=rp2048[0:64, 8:16], npart=64)
        k3 = SQ[0:64, 536:920].rearrange("p (a b) -> p a b", a=3)
        rope_apply(k3[:, :, 0:8], k3[:, :, 8:16], 3, 0, ropetmp[1], "rt1", [FB[0], B("rp2048")], [FB[0]],
                   cs_ap=rp2048[0:64, 0:8], sn_ap=rp2048[0:64, 8:16], npart=64)
        P.dma("sp", kvs_out[li][:, :], SQ[0:64, 536:920], [FB[0]], [B("kvs_out%d" % li)], "kvo0")
        for j in range(2):
            src = bass.AP(cwin[li][j].tensor, 64, [[32768, 64], [4672, 7], [1, 4672]])
            dst = bass.AP(wins_out[li][j].tensor, 0, [[32768, 64], [4672, 7], [1, 4672]])
            P.dma("sp", dst, src, [], [B("wins_%d_%d" % (li, j))], "wino%d" % j)
            P.dma("sp", wins_out[li][j][:, 511 * 64:512 * 64], SQ[0:64, 792 + 64 * j:856 + 64 * j], [FB[0]], [B("winsn_%d_%d" % (li, j))], "winn%d" % j)
        P.dma("sp", sq_d[:, :], SQ[0:64, 0:920], [FB[0]], [B("sq_d")], "sqd")

        def load_LA(g8, sl):
            LA = fbuf[sl]
            for s_ in range(8):
                P.dma("sp", LA[16 * s_:16 * s_ + 16, 0:920], bass.AP(sq_d.tensor, (8 * g8 + s_) * 920, [[0, 16], [1, 920]]),
                      [B("sq_d")], [FB[sl]], "fbufld%d" % sl)
            return LA, FB[sl]

        def gather(pool_ap, g8):
            P.add("pool", lambda e: e.indirect_dma_start(out=GA[:, :], out_offset=None, in_=pool_ap,
                                                         in_offset=bass.IndirectOffsetOnAxis(ap=ptL[:, g8:g8 + 1], axis=0)),
                  [B("ptL")], KTkeys, kind="d", semkey="gath")

        for g8 in range(8):
            LA, kLA = load_LA(g8, g8 % 2)
            qL3 = LA[:, 0:512].rearrange("p (h d) -> p h d", h=8)
            for kv_i in range(2):
                gather(pools[li][kv_i][:, :], g8)
                for q4 in range(16):
                    pb = ps[q4 % 2]
                    for r in range(4):
                        p_ = 4 * q4 + r
                        P.tr(pb[:, r * 128:(r + 1) * 128], GA[:, p_ * 128:(p_ + 1) * 128], ident_f[:, :], KTkeys + [B("ident_f")], [PSB[q4 % 2]])
                    dst = (XTlo if q4 < 8 else XThi)[:, (4 * q4) % 32:(4 * q4) % 32 + 4, :]
                    P.act(dst, pb[:, :].rearrange("p (a b) -> p a b", a=4), AF.Copy, [PSB[q4 % 2]], Vskeys if q4 < 8 else Vwkeys)
                xb = Vskeys + Vwkeys + PTckeys
                for ip in range(8):
                    n_ = 127 if ip == 7 else 128
                    for c in range(16):
                        p_ = 8 * ip + c
                        if p_ < 64:
                            rhs = (XTlo if p_ < 32 else XThi)[:, p_ % 32, 0:n_]
                        else:
                            rhs = XTlo[:, p_ - 64, 1:128]
                        P.mm(ps[2 + ip // 4][:, (ip % 4) * 128:(ip % 4) * 128 + n_], W1s[:, kv_i, c, :], rhs, c == 0, c == 15, xb, [PSB[2 + ip // 4]])
                gelu_to(ps[2][:, 0:512], 512, hb[:, kv_i:kv_i + 1], hsb[0][:, 0:512], PSB[2], B("hsb0"))
                gelu_to(ps[3][:, 0:511], 511, hb[:, kv_i:kv_i + 1], hsb[1][:, 0:511], PSB[3], B("hsb1"))
                for ip in range(8):
                    P.mm(ps[4][:, ip * 64:(ip + 1) * 64], hsb[ip // 4][:, (ip % 4) * 128:(ip % 4 + 1) * 128],
                         w2sb[:, 0:64] if kv_i == 0 else w2sb[:, 128:192], True, True, [B("hsb%d" % (ip // 4)), B("w2sb")], [PSB[4]])
                if kv_i == 0:
                    P.act(qf[:, :], ps[4][:, :], AF.Copy, [PSB[4]], [B("qf")])
                else:
                    P.act(vbf[:, :], ps[4][:, :], AF.Copy, [PSB[4]], [B("vbf")])
            l_scores(qf[:, :].rearrange("p (a d) -> p a d", d=64), 8, qL3, [B("qf")], [kLA])
            dve(lambda e: e.tensor_tensor(sc3[:, :, 0:8], sc3[:, :, 0:8], bcast(cmaskL[:, :], 1, 8), ALU.add), [KSC, B("cmaskL")], [KSC])
            P.act(sc3[:, :, 0:8], sc3[:, :, 0:8], AF.Exp, [KSC], [KSC], scale=0.125)
            l_norm(8, 0, LA, kLA, None)
            pn3 = pnL[:, :].rearrange("p (h q) -> p h q", h=8)
            dve(lambda e: e.tensor_tensor(pn3, sc3[:, :, 0:8], bcast(rZs[:, :], 2, 8), ALU.mult), [KSC, B("rZs")], [B("pnL")])
            dve(lambda e: e.reduce_sum(pcs[:, :], pnL[:, :].rearrange("p (h q) -> p q h", h=8), AX.X), [B("pnL")], [B("pcs")])
            PCe = PT[0][:, :].rearrange("p (q s) -> p q s", q=8)
            pool(lambda e: e.memset(PT[0][:, :], 0.0), [], [B("PT0")])
            for ip in range(8):
                dve(lambda e, ip=ip, g8=g8: e.tensor_scalar(PCe[:, ip, 8 * g8:8 * g8 + 8], d8[:, :], pcs[:, ip:ip + 1], None, ALU.mult), [B("pcs"), B("d8")], [B("PT0")])
            for ip in range(8):
                P.mm(ps[6][0:64, 0:33], PCe[:, ip, :], ovt[:, ip, :], g8 == 0 and ip == 0, g8 == 7 and ip == 7, [B("PT0"), B("ovt")], [PSB[6]])
            dve(lambda e: e.tensor_tensor(pw3[:, :, 0:8], sc3[:, :, 0:8], bcast(Ws[:, :], 2, 8), ALU.mult), [KSC, B("Ws")], [B("qbf")])
            l_pv(8, lambda pos: vbf[:, pos * 64:(pos + 1) * 64], [B("vbf")], True, True)
            dve(lambda e, g8=g8: e.tensor_copy(oacc[0:64, g8 * 64:(g8 + 1) * 64], ps[5][0:64, 0:64]), [PSB[5]], [B("oacc0"), B("oacc1")])

        dve(lambda e: e.tensor_tensor(score[0:64, 0:33], ps[6][0:64, 0:33], tbs[0:64, :], ALU.add), [PSB[6], B("tbs")], [B("score")])
        dve(lambda e: e.max(m8[0:64, 0:8], score[0:64, 0:33]), [B("score")], [B("m8a")])
        dve(lambda e: e.match_replace(score2[0:64, 0:33], m8[0:64, 0:8], score[0:64, 0:33], -1.0e30), [B("score"), B("m8a")], [B("score2")])
        dve(lambda e: e.max(m8[0:64, 8:16], score2[0:64, 0:33]), [B("score2")], [B("m8b")])
        dve(lambda e: e.tensor_scalar(imp[0:64, 0:32], score[0:64, 0:32], m8[0:64, 15:16], NEGB, ALU.is_lt, ALU.mult), [B("score"), B("m8b")], [B("imp")])
        P.dma("sp", selb_d[:, :], imp[0:64, 0:32], [B("imp")], [B("selb_d")], "selbd")

        for g8 in range(8):
            LA, kLA = load_LA(g8, g8 % 2)
            qL3 = LA[:, 0:512].rearrange("p (h d) -> p h d", h=8)
            P.dma("sp", selbL[:, :], bass.AP(selb_d.tensor, 256 * g8, [[2, 128], [1, 2]]), [B("selb_d")], [B("selbL")], "selbl")
            gather(pools[li][2][:, :], g8)
            l_scores(GA[:, :].rearrange("p (a d) -> p a d", d=64), 128, qL3, KTkeys, [kLA])
            dve(lambda e: e.tensor_scalar(sc3[:, :, 0:64], sc3[:, :, 0:64], selbL[:, 0:1], None, ALU.add), [KSC, B("selbL")], [KSC])
            dve(lambda e: e.tensor_scalar(sc3[:, :, 64:128], sc3[:, :, 64:128], selbL[:, 1:2], None, ALU.add), [KSC, B("selbL")], [KSC])
            P.act(sc3[:, :, :], sc3[:, :, :], AF.Exp, [KSC], [KSC], scale=0.125)
            l_norm(128, 8, LA, kLA, 664)
            dve(lambda e: e.tensor_tensor(pw3[:, :, :], sc3[:, :, :], bcast(Ws[:, :], 2, 128), ALU.mult), [KSC, B("Ws")], [B("qbf")])
            gather(pools[li][3][:, :], g8)
            P.act(XTlo[:, :, :].rearrange("p a b -> p (a b)"), GA[:, 0:4096], AF.Copy, KTkeys, Vskeys)
            P.act(XThi[:, :, :].rearrange("p a b -> p (a b)"), GA[:, 4096:8192], AF.Copy, KTkeys, Vwkeys)
            XL2 = XTlo[:, :, :].rearrange("p a b -> p (a b)")
            XH2 = XThi[:, :, :].rearrange("p a b -> p (a b)")
            l_pv(128, lambda pos: (XL2 if pos < 64 else XH2)[:, (pos % 64) * 64:(pos % 64 + 1) * 64], Vskeys + Vwkeys, True, False)
            self_pv(LA, kLA, 728)
            P.dma("sp", GA[:, 0:2048], bass.AP(cwin[li][0].tensor, 8 * g8 * 32768, [[2048, 128], [1, 2048]]), [], KTkeys, "gathw")
            l_scores(GA[:, 0:2048].rearrange("p (a d) -> p a d", d=64), 32, qL3, KTkeys, [kLA])
            P.act(sc3[:, :, 0:32], sc3[:, :, 0:32], AF.Exp, [KSC], [KSC], scale=0.125)
            l_norm(32, 16, LA, kLA, 792)
            dve(lambda e: e.tensor_tensor(pw3[:, :, 0:32], sc3[:, :, 0:32], bcast(Ws[:, :], 2, 32), ALU.mult), [KSC, B("Ws")], [B("qbf")])
            P.dma("sp", GA[:, 0:2048], bass.AP(cwin[li][1].tensor, 8 * g8 * 32768, [[2048, 128], [1, 2048]]), [], KTkeys, "gathw")
            P.act(XL2[:, 0:2048], GA[:, 0:2048], AF.Copy, KTkeys, Vskeys)
            l_pv(32, lambda pos: XL2[:, pos * 64:(pos + 1) * 64], Vskeys, False, False)
            self_pv(LA, kLA, 856, close=True)
            osb = ssm64
            dve(lambda e, g8=g8: e.tensor_tensor(osb[0:64, :], ps[5][0:64, 0:64], oacc[0:64, g8 * 64:(g8 + 1) * 64], ALU.add), [PSB[5], B("oacc0"), B("oacc1")], [B("ssm64")])
            P.dma("sp", so_d[64 * g8:64 * g8 + 64, :], osb[0:64, :], [B("ssm64")], [B("so_d_%d" % g8)], "sod")
        P.dma("sp", fbuf[1][0:64, 0:512], bass.AP(so_d.tensor, 0, [[512, 64], [1, 512]]), [B("so_d_%d" % g) for g in range(8)], [FB[1]], "fbufld1")
        dve(lambda e: e.tensor_tensor(og[0:64, :], fbuf[1][0:64, 0:512], szs[0:64, :], ALU.mult), [FB[1], B("szs")], [B("og")])
        s_out_tail(layer)

    def self_pv(LA, kLA, vcol, close=False):
        dve(lambda e: e.tensor_tensor(wself[:, :], pself[:, :], Ws[:, :], ALU.mult), [B("pself"), B("Ws")], [B("wself")])
        for s_ in range(8):
            dve(lambda e, s_=s_: e.tensor_scalar(Wse[:, s_ * 8:(s_ + 1) * 8], wself[:, :], d0[:, s_:s_ + 1], None, ALU.mult), [B("wself"), B("d0")], [B("Wse")])
        P.act(vsn[:, :], LA[:, vcol:vcol + 64], AF.Copy, [kLA], [B("vsn")])
        P.mm(ps[5][0:64, 0:64], Wse[:, :], vsn[:, :], False, close, [B("Wse"), B("vsn")], [PSB[5]])


    def ret_sample(li, layer):
        load_xsT(layer)
        s_proj(0, 512, 0)
        s_proj(512, 512, 1)
        s_proj(1024, 512, 2)
        P.act(vbf[0:64, :], ps[1][0:64, :], AF.Copy, [PSB[1]], [B("vbf")])
        P.act(szs[0:64, :], ps[2][0:64, :], AF.Silu, [PSB[2]], [B("szs")])
        pool(lambda e: e.tensor_tensor(qf[0:64, :], szs[0:64, :], gng[0:64, :], ALU.mult), [B("szs"), B("gng")], [B("qf")])
        P.dma("sp", xcs[1][0:64, :, :].rearrange("p a b -> p (a b)"), bass.AP(xp2048_d.tensor, 0, [[0, 64], [1, 256]]), [], [B("xcs1")], "xcsa1")
        pv4 = ps[0][0:64, :].rearrange("p (a b t) -> p a b t", a=2, b=2)
        E, O_ = pv4[:, :, 0, :], pv4[:, :, 1, :]
        cosb = bcast(xcs[1][0:64, 0, :], 1, 2)
        sinb = bcast(xcs[1][0:64, 1, :], 1, 2)
        rv = [rtmp[0:64, a, :].rearrange("p (a t) -> p a t", a=2) for a in range(4)]
        rb = [PSB[0], B("xcs1")]
        dve(lambda e: e.tensor_tensor(rv[0], E, cosb, ALU.mult), rb, [B("rtmp0")])
        dve(lambda e: e.tensor_tensor(rv[1], O_, sinb, ALU.mult), rb, [B("rtmp1")])
        dve(lambda e: e.tensor_tensor(rv[2], E, sinb, ALU.mult), rb, [B("rtmp2")])
        dve(lambda e: e.tensor_tensor(rv[3], O_, cosb, ALU.mult), rb, [B("rtmp3")])
        qk = fbuf[1]
        qk4 = qk[0:64, 0:512].rearrange("p (a b t) -> p a b t", a=2, b=2)
        dve(lambda e: e.tensor_tensor(qk4[:, :, 0, :], rv[0], rv[1], ALU.subtract), [B("rtmp0"), B("rtmp1")], [FB[1]])
        dve(lambda e: e.tensor_tensor(qk4[:, :, 1, :], rv[2], rv[3], ALU.add), [B("rtmp2"), B("rtmp3"), FB[1]], [FB[1]])
        dve(lambda e: e.tensor_scalar(qk[0:64, 256:512], qk[0:64, 256:512], 0.0625, None, ALU.mult), [FB[1]], [FB[1]])
        pf = ps[0]
        for ch in range(2):
            P.tr(pf[:, ch * 64:(ch + 1) * 64], qk[0:64, ch * 128:(ch + 1) * 128], ident_f[0:64, 0:64], [FB[1], B("ident_f")], [PSB[0]])
        dve(lambda e: e.tensor_copy(QKr[:, 0:2, 0:64], pf[:, 0:128].rearrange("p (a t) -> p a t", a=2)), [PSB[0]], [B("QKr_e"), B("QKr_o")])
        for s_ in range(64):
            sl = s_ % 2
            Sb = fbuf[2 + sl]
            kS = B("fbuf%d" % (2 + sl))
            P.dma("sp", Sb[:, :].rearrange("p (c e) -> p c e", c=2), sret_d[li][s_, :, :].rearrange("(c p) e -> p c e", c=2), [], [kS], "fbufld%d" % (2 + sl))
            km = PT[1 + sl]
            kmk = B("PT%d" % (1 + sl))
            dve(lambda e, km=km, s_=s_: e.tensor_scalar(km[0:64, 0:256], qk[0:64, 256:512], ident_f[0:64, s_:s_ + 1], None, ALU.mult), [FB[1], B("ident_f")], [kmk])
            for ch in range(2):
                P.mm(ps[5 + ch][:, :], km[0:64, ch * 128:(ch + 1) * 128], vbf[0:64, :], True, True, [kmk, B("vbf")], [PSB[5 + ch]])
                dve(lambda e, Sb=Sb, ch=ch: e.scalar_tensor_tensor(Sb[:, ch * 512:(ch + 1) * 512], Sb[:, ch * 512:(ch + 1) * 512], kcdec[:, 2:3], ps[5 + ch][:, :], ALU.mult, ALU.add),
                    [PSB[5 + ch], kS, B("kcdec")], [kS])
            P.dma("sp", rets_out[li][s_, :, :].rearrange("(c p) e -> p c e", c=2), Sb[:, :].rearrange("p (c e) -> p c e", c=2), [kS], [B("rets_%d_%d" % (li, s_))], "reto%d" % sl)
            P.act(Sbf[:, :, :].rearrange("p c e -> p (c e)"), Sb[:, :], AF.Copy, [kS], [B("Sbf")])
            po = ps[3 + sl]
            for ch in range(2):
                P.mm(po[0:64, :], QKr[:, ch, 0:64], Sbf[:, ch, :], ch == 0, ch == 1, [B("QKr_e"), B("QKr_o"), B("Sbf")], [PSB[3 + sl]])
            if s_ == 0:
                dve(lambda e, po=po, s_=s_: e.tensor_scalar(oacc[0:64, :], po[0:64, :], ident_f[0:64, s_:s_ + 1], None, ALU.mult), [PSB[3 + sl], B("ident_f")], [B("oacc0"), B("oacc1")])
            else:
                dve(lambda e, po=po, s_=s_: e.scalar_tensor_tensor(oacc[0:64, :], po[0:64, :], ident_f[0:64, s_:s_ + 1], oacc[0:64, :], ALU.mult, ALU.add),
                    [PSB[3 + sl], B("ident_f"), B("oacc0"), B("oacc1")], [B("oacc0"), B("oacc1")])
        s_groupnorm_gate(oacc[0:64, :], [B("oacc0"), B("oacc1")], qf[0:64, :], [B("qf")])
        s_out_tail(layer)

    def ln_sample(layer, last):
        P.add("pool", lambda e: e.collective_compute("AllReduce", ALU.add, replica_groups=[[0, 1, 2, 3], [4, 5, 6, 7]],
                                                     ins=[ypart_s[layer].ap().opt()], outs=[ysum_s[layer].ap().opt()]),
              [B("ypart_s%d" % layer)], [B("ysum_s%d" % layer)], kind="cc", semkey="ccs%d" % layer)
        X, Y, st = fbuf[0], fbuf[2], lnst[0]
        kx, ky, kst_ = FB[0], FB[2], B("lnst0")
        src = xs_d if layer == 0 else xs_cur
        P.dma("sp", X[0:64, :], src[:, :], [B("xs_cur")], [kx], "fbufld0")
        P.dma("sp", Y[0:64, :], ysum_s[layer][:, :], [B("ysum_s%d" % layer)], [ky], "fbufld2")
        dve(lambda e: e.scalar_tensor_tensor(X[0:64, :], X[0:64, :], ALPHA, Y[0:64, :], ALU.mult, ALU.add), [kx, ky], [kx])
        dve(lambda e: e.reduce_sum(st[0:64, 0:1], X[0:64, :], AX.X), [kx], [kst_])
        dve(lambda e: e.tensor_scalar(st[0:64, 1:2], st[0:64, 0:1], -1.0 / D, None, ALU.mult), [kst_], [kst_])
        P.act(Y[0:64, :], X[0:64, :], AF.Identity, [kx, kst_], [ky], bias=st[0:64, 1:2])
        P.act(X[0:64, :], Y[0:64, :], AF.Square, [ky], [kx])
        dve(lambda e: e.reduce_sum(st[0:64, 2:3], X[0:64, :], AX.X), [kx], [B("lnsq0")])
        dve(lambda e: e.tensor_scalar(st[0:64, 3:4], st[0:64, 2:3], 1.0 / D, LN_EPS, ALU.mult, ALU.add), [B("lnsq0")], [B("lnr0")])
        P.act(st[0:64, 5:6], st[0:64, 3:4], AF.Sqrt, [B("lnr0")], [B("lnr0")])
        dve(lambda e: e.reciprocal(st[0:64, 4:5], st[0:64, 5:6]), [B("lnr0")], [B("lnr0")])
        dve(lambda e: e.scalar_tensor_tensor(X[0:64, :], Y[0:64, :], st[0:64, 4:5], lng[0:64, 0:D], ALU.mult, ALU.mult),
            [ky, B("lnr0"), B("stg0"), B("lnsq0")], [kx])
        pool(lambda e: e.tensor_tensor(X[0:64, :], X[0:64, :], lnb[0:64, 0:D], ALU.add), [kx, B("stg1")], [kx])
        dst = ys_out if last else xs_cur
        P.dma("sp", dst[:, :], X[0:64, :], [kx], [B("xs_cur")], "lnst0")


    for layer in range(NL):
        if layer % 2 == 0:
            nsa_layer(layer // 2, layer)
        else:
            ret_layer(layer // 2, layer)
        if os.environ.get("KNOLN"):
            continue
        ln_pass(layer, layer == NL - 1)
        if SAMPLE:
            ln_sample(layer, layer == NL - 1)

    P.emit(es)
    es.close()
    return nc, P


_CACHE = {}


def _prep_core(c, inp):
    b, k = c // 4, c % 4
    m = {}
    m["xp"] = np.ascontiguousarray(inp["x_prompt"][b])
    m["rope"] = _CACHE["rope"]
    for kk, v in _CACHE["consts"].items():
        m[kk] = v
    m["ln_g"] = np.ascontiguousarray(inp["ln_g"])
    m["ln_b"] = np.ascontiguousarray(inp["ln_b"])
    for l in range(2):
        w = inp["nsa_w_in"][l]
        cols = [w[:, 512 * k:512 * (k + 1)]]
        for j in range(6):
            o = 2048 + 256 * j + 64 * k
            cols.append(w[:, o:o + 64])
        for br in range(3):
            o = 2048 + 1536 + 32 * br + 8 * k
            cols.append(w[:, o:o + 8])
        o = 2048 + 1536 + 96 + 512 * k
        cols.append(w[:, o:o + 512])
        m["nsa_win%d" % l] = np.ascontiguousarray(np.concatenate(cols, axis=1))
        m["nsa_wout%d" % l] = np.ascontiguousarray(inp["nsa_w_out"][l][512 * k:512 * (k + 1), :])
        w1k = inp["nsa_w1_k"][l].reshape(32, 64, 128).transpose(1, 0, 2).reshape(64, 4096)
        w1v = inp["nsa_w1_v"][l].reshape(32, 64, 128).transpose(1, 0, 2).reshape(64, 4096)
        m["nsa_w1_%d" % l] = np.ascontiguousarray(np.concatenate([w1k, w1v], 0))
        m["nsa_pe%d" % l] = np.ascontiguousarray(np.concatenate([inp["nsa_pe_k"][l].T, inp["nsa_pe_v"][l].T], 0))
        m["nsa_w2_%d" % l] = np.ascontiguousarray(np.concatenate([inp["nsa_w2_k"][l], inp["nsa_w2_k"][l], inp["nsa_w2_v"][l]], 1))
        rw = inp["ret_w_in"][l]
        qc = rw[:, 256 * k:256 * (k + 1)]
        kc_ = rw[:, 1024 + 256 * k:1024 + 256 * (k + 1)]
        m["ret_win%d" % l] = np.ascontiguousarray(np.concatenate(
            [qc[:, 0::2], qc[:, 1::2], kc_[:, 0::2], kc_[:, 1::2],
             rw[:, 2048 + 512 * k:2048 + 512 * (k + 1)], rw[:, 4096 + 512 * k:4096 + 512 * (k + 1)]], 1))
        m["ret_wout%d" % l] = np.ascontiguousarray(inp["ret_w_out"][l][512 * k:512 * (k + 1), :])
        m["ret_gn%d" % l] = np.ascontiguousarray(inp["ret_gn_g"][l][512 * k:512 * (k + 1)][None, :])
    m["xcos"], m["xsin"] = _CACHE["xpos"]
    m["xp2048"] = np.ascontiguousarray(np.concatenate([m["xcos"][:, 2048], m["xsin"][:, 2048]])[None, :])
    g = c // 4
    m["xs"] = np.ascontiguousarray(inp["x_sample"][64 * g:64 * g + 64, 0, :])
    m["ptab"] = np.ascontiguousarray(inp["page_table"][64 * g:64 * g + 64].reshape(8, 128).astype(np.int32))
    for kk, v in _CACHE["stab"].items():
        m[kk] = v
    m["rope2048"] = np.ascontiguousarray(_CACHE["rope"][0, 16, :][None, :])
    for l in range(2):
        for j, nm in enumerate(("cache_k_cmp", "cache_v_cmp", "cache_k_sel", "cache_v_sel")):
            m["pool%d_%d" % (l, j)] = _pool_slice(inp, nm, l, k)
        for j, nm in enumerate(("cache_k_win", "cache_v_win")):
            m["cwin%d_%d" % (l, j)] = np.ascontiguousarray(inp[nm][l, 64 * g:64 * g + 64, :, k, :]).reshape(64, 32768)
        w1n = [inp[nm][l].reshape(16, 128, 128).transpose(1, 0, 2).reshape(128, 2048) for nm in ("nsa_w1_k", "nsa_w1_v")]
        m["nsa_w1s%d" % l] = np.ascontiguousarray(np.concatenate(w1n, 1))
    for l in range(2):
        st = inp["state_ret"][l, 64 * g:64 * g + 64, k]
        m["sret%d" % l] = np.ascontiguousarray(np.concatenate([st[:, 0::2, :], st[:, 1::2, :]], 1))
    for kk, v in _CACHE["dec"][k].items():
        m[kk] = v
    return m


def kernel(**inp):
    stage = os.environ.get("KSTAGE", "full")
    inp = {k: np.asarray(v) for k, v in inp.items()}
    if "rope" not in _CACHE:
        _CACHE["rope"] = _rope_table()
        _CACHE["consts"] = _consts_bf16like()
        _CACHE["xpos"] = _xpos_tables()
        _CACHE["dec"] = [_decay_tables(h) for h in range(4)]
        _CACHE["stab"] = _sample_tables()
    _CACHE["pool"] = {}
    nc, P = build(stage)
    in_maps = [_prep_core(c, inp) for c in range(8)]
    res = run_bass_kernel_spmd(nc, in_maps, core_ids=list(range(8))).results
    B_, DEC = 2, 128
    y_prompt = np.zeros((B_, T, D), np.float32)
    y_sample = np.zeros((DEC, 1, D), np.float32)
    kv_p = [np.zeros((2, B_, T, 4, 64), np.float32) for _ in range(4)]
    kv_s = [np.zeros((2, DEC, 1, 4, 64), np.float32) for _ in range(4)]
    win_p = [np.zeros((2, B_, 512, 4, 64), np.float32) for _ in range(2)]
    win_s = [np.zeros((2, DEC, 512, 4, 64), np.float32) for _ in range(2)]
    ret_p = np.zeros((2, B_, 4, 256, 512), np.float32)
    ret_s = np.zeros((2, DEC, 4, 256, 512), np.float32)
    for c in range(8):
        b, k = c // 4, c % 4
        if k == 0:
            y_prompt[b] = res[c]["yout"]
            y_sample[64 * (c // 4):64 * (c // 4) + 64, 0, :] = res[c]["ys_out"]
        for l in range(2):
            kvo = res[c]["kvout%d" % l]
            for j in range(4):
                kv_p[j][l, b, :, k, :] = kvo[:, 64 * j:64 * (j + 1)]
            for j in range(2):
                win_p[j][l, b, :, k, :] = kvo[T - 512:, 256 + 64 * j:256 + 64 * (j + 1)]
            rs = res[c]["rets_out%d" % l]
            g = c // 4
            kvs = res[c]["kvs_out%d" % l]
            for j in range(4):
                kv_s[j][l, 64 * g:64 * g + 64, 0, k, :] = kvs[:, 64 * j:64 * (j + 1)]
            for j in range(2):
                win_s[j][l, 64 * g:64 * g + 64, :, k, :] = res[c]["wins_out%d_%d" % (l, j)].reshape(64, 512, 64)
            ret_s[l, 64 * g:64 * g + 64, k, 0::2, :] = rs[:, 0:128]
            ret_s[l, 64 * g:64 * g + 64, k, 1::2, :] = rs[:, 128:256]
            ro = res[c]["retout%d" % l]
            ret_p[l, b, k, 0::2, :] = ro[0:128]
            ret_p[l, b, k, 1::2, :] = ro[128:256]
    return (y_prompt, y_sample,
            kv_p[0], kv_s[0], kv_p[1], kv_s[1], kv_p[2], kv_s[2], kv_p[3], kv_s[3],
            win_p[0], win_s[0], win_p[1], win_s[1], ret_p, ret_s)
```
